# Optimizing a Trainium2 kernel written in Bass

```python
import math
import jax, jax.numpy as jnp
from jax import lax
import numpy as np

D_MODEL = 1024
BATCH = 16
SEQ = 2048
DEPTH = 2

CHUNK = 64
Q_BLOCK = 128
N_A = DEPTH // 2
N_B = DEPTH - N_A
ALPHA = (2.0 * DEPTH) ** 0.25
BETA = (8.0 * DEPTH) ** -0.25
A_HEAD_DIM = 64
A_HEADS = D_MODEL // A_HEAD_DIM
LORA_W = 64
LORA_A = 64
LORA_G = 128
GN_EPS = 64e-5
B_HEAD_DIM = 64
B_HEADS = D_MODEL // (2 * B_HEAD_DIM)
SUBLN_EPS = 1e-5
REL_BUCKETS = 32
REL_MAX_DIST = 128
D_FF = 4 * D_MODEL
LN_EPS = 1e-5

kernel_name = "rwkv7_diffattn_yoco_deepnorm"


def _layer_norm(x, g, b, eps):
    xf = x.astype(jnp.float32)
    mu = jnp.mean(xf, axis=-1, keepdims=True)
    var = jnp.mean(jnp.square(xf - mu), axis=-1, keepdims=True)
    return ((xf - mu) * lax.rsqrt(var + eps) * g + b).astype(x.dtype)


def _token_shift(x):
    return jnp.pad(x, ((0, 0), (1, 0), (0, 0)))[:, :-1, :]


def _wkv7_scan(r, decay, k, v, a_in, b_in):
    bsz, _, nh, n = r.shape

    def step(S, inp):
        r_t, w_t, k_t, v_t, a_t, b_t = inp
        sa = jnp.einsum('bhij,bhj->bhi', S, a_t)
        S = (S * w_t[:, :, None, :] + sa[..., None] * b_t[:, :, None, :]
             + v_t[..., None] * k_t[:, :, None, :])
        y_t = jnp.einsum('bhij,bhj->bhi', S, r_t)
        return S, y_t

    xs = tuple(jnp.moveaxis(t, 1, 0) for t in (r, decay, k, v, a_in, b_in))
    S0 = jnp.zeros((bsz, nh, n, n), jnp.float32)
    _, ys = lax.scan(step, S0, xs)
    return jnp.moveaxis(ys, 0, 1)


def _rwkv7_time_mix(x, mu, w_r, w_k, w_v, w_o, w0, w1, w2, a0, a1, a2, g1, g2,
                    k_k, k_a, r_k, lnx_g, lnx_b):
    bsz, t, c = x.shape
    xx = _token_shift(x) - x
    xr = x + xx * mu[0]
    xw = x + xx * mu[1]
    xk = x + xx * mu[2]
    xv = x + xx * mu[3]
    xa = x + xx * mu[4]
    xg = x + xx * mu[5]
    r = xr @ w_r
    w = -jax.nn.softplus(-(w0 + jnp.tanh(xw @ w1) @ w2)) - 0.5
    k = xk @ w_k
    v = xv @ w_v
    a = jax.nn.sigmoid(a0 + (xa @ a1) @ a2)
    g = jax.nn.sigmoid(xg @ g1) @ g2

    def heads(z):
        return z.reshape(bsz, t, A_HEADS, A_HEAD_DIM).astype(jnp.float32)

    r, w, k, v, a = heads(r), heads(w), heads(k), heads(v), heads(a)
    kk = k * k_k.reshape(A_HEADS, A_HEAD_DIM).astype(jnp.float32)
    kk = kk / jnp.maximum(jnp.sqrt(jnp.sum(kk * kk, axis=-1, keepdims=True)), 1e-12)
    k = k * (1.0 + (a - 1.0) * k_a.reshape(A_HEADS, A_HEAD_DIM).astype(jnp.float32))
    decay = jnp.exp(-jnp.exp(w))
    y = _wkv7_scan(r, decay, k, v, -kk, kk * a)
    y = _layer_norm(y, lnx_g.reshape(A_HEADS, A_HEAD_DIM), lnx_b.reshape(A_HEADS, A_HEAD_DIM), GN_EPS)
    y = y + jnp.sum(r * k * r_k.astype(jnp.float32), axis=-1, keepdims=True) * v
    y = y.reshape(bsz, t, c).astype(x.dtype) * g
    return y @ w_o


def _t5_bucket(rel):
    nb = REL_BUCKETS // 2
    max_exact = nb // 2
    ret = jnp.where(rel > 0, nb, 0)
    n = jnp.abs(rel)
    nf = jnp.maximum(n, 1).astype(jnp.float32)
    large = max_exact + (jnp.log(nf / max_exact) / math.log(REL_MAX_DIST / max_exact)
                         * (nb - max_exact)).astype(jnp.int32)
    large = jnp.minimum(large, nb - 1)
    return ret + jnp.where(n < max_exact, n, large)


def _diff_attention(x, k_sh, v_sh, w_q, lam, subln_g, w_o, rel_bias, lambda_init):
    bsz, t, c = x.shape
    scale = B_HEAD_DIM ** -0.5
    q = (x @ w_q).reshape(bsz, t, 2 * B_HEADS, B_HEAD_DIM)
    lamf = lam.astype(jnp.float32)
    lam_full = (jnp.exp(jnp.sum(lamf[0] * lamf[1])) - jnp.exp(jnp.sum(lamf[2] * lamf[3]))
                + lambda_init)
    outs = []
    for blk in range(t // Q_BLOCK):
        q0 = blk * Q_BLOCK
        k_end = q0 + Q_BLOCK
        qb = q[:, q0:k_end]
        kb = k_sh[:, :k_end]
        vb = v_sh[:, :k_end].astype(jnp.float32)
        s = jnp.einsum('bqhd,bkhd->bhqk', qb, kb).astype(jnp.float32) * scale
        s = s.reshape(bsz, B_HEADS, 2, Q_BLOCK, k_end)
        q_pos = jnp.arange(q0, k_end, dtype=jnp.int32)
        k_pos = jnp.arange(k_end, dtype=jnp.int32)
        bias = rel_bias[_t5_bucket(k_pos[None, :] - q_pos[:, None])]
        bias = jnp.transpose(bias, (2, 0, 1)).astype(jnp.float32)
        allowed = (k_pos[None, :] // CHUNK) <= (q_pos[:, None] // CHUNK)
        s = jnp.where(allowed, s + bias[None, :, None], -jnp.inf)
        p = jax.nn.softmax(s, axis=-1)
        attn = p[:, :, 0] - lam_full * p[:, :, 1]
        o = jnp.einsum('bhqk,bkhe->bqhe', attn, vb)
        o = o * lax.rsqrt(jnp.mean(o * o, axis=-1, keepdims=True) + SUBLN_EPS)
        o = o * subln_g.astype(jnp.float32) * (1.0 - lambda_init)
        outs.append(o.reshape(bsz, Q_BLOCK, c).astype(x.dtype))
    return jnp.concatenate(outs, axis=1) @ w_o


def _sqrelu_mlp(x, w1, w2):
    return jnp.square(jax.nn.relu(x @ w1)) @ w2


def setup_inputs(seed: int = 0) -> dict:
    key = jax.random.key(seed)
    ks = iter(jax.random.split(key, 40))
    d = D_MODEL
    f32 = jnp.float32

    def nrm(shape, s):
        return jax.random.normal(next(ks), shape, f32) * s

    x = jax.random.normal(next(ks), (BATCH, SEQ, d), f32)
    a_mu = jax.random.uniform(next(ks), (N_A, 6, d), f32)
    a_w_r = nrm((N_A, d, d), d ** -0.5)
    a_w_k = nrm((N_A, d, d), d ** -0.5)
    a_w_v = nrm((N_A, d, d), BETA * d ** -0.5)
    a_w_o = nrm((N_A, d, d), BETA * d ** -0.5)
    a_w0 = jax.random.uniform(next(ks), (N_A, d), f32, -6.0, 1.0)
    a_w1 = nrm((N_A, d, LORA_W), d ** -0.5)
    a_w2 = nrm((N_A, LORA_W, d), 0.5 * LORA_W ** -0.5)
    a_a0 = nrm((N_A, d), 0.1)
    a_a1 = nrm((N_A, d, LORA_A), d ** -0.5)
    a_a2 = nrm((N_A, LORA_A, d), 0.5 * LORA_A ** -0.5)
    a_g1 = nrm((N_A, d, LORA_G), d ** -0.5)
    a_g2 = nrm((N_A, LORA_G, d), LORA_G ** -0.5)
    a_k_k = 0.85 + nrm((N_A, d), 0.05)
    a_k_a = 1.0 + nrm((N_A, d), 0.05)
    a_r_k = nrm((N_A, A_HEADS, A_HEAD_DIM), 0.1)
    a_lnx_g = 1.0 + nrm((N_A, d), 0.05)
    a_lnx_b = nrm((N_A, d), 0.01)
    b_w_kv = jnp.concatenate([nrm((d, d), d ** -0.5), nrm((d, d), BETA * d ** -0.5)], axis=1)
    b_w_q = nrm((N_B, d, d), d ** -0.5)
    b_lam = nrm((N_B, 4, B_HEAD_DIM), 0.1)
    b_subln_g = 1.0 + nrm((N_B, 2 * B_HEAD_DIM), 0.05)
    b_w_o = nrm((N_B, d, d), BETA * d ** -0.5)
    rel_bias = nrm((REL_BUCKETS, B_HEADS), 0.5)
    mlp_w1 = nrm((DEPTH, d, D_FF), BETA * d ** -0.5)
    mlp_w2 = nrm((DEPTH, D_FF, d), BETA * D_FF ** -0.5)
    ln_g = 1.0 + nrm((DEPTH, 2, d), 0.05)
    ln_b = nrm((DEPTH, 2, d), 0.01)
    return {"x": x, "a_mu": a_mu, "a_w_r": a_w_r, "a_w_k": a_w_k, "a_w_v": a_w_v,
            "a_w_o": a_w_o, "a_w0": a_w0, "a_w1": a_w1, "a_w2": a_w2, "a_a0": a_a0,
            "a_a1": a_a1, "a_a2": a_a2, "a_g1": a_g1, "a_g2": a_g2, "a_k_k": a_k_k,
            "a_k_a": a_k_a, "a_r_k": a_r_k, "a_lnx_g": a_lnx_g, "a_lnx_b": a_lnx_b,
            "b_w_kv": b_w_kv, "b_w_q": b_w_q, "b_lam": b_lam, "b_subln_g": b_subln_g,
            "b_w_o": b_w_o, "rel_bias": rel_bias, "mlp_w1": mlp_w1, "mlp_w2": mlp_w2,
            "ln_g": ln_g, "ln_b": ln_b}


def reference(x, a_mu, a_w_r, a_w_k, a_w_v, a_w_o, a_w0, a_w1, a_w2, a_a0, a_a1, a_a2,
              a_g1, a_g2, a_k_k, a_k_a, a_r_k, a_lnx_g, a_lnx_b, b_w_kv, b_w_q, b_lam,
              b_subln_g, b_w_o, rel_bias, mlp_w1, mlp_w2, ln_g, ln_b):
    bsz, t, d = x.shape
    k_sh = None
    v_sh = None
    for l in range(DEPTH):
        if l < N_A:
            i = l
            h = _rwkv7_time_mix(x, a_mu[i], a_w_r[i], a_w_k[i], a_w_v[i], a_w_o[i],
                                a_w0[i], a_w1[i], a_w2[i], a_a0[i], a_a1[i], a_a2[i],
                                a_g1[i], a_g2[i], a_k_k[i], a_k_a[i], a_r_k[i],
                                a_lnx_g[i], a_lnx_b[i])
        else:
            if l == N_A:
                kv = x @ b_w_kv
                k_sh = kv[..., :d].reshape(bsz, t, 2 * B_HEADS, B_HEAD_DIM)
                v_sh = kv[..., d:].reshape(bsz, t, B_HEADS, 2 * B_HEAD_DIM)
            j = l - N_A
            lambda_init = 0.8 - 0.6 * math.exp(-0.3 * l)
            h = _diff_attention(x, k_sh, v_sh, b_w_q[j], b_lam[j], b_subln_g[j], b_w_o[j],
                                rel_bias, lambda_init)
        x = _layer_norm(ALPHA * x + h, ln_g[l, 0], ln_b[l, 0], LN_EPS)
        x = _layer_norm(ALPHA * x + _sqrelu_mlp(x, mlp_w1[l], mlp_w2[l]), ln_g[l, 1], ln_b[l, 1], LN_EPS)
    return x
```

```python
import math
import numpy as np
from contextlib import ExitStack
import concourse.bass as bass
import concourse.mybir as mybir
from concourse.bass_utils import run_bass_kernel_spmd

F32 = mybir.dt.float32
BF16 = mybir.dt.bfloat16
AF = mybir.ActivationFunctionType
ALU = mybir.AluOpType
AX = mybir.AxisListType

D = 1024
KC = 8
DFF = 4096
DEPTH = 2
ALPHA = (2.0 * DEPTH) ** 0.25
C1 = math.exp(-0.5)
GN_EPS = 64e-5
LN_EPS = 1e-5
SUBLN_EPS = 1e-5
NDMA = 40
STAGE = 9
SUB = 9


class Sync:
    def __init__(self, nc, es):
        self.nc = nc
        self.eng = {"pe": nc.tensor, "act": nc.scalar, "dve": nc.vector, "pool": nc.gpsimd, "sp": nc.sync}
        self.sem = {k: es.enter_context(nc.semaphore("s_" + k)) for k in self.eng}
        self.cnt = {k: 0 for k in self.eng}
        self.dsem = [es.enter_context(nc.semaphore("s_dma%d" % i)) for i in range(2 * NDMA)]
        self.dcnt = [0] * (2 * NDMA)
        self.drr = {"sp": 0, "pool": 0, "act": 0}
        self.known = {k: {} for k in self.eng}
        self.res = {}
        self.bank = {}
        self.ninst = 0

    def _semof(self, sk):
        return self.sem[sk] if isinstance(sk, str) else self.dsem[sk]

    def _wait(self, e, deps):
        kn = self.known[e]
        for sk, val in deps.items():
            if sk == "pe" and e == "pe":
                continue
            if kn.get(sk, 0) >= val:
                continue
            self.eng[e].wait_ge(self._semof(sk), val)
            self.ninst += 1
            kn[sk] = val

    def _deps(self, e, r, w, banks):
        deps = {}

        def add(tok):
            if tok is None:
                return
            sk, val = tok
            if deps.get(sk, 0) < val:
                deps[sk] = val
        for k in r:
            st = self.res.get(k)
            if st:
                add(st[0])
        for k in w:
            st = self.res.get(k)
            if st:
                add(st[0])
                for t in st[1]:
                    add(t)
        for b in banks:
            for e2, c in self.bank.get(b, {}).items():
                if e2 != e:
                    add((e2, c))
        return deps

    def _commit(self, tok, r, w):
        for k in r:
            self.res.setdefault(k, [None, []])[1].append(tok)
        for k in w:
            self.res[k] = [tok, []]

    def op(self, e, fns, r=(), w=(), banks=()):
        self._wait(e, self._deps(e, r, w, banks))
        if not isinstance(fns, (list, tuple)):
            fns = [fns]
        ins = None
        for f in fns:
            ins = f(self.eng[e])
            self.ninst += 1
        self.cnt[e] += 1
        ins.then_inc(self.sem[e], 1)
        tok = (e, self.cnt[e])
        self._commit(tok, r, w)
        for b in banks:
            self.bank.setdefault(b, {})[e] = self.cnt[e]
        return tok

    def dma(self, q, out, in_, r=(), w=()):
        i = self.drr[q] + (NDMA if q == "pool" else 0)
        self.drr[q] = (self.drr[q] + 1) % NDMA
        deps = self._deps(q, r, w, ())
        if self.dcnt[i] and deps.get(i, 0) < self.dcnt[i]:
            deps[i] = self.dcnt[i]
        self._wait(q, deps)
        self.eng[q].dma_start(out=out, in_=in_).then_inc(self.dsem[i], 16)
        self.ninst += 1
        self.dcnt[i] += 16
        tok = (i, self.dcnt[i])
        self._commit(tok, r, w)
        return tok

    def barrier(self):
        for e in self.eng:
            deps = {k: c for k, c in self.cnt.items() if c > 0}
            for i in range(2 * NDMA):
                if self.dcnt[i]:
                    deps[i] = self.dcnt[i]
            kn = self.known[e]
            for sk, val in deps.items():
                if kn.get(sk, 0) >= val:
                    continue
                self.eng[e].wait_ge(self._semof(sk), val)
                kn[sk] = val
        self.res = {}
        self.bank = {}


class Ctx:
    pass


def _consts_np(NB):
    import ml_dtypes
    c = {}
    p = np.arange(128)
    same = (p[:, None] // 64) == (p[None, :] // 64)
    s_lt_t = (p[:, None] % 64) < (p[None, :] % 64)
    s_le_t = (p[:, None] % 64) <= (p[None, :] % 64)
    strict = (same & s_lt_t).astype(np.float32)
    incl = (same & s_le_t).astype(np.float32)
    lower = (same & s_lt_t.T).astype(np.float32)
    m12 = np.concatenate([strict, incl], 1)
    c["mask12"] = np.concatenate([m12, m12], 1).astype(ml_dtypes.bfloat16)
    c["mask3"] = np.concatenate([lower, lower], 1).astype(ml_dtypes.bfloat16)
    c["identf"] = np.eye(128, dtype=np.float32)
    c["identb"] = np.eye(128).astype(ml_dtypes.bfloat16)
    c["blockones"] = same.astype(np.float32)
    sel = np.zeros((128, 2), np.float32)
    sel[:64, 0] = 1
    sel[64:, 1] = 1
    c["sel"] = sel.astype(ml_dtypes.bfloat16)
    rm = np.ones((128, NB), np.float32)
    rm[:, ::64] = 0
    c["resetmask"] = rm
    c["neghalf"] = np.full((128, NB), -0.5, np.float32)
    c["antiI"] = np.ascontiguousarray(np.eye(128, dtype=np.float32)[::-1])
    c["maskd"] = ((p[:, None] // 64) <= (p[None, :] // 64)).astype(np.float32)
    return c


CONST_SPECS = None


def build_program(T, NSEQ, dbg=False, layers=(0, 1)):
    NB = 256
    NTOK = T * NSEQ
    nc = bass.Bass("TRN2", target_bir_lowering=False)
    cn = _consts_np(NB)

    def din(name, shape, dt=F32):
        return nc.dram_tensor(name, list(shape), dt, kind="ExternalInput").ap()

    def dscr(name, shape, dt):
        return nc.dram_tensor(name, list(shape), dt, kind="Internal").ap()

    g = Ctx()
    g.nc = nc
    g.T, g.NSEQ, g.NTOK, g.NB = T, NSEQ, NTOK, NB
    I = {}
    I["x"] = din("x", [NTOK, D])
    for nm in ("a_w_r", "a_w_k", "a_w_v", "a_w_o", "b_w_q", "b_w_o"):
        I[nm] = din(nm, [D, D])
    I["b_w_kv"] = din("b_w_kv", [D, 2 * D])
    I["a_w1"] = din("a_w1", [D, 64]); I["a_a1"] = din("a_a1", [D, 64]); I["a_g1"] = din("a_g1", [D, 128])
    I["a_w2"] = din("a_w2", [64, D]); I["a_a2"] = din("a_a2", [64, D]); I["a_g2"] = din("a_g2", [128, D])
    I["mlp_w1"] = din("mlp_w1", [2, D, DFF]); I["mlp_w2"] = din("mlp_w2", [2, DFF, D])
    I["pf"] = din("pf", [128, NPF * 8])
    I["pt"] = din("pt", [NPT, D])
    I["b_lam"] = din("b_lam", [1, 256])
    I["rel_bias"] = din("rel_bias", [32, 8])
    I["onehot"] = din("onehot", [32, 384])
    for k, v in cn.items():
        I["c_" + k] = din("c_" + k, v.shape, BF16 if v.dtype != np.float32 else F32)
    g.I = I
    out = nc.dram_tensor("out", [NTOK, D], F32, kind="ExternalOutput").ap()
    g.out = out
    g.XF = dscr("s_xf", [NTOK // 128, 64, 16, 4, 128], BF16)
    g.TMO = dscr("s_tmo", [4, NTOK, D], BF16)
    g.SG = dscr("s_sg", [128, NTOK], BF16)
    g.AO = dscr("s_ao", [NTOK // 128, 8, 128, 1024], BF16)
    g.X1 = dscr("s_x1", [NTOK, D], F32)
    g.X1T = dscr("s_x1t", [8, 128, NTOK], BF16)
    g.X2 = dscr("s_x2", [NTOK, D], F32)
    g.X2T = dscr("s_x2t", [8, 128, NTOK], BF16)
    g.X3 = dscr("s_x3", [NTOK, D], F32)
    g.X3T = dscr("s_x3t", [8, 128, NTOK], BF16)
    if dbg:
        g.dbg = {}
        g.dbg["h0"] = nc.dram_tensor("dbg_h0", [NTOK, D], F32, kind="ExternalOutput").ap()
        g.X2 = nc.dram_tensor("dbg_x2", [NTOK, D], F32, kind="ExternalOutput").ap()

    with ExitStack() as es:
        sy = Sync(nc, es)
        g.sy = sy
        g.ps = [es.enter_context(nc.psum_tensor("psb%d" % b, [128, 512], F32)) for b in range(8)]
        used = {}

        def sb(name, shape, dt, stack=es):
            used[name] = used.get(name, 0) + 1
            if used[name] > 1:
                name = "%s_v%d" % (name, used[name])
            return stack.enter_context(nc.sbuf_tensor(name, list(shape), dt))
        g.sb = sb
        g.C = {}
        for k, v in cn.items():
            t = sb("k_" + k, v.shape, BF16 if v.dtype != np.float32 else F32)
            sy.dma("sp", t[:], I["c_" + k][:, :], w=[("c", k)])
            g.C[k] = t
        g.pf = sb("pfs", [128, NPF, 8], F32)
        sy.dma("sp", g.pf[:], I["pf"].rearrange("p (v c) -> p v c", c=8), w=["pf"])
        for b in range(8):
            sy.op("dve", lambda e, b=b: e.memset(g.ps[b][:], 0.0), w=[("ps", b)], banks=[b])
        sy.barrier()
        if 0 in layers and STAGE >= 2:
            rwkv_layer(g)
            sy.barrier()
            if STAGE >= 4:
                mlp_layer(g, 0, g.X1, g.X1T, g.X2, g.X2T, final=False)
            sy.barrier()
        if 1 in layers:
            attn_layer(g)
            sy.barrier()
            mlp_layer(g, 1, g.X3, g.X3T, g.out, None, final=True)
        sy.barrier()
    g.ninst = sy.ninst
    return nc, g


PF_NAMES = ["mu0", "mu1", "mu2", "mu3", "mu4", "mu5", "w0", "a0", "k_k", "k_a", "r_k",
            "ln_g00", "ln_b00", "ln_g01", "ln_b01", "ln_g10", "ln_b10", "ln_g11", "ln_b11"]
NPF = len(PF_NAMES)
PT_NAMES = ["lnx_g", "lnx_b", "ln_g00", "ln_b00", "ln_g01", "ln_b01", "ln_g10", "ln_b10", "ln_g11", "ln_b11",
            "subln"]
NPT = len(PT_NAMES)


def PFI(name):
    return PF_NAMES.index(name)


def PTI(name):
    return PT_NAMES.index(name)


class Banks:
    def __init__(self, ids):
        self.ids = list(ids)
        self.i = 0

    def next(self):
        b = self.ids[self.i % len(self.ids)]
        self.i += 1
        return b


def load_w(g, es, name, src, kcs, n):
    t = g.sb(name, [128, kcs, n], BF16, es)
    g.sy.dma("pool", t[:], src.rearrange("(kc p) n -> p kc n", p=128), w=[name])
    return t


def bcast_rows(g, es, name, row):
    t = g.sb(name, [128, D], F32, es)
    g.sy.dma("sp", t[:], g.I["pt"][row:row + 1, :].partition_broadcast(128), w=[name])
    return t


def resid_ln(g, es_bufs, z, zkey, li, out_tm, out_fm, tk0, bk, tag, use_sqrt=True):
    sy, nc, pf = g.sy, g.nc, g.pf
    B = es_bufs
    q = B["cnt"][0] % B["nbuf"]
    B["cnt"][0] += 1
    tq_ = "%s_%d" % (tag, q)
    st, mv, sm = B["st"][q], B["mv"][q], B["sm"][q]
    sy.op("dve", [lambda e, h=h: e.bn_stats(out=st[:, h, :], in_=z[:, h * 512:(h + 1) * 512]) for h in range(2)],
          r=[zkey], w=[tq_ + "st"])
    sy.op("dve", lambda e: e.bn_aggr(out=mv[:, :], in_=st[:].rearrange("p a b -> p (a b)")), r=[tq_ + "st"], w=[tq_ + "mv"])
    sy.op("dve", lambda e: e.tensor_scalar(out=sm[:, 0:1], in0=mv[:, 1:2], scalar1=LN_EPS, scalar2=None, op0=ALU.add),
          r=[tq_ + "mv"], w=[tq_ + "sm0"])
    if use_sqrt:
        sy.op("act", lambda e: e.activation(out=sm[:, 3:4], in_=sm[:, 0:1], func=AF.Sqrt), r=[tq_ + "sm0"], w=[tq_ + "sm3"])
        sy.op("dve", lambda e: e.reciprocal(out=sm[:, 1:2], in_=sm[:, 3:4]), r=[tq_ + "sm3"], w=[tq_ + "sm1"])
    else:
        sy.op("pool", lambda e: e.tensor_tensor(out=sm[:, 1:2], in0=sm[:, 0:1], in1=g.C["neghalf"][:, 0:1], op=ALU.pow),
              r=[tq_ + "sm0", ("c", "neghalf")], w=[tq_ + "sm1"])
    sy.op("dve", lambda e: e.scalar_tensor_tensor(out=sm[:, 2:3], in0=mv[:, 0:1], scalar=-1.0, in1=sm[:, 1:2],
                                                  op0=ALU.mult, op1=ALU.mult), r=[tq_ + "mv", tq_ + "sm1"], w=[tq_ + "sm2"])
    xn = B["xn"][q]
    sy.op("act", lambda e: e.activation(out=xn[:, :], in_=z[:, :], func=AF.Identity, bias=sm[:, 2:3], scale=sm[:, 1:2]),
          r=[zkey, tq_ + "sm1", tq_ + "sm2"], w=[tq_ + "xn"])
    gi, bi = PFI("ln_g%d%d" % li), PFI("ln_b%d%d" % li)
    xo = B["xo"][q]
    sy.op("dve", lambda e: e.tensor_tensor(out=xo[:, :], in0=xn[:, :], in1=B["lng"][:, :], op=ALU.mult),
          r=[tq_ + "xn", tag + "lng"], w=[tq_ + "xo"])
    sy.op("pool", lambda e: e.tensor_tensor(out=xo[:, :], in0=xo[:, :], in1=B["lnb"][:, :], op=ALU.add),
          r=[tq_ + "xo", tag + "lnb"], w=[tq_ + "xo"])
    sy.dma("sp", out_tm[tk0:tk0 + 128, :], xo[:, :], r=[tq_ + "xo"], w=[("dram", out_tm.name, tk0)])
    if out_fm is not None:
        xT = B["xT"][q]
        for half in range(2):
            b = bk.next()
            sy.op("pe", [lambda e, j=j, b=b, half=half: e.transpose(out=g.ps[b][:, j * 128:(j + 1) * 128],
                                                                      in_=xn[:, (half * 4 + j) * 128:(half * 4 + j + 1) * 128],
                                                                      identity=g.C["identf"][:, :]) for j in range(4)],
                  r=[tq_ + "xn", ("c", "identf")], w=[("ps", b)], banks=[b])
            sy.op("act", [lambda e, j=j, b=b, half=half: e.activation(
                out=xT[:, half * 4 + j, :], in_=g.ps[b][:, j * 128:(j + 1) * 128], func=AF.Identity,
                bias=pf[:, bi, half * 4 + j:half * 4 + j + 1], scale=pf[:, gi, half * 4 + j:half * 4 + j + 1]) for j in range(4)],
                  r=[("ps", b), "pf"], w=[tq_ + "xT"], banks=[b])
        sy.dma("sp", out_fm[:, :, tk0:tk0 + 128].rearrange("c p t -> p c t"), xT[:, :, :], r=[tq_ + "xT"],
               w=[("dram", out_fm.name, tk0)])


def ln_bufs(g, es, li, tag, nbuf=1):
    B = {"nbuf": nbuf, "cnt": [0]}
    B["st"] = [g.sb(tag + "st%d" % i, [128, 2, 6], F32, es) for i in range(nbuf)]
    B["mv"] = [g.sb(tag + "mv%d" % i, [128, 2], F32, es) for i in range(nbuf)]
    B["sm"] = [g.sb(tag + "sm%d" % i, [128, 4], F32, es) for i in range(nbuf)]
    B["xn"] = [g.sb(tag + "xn%d" % i, [128, D], F32, es) for i in range(nbuf)]
    B["xo"] = [g.sb(tag + "xo%d" % i, [128, D], F32, es) for i in range(nbuf)]
    B["xT"] = [g.sb(tag + "xT%d" % i, [128, 8, 128], BF16, es) for i in range(nbuf)]
    B["lng"] = bcast_rows(g, es, tag + "lng", PTI("ln_g%d%d" % li))
    B["lnb"] = bcast_rows(g, es, tag + "lnb", PTI("ln_b%d%d" % li))
    return B


def rwkv_layer(g):
    sy, nc, C, pf, I = g.sy, g.nc, g.C, g.pf, g.I
    T, NSEQ, NTOK, NB = g.T, g.NSEQ, g.NTOK, g.NB
    NTB = NB // 128
    NCH = NB // 64
    ps = g.ps
    with ExitStack() as esl:
        GC = g.sb("GC", [128, 8, NTOK // 64], F32, esl)
        S_all = g.sb("S_all", [128, NTOK // 128, 16], F32, esl)
        sgT = None
        with ExitStack() as es:
            sb = lambda n, s, d: g.sb(n, s, d, es)
            Wr = load_w(g, es, "Wr", I["a_w_r"], 8, D)
            Wk = load_w(g, es, "Wk", I["a_w_k"], 8, D)
            Wv = load_w(g, es, "Wv", I["a_w_v"], 8, D)
            w1 = load_w(g, es, "w1", I["a_w1"], 8, 64)
            a1 = load_w(g, es, "a1", I["a_a1"], 8, 64)
            g1 = load_w(g, es, "g1", I["a_g1"], 8, 128)
            w2 = sb("w2", [64, D], BF16); sy.dma("pool", w2[:], I["a_w2"][:, :], w=["w2"])
            a2 = sb("a2", [64, D], BF16); sy.dma("pool", a2[:], I["a_a2"][:, :], w=["a2"])
            dv = sb("dv", [128, 3, 8], F32)
            sy.op("dve", lambda e: e.tensor_scalar(out=dv[:, 0, :], in0=pf[:, PFI("w0"), :], scalar1=0.5, scalar2=None, op0=ALU.mult), r=["pf"], w=["dv"])
            sy.op("dve", lambda e: e.tensor_scalar(out=dv[:, 1, :], in0=pf[:, PFI("a0"), :], scalar1=0.5, scalar2=None, op0=ALU.mult), r=["pf"], w=["dv"])
            sy.op("dve", lambda e: e.tensor_scalar(out=dv[:, 2, :], in0=pf[:, PFI("k_a"), :], scalar1=-1.0, scalar2=1.0, op0=ALU.mult, op1=ALU.add), r=["pf"], w=["dv"])
            XS = [sb("XS%d" % i, [128, 8, NB + 1], BF16) for i in range(2)]
            xx = sb("xx", [128, 8, NB], BF16)
            xm = [sb("xm%d" % i, [128, 8, NB], BF16) for i in range(2)]
            xmk = [sb("xmk%d" % i, [128, 8, NB], BF16) for i in range(2)]
            xt = [sb("xt%d" % i, [128, D], F32) for i in range(4)]
            Rfs = [sb("Rf%d" % i, [128, 8, NB], F32) for i in range(2)]
            h1s = [sb("h1_%d" % i, [64, NB], BF16) for i in range(2)]
            h2s = [sb("h2_%d" % i, [64, NB], BF16) for i in range(2)]
            tgb = sb("tgb", [128, NB], F32)
            Vt = [sb("Vt%d" % i, [128, D], BF16) for i in range(2)]
            XFb = [sb("XFb%d" % i, [128, 4, NB], BF16) for i in range(2)]
            TMb = sb("TMb", [128, NTB, 3, D], BF16)
            tn = ["tw", "ta", "CS", "CSx", "Dh", "E1", "E2", "Ep", "Eh", "kkraw", "sq", "ssb", "rn", "kk", "t1", "k2",
                  "kb", "Bhat", "Khat", "Atm", "tg"]
            tms = [{n: sb("t%d_" % i + n, [128, NB], F32) for n in tn} for i in range(2)]
            tm = tms[0]
            prods = [sb("prod%d" % i, [128, NB], BF16) for i in range(2)]
            sgb = [sb("sgb%d" % i, [128, NB], BF16) for i in range(2)]
            bk = Banks([6])
            SB_ = 7
            bkos = [Banks([1, 2]), Banks([4, 5])]

            def genPre(blk, s, b):
                bp = blk % 2
                Rf, h1, h2 = Rfs[bp], h1s[bp], h2s[bp]
                if True:
                    tok0 = s * T + b * NB
                    XSc, XSp = XS[blk % 2], XS[(blk + 1) % 2]
                    kXS = ("XS", blk % 2)
                    if b == 0:
                        sy.op("pool", lambda e: e.memset(XSc[:, :, 0:1], 0.0), w=[kXS])
                        yield
                    else:
                        sy.op("pool", lambda e: e.tensor_copy(out=XSc[:, :, 0:1], in_=XSp[:, :, NB:NB + 1]),
                              r=[("XS", (blk + 1) % 2)], w=[kXS])
                        yield
                    def xload(bi_):
                        s2_, b2_ = blocks[bi_]
                        t0_ = s2_ * T + b2_ * NB
                        for tt_ in range(NTB):
                            xi_ = (bi_ % 2) * 2 + tt_
                            sy.dma("sp", xt[xi_][:, :], I["x"][t0_ + tt_ * 128: t0_ + (tt_ + 1) * 128, :], w=[("xt", xi_)])
                    if blk == 0:
                        xload(0)
                    if blk + 1 < len(blocks):
                        xload(blk + 1)
                    for tt in range(NTB):
                        xti = xt[(blk % 2) * 2 + tt]
                        for half in range(2):
                            bb = bk.next()
                            sy.op("pe", [lambda e, j=j, bb=bb, half=half, xti=xti: e.transpose(
                                out=ps[bb][:, j * 128:(j + 1) * 128], in_=xti[:, (half * 4 + j) * 128:(half * 4 + j + 1) * 128],
                                identity=C["identf"][:, :]) for j in range(4)],
                                r=[("xt", (blk % 2) * 2 + tt), ("c", "identf")], w=[("ps", bb)], banks=[bb])
                            sy.op("act", lambda e, bb=bb, half=half, tt=tt: e.activation(
                                out=XSc[:, half * 4:(half + 1) * 4, 1 + tt * 128:1 + (tt + 1) * 128],
                                in_=ps[bb][:, :].rearrange("p (j t) -> p j t", t=128), func=AF.Copy),
                                r=[("ps", bb)], w=[kXS], banks=[bb])
                            yield
                    sy.op("dve", lambda e: e.tensor_tensor(out=xx[:, :, :], in0=XSc[:, :, 0:NB], in1=XSc[:, :, 1:NB + 1], op=ALU.subtract),
                          r=[kXS], w=["xx"])
                    yield

                    def mix(mi, di, dst=None, key=None):
                        dst = xm[di] if dst is None else dst
                        key = ("xm", di) if key is None else key
                        sy.op("dve", [lambda e, kc=kc: e.scalar_tensor_tensor(
                            out=dst[:, kc, :], in0=xx[:, kc, :], scalar=pf[:, mi, kc:kc + 1], in1=XSc[:, kc, 1:NB + 1],
                            op0=ALU.mult, op1=ALU.add) for kc in range(8)], r=["xx", kXS, "pf"], w=[key])
                        return dst, key

                    xa_, kx = mix(1, 0)
                    bb = bk.next()
                    sy.op("pe", [lambda e, kc=kc, bb=bb, xa_=xa_: e.matmul(ps[bb][0:64, 0:NB], lhsT=w1[:, kc, :], rhs=xa_[:, kc, :],
                                                                         start=(kc == 0), stop=(kc == 7)) for kc in range(8)],
                          r=[kx, "w1"], w=[("ps", bb)], banks=[bb])
                    yield
                    sy.op("act", lambda e, bb=bb: e.activation(out=h1[:, :], in_=ps[bb][0:64, 0:NB], func=AF.Tanh),
                          r=[("ps", bb)], w=[("h1", bp)], banks=[bb])
                    yield
                    xa_, kx = mix(4, 1)
                    bb = bk.next()
                    sy.op("pe", [lambda e, kc=kc, bb=bb, xa_=xa_: e.matmul(ps[bb][0:64, 0:NB], lhsT=a1[:, kc, :], rhs=xa_[:, kc, :],
                                                                         start=(kc == 0), stop=(kc == 7)) for kc in range(8)],
                          r=[kx, "a1"], w=[("ps", bb)], banks=[bb])
                    yield
                    sy.op("act", lambda e, bb=bb: e.activation(out=h2[:, :], in_=ps[bb][0:64, 0:NB], func=AF.Copy),
                          r=[("ps", bb)], w=[("h2", bp)], banks=[bb])
                    yield
                    xa_, kx = mix(5, 0)
                    bb = bk.next()
                    sy.op("pe", [lambda e, kc=kc, bb=bb, xa_=xa_: e.matmul(ps[bb][:, 0:NB], lhsT=g1[:, kc, :], rhs=xa_[:, kc, :],
                                                                         start=(kc == 0), stop=(kc == 7)) for kc in range(8)],
                          r=[kx, "g1"], w=[("ps", bb)], banks=[bb])
                    yield
                    sy.op("act", lambda e, bb=bb: e.activation(out=tgb[:, :], in_=ps[bb][:, 0:NB], func=AF.Tanh, scale=0.5),
                          r=[("ps", bb)], w=["tg"], banks=[bb])
                    yield
                    sgi = sgb[blk % 2]
                    sy.op("dve", lambda e, sgi=sgi: e.tensor_scalar(out=sgi[:, :], in0=tgb[:, :], scalar1=0.5, scalar2=0.5,
                                                                    op0=ALU.mult, op1=ALU.add), r=["tg"], w=[("sgb", blk % 2)])
                    yield
                    sy.dma("sp", g.SG[:, tok0:tok0 + NB], sgi[:, :], r=[("sgb", blk % 2)], w=[("dram", "sg", tok0)])
                    yield
                    xa_, kx = mix(3, 1)
                    for tt in range(NTB):
                        vti = Vt[tt % 2]
                        for half in range(2):
                            bb = bk.next()
                            sy.op("pe", [lambda e, kc=kc, bb=bb, xa_=xa_, tt=tt, half=half: e.matmul(
                                ps[bb][:, :], lhsT=xa_[:, kc, tt * 128:(tt + 1) * 128], rhs=Wv[:, kc, half * 512:(half + 1) * 512],
                                start=(kc == 0), stop=(kc == 7)) for kc in range(8)],
                                r=[kx, "Wv"], w=[("ps", bb)], banks=[bb])
                            sy.op("act", lambda e, bb=bb, vti=vti, half=half: e.activation(
                                out=vti[:, half * 512:(half + 1) * 512], in_=ps[bb][:, :], func=AF.Copy),
                                r=[("ps", bb)], w=[("Vt", tt % 2)], banks=[bb])
                        sy.dma("sp", g.TMO[0, tok0 + tt * 128:tok0 + (tt + 1) * 128, :], vti[:, :], r=[("Vt", tt % 2)],
                               w=[("dram", "tmo0", tok0 + tt * 128)])
                        yield
                    xa_, kx = mix(0, 0)
                    for oc in range(8):
                        bb = bk.next()
                        sy.op("pe", [lambda e, kc=kc, bb=bb, xa_=xa_, oc=oc: e.matmul(
                            ps[bb][:, 0:NB], lhsT=Wr[:, kc, oc * 128:(oc + 1) * 128], rhs=xa_[:, kc, :],
                            start=(kc == 0), stop=(kc == 7)) for kc in range(8)], r=[kx, "Wr"], w=[("ps", bb)], banks=[bb])
                        sy.op("act", lambda e, bb=bb, oc=oc: e.activation(out=Rf[:, oc, :], in_=ps[bb][:, 0:NB], func=AF.Copy),
                              r=[("ps", bb)], w=[("Rf", bp, oc)], banks=[bb])
                        yield
                    mix(2, 1, xmk[bp], ("xmk", bp))
                    yield

            def genMain(blk, s, b):
                bp = blk % 2
                Rf, h1, h2 = Rfs[bp], h1s[bp], h2s[bp]
                xa_, kx = xmk[bp], ("xmk", bp)
                tok0 = s * T + b * NB
                if True:
                    def genOC(oc, t, prod, tsi):
                        bko = bkos[tsi]
                        bK, bW, bA = (0, 3)[tsi], bko.next(), bko.next()
                        yield
                        sy.op("pe", [lambda e, kc=kc, xa_=xa_, oc=oc, bK=bK: e.matmul(
                            ps[bK][:, 0:NB], lhsT=Wk[:, kc, oc * 128:(oc + 1) * 128], rhs=xa_[:, kc, :],
                            start=(kc == 0), stop=(kc == 7)) for kc in range(8)], r=[kx, "Wk"], w=[("ps", bK)], banks=[bK])
                        yield
                        sy.op("pe", lambda e, oc=oc, bW=bW: e.matmul(ps[bW][:, 0:NB], lhsT=w2[:, oc * 128:(oc + 1) * 128], rhs=h1[:, :],
                                                                    start=True, stop=True), r=["w2", ("h1", bp)], w=[("ps", bW)], banks=[bW])
                        yield
                        sy.op("pe", lambda e, oc=oc, bA=bA: e.matmul(ps[bA][:, 0:NB], lhsT=a2[:, oc * 128:(oc + 1) * 128], rhs=h2[:, :],
                                                                    start=True, stop=True), r=["a2", ("h2", bp)], w=[("ps", bA)], banks=[bA])
                        yield
                        sy.op("act", lambda e, oc=oc, bW=bW: e.activation(out=t["tw"][:, :], in_=ps[bW][:, 0:NB], func=AF.Tanh,
                                                                          bias=dv[:, 0, oc:oc + 1], scale=0.5),
                              r=[("ps", bW), "dv"], w=[(tsi, "tw")], banks=[bW])
                        yield
                        sy.op("act", lambda e, oc=oc, bA=bA: e.activation(out=t["ta"][:, :], in_=ps[bA][:, 0:NB], func=AF.Tanh,
                                                                          bias=dv[:, 1, oc:oc + 1], scale=0.5),
                              r=[("ps", bA), "dv"], w=[(tsi, "ta")], banks=[bA])
                        yield
                        sy.op("act", lambda e, oc=oc, bK=bK: e.activation(out=t["kkraw"][:, :], in_=ps[bK][:, 0:NB], func=AF.Identity,
                                                                          scale=pf[:, PFI("k_k"), oc:oc + 1]),
                              r=[("ps", bK), "pf"], w=[(tsi, "kkraw")], banks=[bK])
                        yield
                        sy.op("dve", lambda e: e.tensor_scalar(out=t["tw"][:, :], in0=t["tw"][:, :], scalar1=0.5, scalar2=0.5,
                                                               op0=ALU.mult, op1=ALU.add), r=[(tsi, "tw")], w=[(tsi, "tw")])
                        yield
                        sy.op("dve", lambda e: e.tensor_scalar(out=t["ta"][:, :], in0=t["ta"][:, :], scalar1=0.5, scalar2=0.5,
                                                               op0=ALU.mult, op1=ALU.add), r=[(tsi, "ta")], w=[(tsi, "ta")])
                        yield
                        sy.op("pool", lambda e: e.tensor_tensor(out=t["sq"][:, :], in0=t["kkraw"][:, :], in1=t["kkraw"][:, :], op=ALU.mult),
                              r=[(tsi, "kkraw")], w=[(tsi, "sq")])
                        yield
                        bS = bko.next()
                        yield
                        sy.op("pe", lambda e, bS=bS: e.matmul(ps[bS][:, 0:NB], lhsT=C["blockones"][:, :], rhs=t["sq"][:, :], start=True, stop=True),
                              r=[(tsi, "sq"), ("c", "blockones")], w=[("ps", bS)], banks=[bS])
                        yield
                        sy.op("dve", lambda e: e.tensor_tensor_scan(out=t["CS"][:, :], data0=C["resetmask"][:, :], data1=t["tw"][:, :],
                                                                    initial=0.0, op0=ALU.mult, op1=ALU.add),
                              r=[(tsi, "tw"), ("c", "resetmask")], w=[(tsi, "CS")])
                        yield
                        sy.op("dve", lambda e, oc=oc: e.tensor_scalar(out=t["t1"][:, :], in0=t["ta"][:, :], scalar1=pf[:, PFI("k_a"), oc:oc + 1],
                                                                      scalar2=dv[:, 2, oc:oc + 1], op0=ALU.mult, op1=ALU.add),
                              r=[(tsi, "ta"), "pf", "dv"], w=[(tsi, "t1")])
                        yield
                        sy.op("dve", lambda e, bK=bK: e.tensor_tensor(out=t["k2"][:, :], in0=ps[bK][:, 0:NB], in1=t["t1"][:, :], op=ALU.mult),
                              r=[("ps", bK), (tsi, "t1")], w=[(tsi, "k2")], banks=[bK])
                        yield
                        sy.op("dve", lambda e, bS=bS: e.tensor_scalar(out=t["ssb"][:, :], in0=ps[bS][:, 0:NB], scalar1=1e-24, scalar2=None, op0=ALU.max),
                              r=[("ps", bS)], w=[(tsi, "ssb")], banks=[bS])
                        yield
                        sy.op("dve", lambda e: e.tensor_tensor(out=t["CSx"][:, :], in0=t["CS"][:, :], in1=t["tw"][:, :], op=ALU.subtract),
                              r=[(tsi, "CS"), (tsi, "tw")], w=[(tsi, "CSx")])
                        yield
                        cs3 = t["CS"][:, :].rearrange("p (c j) -> p c j", j=64)
                        yield
                        sy.op("dve", lambda e, cs3=cs3: e.tensor_tensor(out=t["Dh"][:, :].rearrange("p (c j) -> p c j", j=64),
                                                                        in0=cs3[:, :, 63:64].to_broadcast([128, NCH, 64]), in1=cs3,
                                                                        op=ALU.subtract), r=[(tsi, "CS")], w=[(tsi, "Dh")])
                        yield
                        sy.op("act", lambda e: e.activation(out=t["E1"][:, :], in_=t["CS"][:, :], func=AF.Exp, scale=-C1), r=[(tsi, "CS")], w=[(tsi, "E1")])
                        yield
                        sy.op("act", lambda e: e.activation(out=t["E2"][:, :], in_=t["CS"][:, :], func=AF.Exp, scale=C1), r=[(tsi, "CS")], w=[(tsi, "E2")])
                        yield
                        sy.op("act", lambda e: e.activation(out=t["sq"][:, :], in_=t["ssb"][:, :], func=AF.Sqrt), r=[(tsi, "ssb")], w=[(tsi, "sq")])
                        yield
                        sy.op("act", lambda e: e.activation(out=t["Ep"][:, :], in_=t["CSx"][:, :], func=AF.Exp, scale=-C1), r=[(tsi, "CSx")], w=[(tsi, "Ep")])
                        yield
                        sy.op("act", lambda e: e.activation(out=t["Eh"][:, :], in_=t["Dh"][:, :], func=AF.Exp, scale=-C1), r=[(tsi, "Dh")], w=[(tsi, "Eh")])
                        yield
                        ch0 = tok0 // 64
                        yield
                        sy.op("dve", lambda e, oc=oc, ch0=ch0: e.tensor_copy(
                            out=GC[:, oc, ch0:ch0 + NCH], in_=t["E1"][:, :].rearrange("p (c j) -> p c j", j=64)[:, :, 63]),
                            r=[(tsi, "E1")], w=[("GC", blk, oc)])
                        yield
                        sy.op("dve", lambda e: e.reciprocal(out=t["rn"][:, :], in_=t["sq"][:, :]), r=[(tsi, "sq")], w=[(tsi, "rn")])
                        yield
                        sy.op("pool", lambda e: e.tensor_tensor(out=t["kk"][:, :], in0=t["kkraw"][:, :], in1=t["rn"][:, :], op=ALU.mult),
                              r=[(tsi, "kkraw"), (tsi, "rn")], w=[(tsi, "kk")])
                        yield
                        sy.op("pool", lambda e: e.tensor_tensor(out=t["kb"][:, :], in0=t["kk"][:, :], in1=t["ta"][:, :], op=ALU.mult),
                              r=[(tsi, "kk"), (tsi, "ta")], w=[(tsi, "kb")])
                        yield
                        xfb = XFb[oc % 2]
                        yield
                        kxf = ("XFb", oc % 2)
                        yield
                        sy.op("pool", lambda e, xfb=xfb: e.tensor_tensor(out=xfb[:, 0, :], in0=t["kb"][:, :], in1=t["E2"][:, :], op=ALU.mult),
                              r=[(tsi, "kb"), (tsi, "E2")], w=[kxf])
                        yield
                        sy.op("pool", lambda e: e.tensor_tensor(out=t["Bhat"][:, :], in0=t["kb"][:, :], in1=t["Eh"][:, :], op=ALU.mult),
                              r=[(tsi, "kb"), (tsi, "Eh")], w=[(tsi, "Bhat")])
                        yield
                        sy.op("dve", lambda e, xfb=xfb: e.tensor_tensor(out=xfb[:, 1, :], in0=t["k2"][:, :], in1=t["E2"][:, :], op=ALU.mult),
                              r=[(tsi, "k2"), (tsi, "E2"), kxf], w=[kxf])
                        yield
                        sy.op("pool", lambda e: e.tensor_tensor(out=t["Khat"][:, :], in0=t["k2"][:, :], in1=t["Eh"][:, :], op=ALU.mult),
                              r=[(tsi, "k2"), (tsi, "Eh")], w=[(tsi, "Khat")])
                        yield
                        sy.op("dve", lambda e: e.scalar_tensor_tensor(out=t["Atm"][:, :], in0=t["kk"][:, :], scalar=-1.0, in1=t["Ep"][:, :],
                                                                      op0=ALU.mult, op1=ALU.mult), r=[(tsi, "kk"), (tsi, "Ep")], w=[(tsi, "Atm")])
                        yield
                        sy.op("act", lambda e, xfb=xfb: e.activation(out=xfb[:, 2, :], in_=t["Atm"][:, :], func=AF.Copy), r=[(tsi, "Atm"), kxf], w=[kxf])
                        yield
                        sy.op("dve", lambda e, xfb=xfb, oc=oc: e.tensor_tensor(out=xfb[:, 3, :], in0=Rf[:, oc, :], in1=t["E1"][:, :], op=ALU.mult),
                              r=[("Rf", bp, oc), (tsi, "E1"), kxf], w=[kxf])
                        yield
                        sy.op("dve", lambda e, oc=oc: e.scalar_tensor_tensor(out=prod[:, :], in0=Rf[:, oc, :], scalar=pf[:, PFI("r_k"), oc:oc + 1],
                                                                             in1=t["k2"][:, :], op0=ALU.mult, op1=ALU.mult),
                              r=[("Rf", bp, oc), (tsi, "k2"), "pf"], w=[(tsi, "prod")])
                        yield
                        sy.op("pe", [lambda e, tt=tt, oc=oc: e.matmul(ps[SB_][:, tt * 16 + oc * 2: tt * 16 + oc * 2 + 2],
                                                                      lhsT=prod[:, tt * 128:(tt + 1) * 128], rhs=C["sel"][:, :],
                                                                      start=True, stop=True) for tt in range(NTB)],
                              r=[(tsi, "prod"), ("c", "sel")], w=[("ps", SB_)], banks=[SB_])
                        yield
                        for tt in range(NTB):
                            bT = bko.next()
                            srcs = [t["Khat"], t["Bhat"], t["Atm"]]
                            sy.op("pe", [lambda e, j=j, tt=tt, bT=bT, srcs=srcs: e.transpose(
                                out=ps[bT][:, j * 128:(j + 1) * 128], in_=srcs[j][:, tt * 128:(tt + 1) * 128], identity=C["identf"][:, :])
                                for j in range(3)], r=[(tsi, "Khat"), (tsi, "Bhat"), (tsi, "Atm"), ("c", "identf")], w=[("ps", bT)], banks=[bT])
                            sy.op("act", lambda e, tt=tt, bT=bT, oc=oc: e.activation(
                                out=TMb[:, tt, :, oc * 128:(oc + 1) * 128], in_=ps[bT][:, 0:384].rearrange("p (j t) -> p j t", t=128),
                                func=AF.Copy), r=[("ps", bT)], w=[("TMb", tt, oc)], banks=[bT])
                            yield
                        for h_ in range(2):
                            for tt_ in range(NTB):
                                sy.dma("sp", g.XF[tok0 // 128 + tt_, :, 2 * oc + h_, :, :], xfb[h_ * 64:(h_ + 1) * 64, :, tt_ * 128:(tt_ + 1) * 128],
                                       r=[kxf], w=[("dram", "xf", oc, tok0, h_, tt_)])
                        yield
                    for oc in range(0, 8, 2):
                        alive = [genOC(oc, tms[0], prods[0], 0), genOC(oc + 1, tms[1], prods[1], 1)]
                        while alive:
                            for gn_ in list(alive):
                                try:
                                    next(gn_)
                                except StopIteration:
                                    alive.remove(gn_)
                            yield
                    for tt in range(NTB):
                        for j in range(3):
                            sy.dma("sp", g.TMO[1 + j, tok0 + tt * 128:tok0 + (tt + 1) * 128, :], TMb[:, tt, j, :], r=[("TMb", tt, o_) for o_ in range(8)],
                                   w=[("dram", "tmo%d" % (1 + j), tok0 + tt * 128)])
                    gt0 = tok0 // 128
                    sy.op("dve", lambda e, gt0=gt0: e.tensor_copy(out=S_all[:, gt0:gt0 + NTB, :],
                                                                  in_=ps[SB_][:, 0:NTB * 16].rearrange("p (t h) -> p t h", h=16)),
                          r=[("ps", SB_)], w=[("S_all", blk)], banks=[SB_])
                    yield

            blocks = [(s_, b_) for s_ in range(NSEQ) for b_ in range(T // NB)]
            for _ in genPre(0, *blocks[0]):
                pass
            for i_ in range(len(blocks)):
                nxt = genPre(i_ + 1, *blocks[i_ + 1]) if i_ + 1 < len(blocks) else None
                interleave(genMain(i_, *blocks[i_]), nxt, ratio=3)
        sy.barrier()
        if STAGE >= 3:
            rwkv_p2a(g)
            sy.barrier()
            rwkv_p2b(g, GC, S_all)


def interleave(ga, gb, ratio=4):
    da = ga is None
    db = gb is None
    while not (da and db):
        if not da:
            for _ in range(ratio):
                try:
                    next(ga)
                except StopIteration:
                    da = True
                    break
        if not db:
            try:
                next(gb)
            except StopIteration:
                db = True


def rwkv_p2_old(g, GC, S_all, sgT):
    sy, nc, C, pf, I = g.sy, g.nc, g.C, g.pf, g.I
    T, NSEQ, NTOK = g.T, g.NSEQ, g.NTOK
    ps = g.ps
    with ExitStack() as es:
        sb = lambda n, s, d: g.sb(n, s, d, es)
        Wo = load_w(g, es, "Wo", I["a_w_o"], 8, D)
        g2 = sb("g2", [128, D], BF16); sy.dma("pool", g2[:], I["a_g2"][:, :], w=["g2"])
        lnxg = bcast_rows(g, es, "lnxg", PTI("lnx_g"))
        lnxb = bcast_rows(g, es, "lnxb", PTI("lnx_b"))
        LB = ln_bufs(g, es, (0, 0), "l00")
        XFt = [sb("XFt%d" % i, [128, 16, 4, 128], BF16) for i in range(2)]
        GC2 = sb("GC2", [128, 16, NTOK // 64], F32)
        for d_ in range(2):
            for h_ in range(2):
                sy.dma("sp", GC2[d_ * 64:(d_ + 1) * 64].rearrange("k (o h) c -> k o h c", h=2)[:, :, h_, :],
                       GC[h_ * 64:(h_ + 1) * 64, :, :], w=["GC2"])
        xfsrc = g.XF.rearrange("o (h k) f t -> k (o h) f t", h=2)
        TMt = [sb("TMt%d" % i, [128, 4, D], BF16) for i in range(2)]
        sgt = [sb("sgt%d" % i, [128, 128], BF16) for i in range(2)]
        AM1 = [[sb("AM1_%d_%d" % (b, p), [128, 2, 2, 128], BF16) for p in range(8)] for b in range(2)]
        AM2 = [[sb("AM2_%d_%d" % (b, p), [128, 2, 2, 128], BF16) for p in range(8)] for b in range(2)]
        TTo = [[sb("TTo%d_%d" % (b, p), [128, 2, 128], BF16) for p in range(8)] for b in range(2)]
        AW = [[sb("AW%d_%d" % (b, p), [128, 256], BF16) for p in range(8)] for b in range(2)]
        DD = [[sb("DD%d_%d" % (p, i), [128, 2, 2, 128], BF16) for i in range(2)] for p in range(8)]
        TT = [[sb("TT%d_%d" % (p, i), [128, 2, 128], BF16) for i in range(2)] for p in range(8)]
        U = sb("U", [128, 16, 64], BF16)
        Hf = sb("Hf", [128, 16, 64], F32)
        Hbs = [sb("Hb%d" % i, [128, 16, 64], BF16) for i in range(2)]
        xres = sb("xres", [128, D], F32)
        Ysb = sb("Ysb", [128, D], F32)
        bv = LB["xo"][0]
        yT = sb("yT", [128, 8, 128], BF16)
        z = sb("z", [128, D], F32)
        ysq = z
        sm = sb("gnsm", [128, 6, 16], F32)
        bkA = Banks([2, 3, 4, 5])
        bkB = Banks([6, 7])
        tiles = [(s_, tl_) for s_ in range(NSEQ) for tl_ in range(T // 128)]

        def loads(i):
            s_, tl = tiles[i]
            tk0 = s_ * T + tl * 128
            xf, tmt = XFt[i % 2], TMt[i % 2]
            kxf, ktm = ("XFt", i % 2), ("TMt", i % 2)
            for d_ in range(2):
                for q_ in range(16):
                    sy.dma("sp", xf[d_ * 64:(d_ + 1) * 64, q_, :, :], xfsrc[:, q_, :, tk0:tk0 + 128], w=[kxf])
            for j in range(4):
                sy.dma("sp", tmt[:, j, :], g.TMO[j, tk0:tk0 + 128, :], w=[ktm])
            sy.dma("sp", sgt[i % 2][:, :], g.SG[:, tk0:tk0 + 128], w=[("sgt", i % 2)])

        def phaseA(i):
            b = i % 2
            xf, tmt = XFt[b], TMt[b]
            kxf, ktm = ("XFt", b), ("TMt", b)
            bk = bkA
            for pb in (range(0, 4), range(4, 8)):
                bl = {}
                for p in pb:
                    b1, b2, b3 = bk.next(), bk.next(), bk.next()
                    bl[p] = (b1, b2, b3)
                    fns1, fns2, fns3 = [], [], []
                    for h in range(2):
                        hh = 2 * p + h
                        fns1.append(lambda e, h=h, hh=hh, b1=b1: e.matmul(
                            ps[b1][:, h * 256:(h + 1) * 256], lhsT=xf[0:64, hh, 0, :], rhs=xf[0:64, hh, 2:4, :], start=True, stop=True))
                        fns2.append(lambda e, h=h, hh=hh, b2=b2: e.matmul(
                            ps[b2][:, h * 256:(h + 1) * 256], lhsT=xf[0:64, hh, 1, :], rhs=xf[0:64, hh, 2:4, :], start=True, stop=True))
                        fns3.append(lambda e, h=h, hh=hh, b3=b3: e.matmul(
                            ps[b3][:, h * 128:(h + 1) * 128], lhsT=xf[0:64, hh, 2, :], rhs=xf[0:64, hh, 0, :], start=True, stop=True))
                    sy.op("pe", fns1, r=[kxf], w=[("ps", b1)], banks=[b1])
                    yield
                    sy.op("dve", lambda e, p=p, b1=b1: e.tensor_tensor(out=AM1[b][p][:].rearrange("p a b c -> p (a b c)"), in0=ps[b1][:, :],
                                                                       in1=C["mask12"][:, :], op=ALU.mult),
                          r=[("ps", b1), ("c", "mask12")], w=[("AM1", b, p)], banks=[b1])
                    yield
                    sy.op("pe", fns2, r=[kxf], w=[("ps", b2)], banks=[b2])
                    yield
                    sy.op("dve", lambda e, p=p, b2=b2: e.tensor_tensor(out=AM2[b][p][:].rearrange("p a b c -> p (a b c)"), in0=ps[b2][:, :],
                                                                       in1=C["mask12"][:, :], op=ALU.mult),
                          r=[("ps", b2), ("c", "mask12")], w=[("AM2", b, p)], banks=[b2])
                    yield
                    sy.op("pe", fns3, r=[kxf], w=[("ps", b3)], banks=[b3])
                    yield
                    sy.op("dve", lambda e, p=p, b3=b3: e.tensor_tensor(out=DD[p][0][:, :, 1, :], in0=ps[b3][:, 0:256].rearrange("p (h t) -> p h t", t=128),
                                                                       in1=C["mask3"][:, :].rearrange("p (h t) -> p h t", t=128), op=ALU.mult),
                          r=[("ps", b3), ("c", "mask3")], w=[("DDt", p, 0)], banks=[b3])
                    yield
                    sy.op("act", lambda e, p=p: e.activation(out=DD[p][0][:, :, 0, :], in_=AM1[b][p][:, :, 0, :], func=AF.Copy),
                          r=[("AM1", b, p)], w=[("DDn", p, 0)])
                    yield
                    sy.op("pool", lambda e, p=p: e.tensor_tensor(out=TT[p][0][:, :, :], in0=AM1[b][p][:, :, 0, :],
                                                                 in1=C["identb"][:, :].unsqueeze(1).to_broadcast([128, 2, 128]), op=ALU.add),
                          r=[("AM1", b, p), ("c", "identb")], w=[("TT", p, 0)])
                    yield
            for k in range(5):
                ci, ni = k % 2, (k + 1) % 2
                for pb in (range(0, 4), range(4, 8)):
                    bqs = {}
                    for p in pb:
                        bq = bk.next()
                        bqs[p] = bq
                        fns = []
                        for h in range(2):
                            if k < 4:
                                fns.append(lambda e, h=h, p=p, bq=bq: e.matmul(ps[bq][:, h * 256:h * 256 + 128], lhsT=DD[p][ci][:, h, 1, :],
                                                                               rhs=DD[p][ci][:, h, 0, :], start=True, stop=True))
                            fns.append(lambda e, h=h, p=p, bq=bq: e.matmul(ps[bq][:, h * 256 + 128:h * 256 + 256], lhsT=DD[p][ci][:, h, 0, :],
                                                                           rhs=DD[p][ci][:, h, 1, :], start=True, stop=True))
                        sy.op("pe", fns, r=[("DDn", p, ci), ("DDt", p, ci)], w=[("ps", bq)], banks=[bq])
                        yield
                    for p in pb:
                        bq = bqs[p]
                        if k < 4:
                            sy.op("act", lambda e, p=p, bq=bq: e.activation(out=DD[p][ni][:].rearrange("p a b c -> p (a b c)"),
                                                                            in_=ps[bq][:, :], func=AF.Copy),
                                  r=[("ps", bq)], w=[("DDn", p, ni), ("DDt", p, ni)], banks=[bq])
                        else:
                            sy.op("act", lambda e, p=p, bq=bq: e.activation(
                                out=DD[p][ni][:, :, 1, :], in_=ps[bq][:, :].rearrange("p (h x t) -> p h x t", h=2, x=2)[:, :, 1, :], func=AF.Copy),
                                r=[("ps", bq)], w=[("DDt", p, ni)], banks=[bq])
                        yield
                    bts = {}
                    for p in pb:
                        bt = bk.next()
                        bts[p] = bt
                        fns = []
                        for h in range(2):
                            fns.append(lambda e, h=h, p=p, bt=bt: e.matmul(ps[bt][:, h * 128:(h + 1) * 128], lhsT=DD[p][ni][:, h, 1, :],
                                                                           rhs=TT[p][ci][:, h, :], start=True, stop=True))
                        sy.op("pe", fns, r=[("TT", p, ci), ("DDt", p, ni)], w=[("ps", bt)], banks=[bt])
                        yield
                    for p in pb:
                        bt = bts[p]
                        dst = TTo[b][p] if k == 4 else TT[p][ni]
                        kd = ("TTo", b, p) if k == 4 else ("TT", p, ni)
                        sy.op("dve", lambda e, p=p, bt=bt, dst=dst: e.tensor_tensor(out=dst[:].rearrange("p h t -> p (h t)"), in0=ps[bt][:, 0:256],
                                                                                   in1=TT[p][ci][:].rearrange("p h t -> p (h t)"), op=ALU.add),
                              r=[("ps", bt), ("TT", p, ci)], w=[kd], banks=[bt])
                        yield
            for p in range(8):
                ba = bk.next()
                fns = []
                for h in range(2):
                    hh = 2 * p + h
                    fns.append(lambda e, h=h, p=p, ba=ba, hh=hh: e.matmul(ps[ba][:, h * 64:(h + 1) * 64], lhsT=AM2[b][p][:, h, 0, :],
                                                                          rhs=tmt[:, 0, hh * 64:(hh + 1) * 64], start=True, stop=True))
                    for c_ in range(2):
                        fns.append(lambda e, h=h, p=p, ba=ba, hh=hh, c_=c_: e.matmul(
                            ps[ba][c_ * 64:(c_ + 1) * 64, 128 + h * 64:128 + (h + 1) * 64], lhsT=tmt[:, 3, hh * 64:(hh + 1) * 64],
                            rhs=TTo[b][p][:, h, c_ * 64:(c_ + 1) * 64], start=True, stop=True))
                sy.op("pe", fns, r=[("AM2", b, p), ktm, ("TTo", b, p)], w=[("ps", ba)], banks=[ba])
                yield
                sy.op("act", lambda e, p=p, ba=ba: e.activation(out=AW[b][p][:, :], in_=ps[ba][:, 0:256], func=AF.Copy),
                      r=[("ps", ba)], w=[("AW", b, p)], banks=[ba])
                yield

        def phaseBC(i):
            s_, tl = tiles[i]
            tk0 = s_ * T + tl * 128
            b = i % 2
            gt = tk0 // 128
            xf, tmt = XFt[b], TMt[b]
            kxf, ktm = ("XFt", b), ("TMt", b)
            bk = bkB
            if tl == 0:
                sy.op("pool", lambda e: e.memset(Hf[:, :, :], 0.0), w=[("Hf", 0), ("Hf", 1)])
                sy.op("pool", lambda e: e.memset(Hbs[0][:, :, :], 0.0), w=[("Hb", 0)])
            kA = [("AW", b, q) for q in range(8)] + [("TTo", b, q) for q in range(8)]
            kAM = [("AM1", b, q) for q in range(8)] + [("AM2", b, q) for q in range(8)]
            for c in range(2):
                cr = slice(c * 64, (c + 1) * 64)
                ch = tk0 // 64 + c
                Hb, Hbn = Hbs[c], Hbs[1 - c]
                kHb, kHbn = ("Hb", c), ("Hb", 1 - c)
                ub = [bk.next(), bk.next()]
                for hb in range(2):
                    fns = []
                    for h8 in range(8):
                        hh = hb * 8 + h8
                        p, h = hh // 2, hh % 2
                        fns.append(lambda e, p=p, h=h, hh=hh, h8=h8, hb=hb: e.matmul(
                            ps[ub[hb]][cr, h8 * 64:(h8 + 1) * 64], lhsT=AW[b][p][cr, 128 + h * 64:128 + (h + 1) * 64], rhs=Hb[cr, hh, :],
                            start=True, stop=False))
                        fns.append(lambda e, p=p, h=h, h8=h8, hb=hb: e.matmul(
                            ps[ub[hb]][cr, h8 * 64:(h8 + 1) * 64], lhsT=TTo[b][p][cr, h, c * 64:(c + 1) * 64], rhs=AW[b][p][cr, h * 64:(h + 1) * 64],
                            start=False, stop=True))
                    sy.op("pe", fns, r=kA + [kHb], w=[("ps", ub[hb])], banks=[ub[hb]])
                    yield
                    sy.op("act", lambda e, hb=hb: e.activation(out=U[cr, hb * 8:(hb + 1) * 8, :].rearrange("p a b -> p (a b)"),
                                                               in_=ps[ub[hb]][cr, :], func=AF.Copy),
                          r=[("ps", ub[hb])], w=[("U", hb)], banks=[ub[hb]])
                    yield
                bhs = [bk.next(), bk.next()]
                sy.op("pool", lambda e, ch=ch: e.tensor_tensor(out=Hf[:, :, :], in0=Hf[:, :, :],
                                                               in1=GC2[:, :, ch:ch + 1].to_broadcast([128, 16, 64]), op=ALU.mult),
                      r=[("Hf", 0), ("Hf", 1), "GC2"], w=[("Hf", 0), ("Hf", 1)])
                yield
                for hb in range(2):
                    bh = bhs[hb]
                    fns = []
                    for h8 in range(8):
                        hh = hb * 8 + h8
                        p, h = hh // 2, hh % 2
                        for d_ in range(2):
                            ho = ps[bh][d_ * 64:(d_ + 1) * 64, h8 * 64:(h8 + 1) * 64]
                            fns.append(lambda e, ho=ho, hh=hh: e.matmul(ho, lhsT=tmt[cr, 2, hh * 64:(hh + 1) * 64], rhs=U[cr, hh, :],
                                                                        start=True, stop=False))
                            fns.append(lambda e, ho=ho, hh=hh: e.matmul(ho, lhsT=tmt[cr, 1, hh * 64:(hh + 1) * 64],
                                                                        rhs=tmt[cr, 0, hh * 64:(hh + 1) * 64], start=False, stop=True))
                    sy.op("pe", fns, r=[ktm, ("U", hb)], w=[("ps", bh)], banks=[bh])
                    yield
                    sy.op("dve", lambda e, hb=hb, bh=bh: e.tensor_tensor(out=Hf[:, hb * 8:(hb + 1) * 8, :].rearrange("p a b -> p (a b)"), in0=ps[bh][:, :],
                                                                         in1=Hf[:, hb * 8:(hb + 1) * 8, :].rearrange("p a b -> p (a b)"), op=ALU.add),
                          r=[("ps", bh), ("Hf", hb)], w=[("Hf", hb)], banks=[bh])
                    yield
                    sy.op("act", lambda e, hb=hb, Hbn=Hbn: e.activation(out=Hbn[:, hb * 8:(hb + 1) * 8, :].rearrange("p a b -> p (a b)"),
                                                                        in_=Hf[:, hb * 8:(hb + 1) * 8, :].rearrange("p a b -> p (a b)"), func=AF.Copy),
                          r=[("Hf", hb)], w=[kHbn])
                    yield
                for hb in range(2):
                    fns = []
                    for h8 in range(8):
                        hh = hb * 8 + h8
                        p, h = hh // 2, hh % 2
                        yo = ps[hb][cr, h8 * 64:(h8 + 1) * 64]
                        fns.append(lambda e, hh=hh, yo=yo: e.matmul(yo, lhsT=xf[cr, hh, 3, c * 64:(c + 1) * 64], rhs=Hb[cr, hh, :],
                                                                    start=True, stop=False))
                        fns.append(lambda e, p=p, h=h, yo=yo, hh=hh: e.matmul(yo, lhsT=AM1[b][p][cr, h, 1, c * 64:(c + 1) * 64], rhs=U[cr, hh, :],
                                                                              start=False, stop=False))
                        fns.append(lambda e, p=p, h=h, yo=yo, hh=hh: e.matmul(yo, lhsT=AM2[b][p][cr, h, 1, c * 64:(c + 1) * 64],
                                                                              rhs=tmt[cr, 0, hh * 64:(hh + 1) * 64], start=False, stop=True))
                    sy.op("pe", fns, r=[kxf, ktm, kHb, ("U", hb)] + kAM, w=[("ps", hb)], banks=[hb])
                    yield
            sy.dma("sp", xres[:, :], I["x"][tk0:tk0 + 128, :], w=["xres"])
            sy.op("pool", lambda e: e.tensor_tensor(out=bv[:, :].rearrange("p (h v) -> p h v", v=64),
                                                    in0=tmt[:, 0, :].rearrange("p (h v) -> p h v", v=64),
                                                    in1=S_all[:, gt, :].unsqueeze(2).to_broadcast([128, 16, 64]), op=ALU.mult),
                  r=[ktm, ("S_all",)], w=["bv", "l00xo"])
            yield
            sy.op("pool", lambda e: e.tensor_tensor(out=bv[:, :], in0=bv[:, :], in1=lnxb[:, :], op=ALU.add), r=["bv", "lnxb"], w=["bv", "l00xo"])
            yield
            for hb in range(2):
                sy.op("act", lambda e, hb=hb: e.activation(out=Ysb[:, hb * 512:(hb + 1) * 512], in_=ps[hb][:, :], func=AF.Copy),
                      r=[("ps", hb)], w=[("Ysb", hb)], banks=[hb])
                yield
            kY = [("Ysb", 0), ("Ysb", 1)]
            y3 = Ysb[:, :].rearrange("p (h v) -> p h v", v=64)
            sy.op("act", lambda e: e.activation(out=ysq[:, :], in_=Ysb[:, :], func=AF.Square), r=kY + ["z"], w=["ysq", "z"])
            yield
            sy.op("dve", lambda e: e.tensor_reduce(out=sm[:, 0, :], in_=y3, axis=AX.X, op=ALU.add), r=kY, w=["sm0"])
            yield
            sy.op("dve", lambda e: e.tensor_reduce(out=sm[:, 1, :], in_=ysq[:, :].rearrange("p (h v) -> p h v", v=64), axis=AX.X, op=ALU.add),
                  r=["ysq"], w=["sm1"])
            yield
            sy.op("dve", lambda e: e.tensor_scalar(out=sm[:, 2, :], in0=sm[:, 0, :], scalar1=1.0 / 64, scalar2=None, op0=ALU.mult),
                  r=["sm0"], w=["sm2"])
            sy.op("dve", lambda e: e.tensor_tensor(out=sm[:, 3, :], in0=sm[:, 2, :], in1=sm[:, 2, :], op=ALU.mult), r=["sm2"], w=["sm3"])
            sy.op("dve", lambda e: e.scalar_tensor_tensor(out=sm[:, 4, :], in0=sm[:, 1, :], scalar=1.0 / 64, in1=sm[:, 3, :],
                                                          op0=ALU.mult, op1=ALU.subtract), r=["sm1", "sm3"], w=["sm4"])
            sy.op("dve", lambda e: e.tensor_scalar(out=sm[:, 4, :], in0=sm[:, 4, :], scalar1=GN_EPS, scalar2=None, op0=ALU.add),
                  r=["sm4"], w=["sm4"])
            yield
            sy.op("act", lambda e: e.activation(out=sm[:, 3, :], in_=sm[:, 4, :], func=AF.Sqrt), r=["sm4", "sm3"], w=["sm3"])
            sy.op("dve", lambda e: e.reciprocal(out=sm[:, 5, :], in_=sm[:, 3, :]), r=["sm3"], w=["sm5"])
            yield
            sy.op("dve", lambda e: e.tensor_tensor(out=y3, in0=y3, in1=sm[:, 2, :].unsqueeze(2).to_broadcast([128, 16, 64]), op=ALU.subtract),
                  r=kY + ["sm2", "ysq"], w=kY)
            yield
            sy.op("dve", lambda e: e.tensor_tensor(out=y3, in0=y3, in1=sm[:, 5, :].unsqueeze(2).to_broadcast([128, 16, 64]), op=ALU.mult),
                  r=kY + ["sm5"], w=kY)
            yield
            sy.op("dve", lambda e: e.tensor_tensor(out=Ysb[:, :], in0=Ysb[:, :], in1=lnxg[:, :], op=ALU.mult), r=kY + ["lnxg"], w=kY)
            yield
            sy.op("dve", lambda e: e.tensor_tensor(out=Ysb[:, :], in0=Ysb[:, :], in1=bv[:, :], op=ALU.add), r=kY + ["bv", "l00xo"], w=kY)
            yield
            for hb in range(2):
                bg = bk.next()
                sy.op("pe", lambda e, hb=hb, bg=bg: e.matmul(ps[bg][:, :], lhsT=sgt[b][:, :], rhs=g2[:, hb * 512:(hb + 1) * 512],
                                                             start=True, stop=True), r=[("sgt", b), "g2"], w=[("ps", bg)], banks=[bg])
                yield
                sy.op("dve", lambda e, hb=hb, bg=bg: e.tensor_tensor(out=Ysb[:, hb * 512:(hb + 1) * 512], in0=ps[bg][:, :],
                                                                     in1=Ysb[:, hb * 512:(hb + 1) * 512], op=ALU.mult),
                      r=[("ps", bg)] + kY, w=kY, banks=[bg])
                yield
            for half in range(2):
                bb = bk.next()
                sy.op("pe", [lambda e, j=j, bb=bb, half=half: e.transpose(out=ps[bb][:, j * 128:(j + 1) * 128],
                                                                          in_=Ysb[:, (half * 4 + j) * 128:(half * 4 + j + 1) * 128],
                                                                          identity=C["identf"][:, :]) for j in range(4)],
                      r=kY + [("c", "identf")], w=[("ps", bb)], banks=[bb])
                yield
                sy.op("act", lambda e, bb=bb, half=half: e.activation(out=yT[:, half * 4:(half + 1) * 4, :],
                                                                      in_=ps[bb][:, :].rearrange("p (j t) -> p j t", t=128), func=AF.Copy),
                      r=[("ps", bb)], w=["yT"], banks=[bb])
                yield
            for hb in range(2):
                bo = bk.next()
                sy.op("pe", [lambda e, kc=kc, bo=bo, hb=hb: e.matmul(ps[bo][:, :], lhsT=yT[:, kc, :], rhs=Wo[:, kc, hb * 512:(hb + 1) * 512],
                                                                     start=(kc == 0), stop=(kc == 7)) for kc in range(8)],
                      r=["yT", "Wo"], w=[("ps", bo)], banks=[bo])
                yield
                if hasattr(g, "dbg"):
                    sy.op("act", lambda e, bo=bo, hb=hb: e.activation(out=bv[:, hb * 512:(hb + 1) * 512], in_=ps[bo][:, :], func=AF.Copy),
                          r=[("ps", bo)], w=["bv"], banks=[bo])
                sy.op("dve", lambda e, bo=bo, hb=hb: e.scalar_tensor_tensor(
                    out=z[:, hb * 512:(hb + 1) * 512], in0=xres[:, hb * 512:(hb + 1) * 512], scalar=ALPHA, in1=ps[bo][:, :],
                    op0=ALU.mult, op1=ALU.add), r=[("ps", bo), "xres", "ysq"], w=["z"], banks=[bo])
                yield
            if hasattr(g, "dbg"):
                sy.dma("sp", g.dbg["h0"][tk0:tk0 + 128, :], bv[:, :], r=["bv"], w=[("dram", "dbgh0", tk0)])
            resid_ln(g, LB, z, "z", (0, 0), g.X1, g.X1T, tk0, bk, "l00")
            yield

        loads(0)
        for _ in phaseA(0):
            pass
        for i in range(len(tiles)):
            if i + 1 < len(tiles):
                loads(i + 1)
            interleave(phaseA(i + 1) if i + 1 < len(tiles) else None, phaseBC(i), ratio=3)


def rwkv_p2a(g):
    sy, nc, C, pf, I = g.sy, g.nc, g.C, g.pf, g.I
    T, NSEQ, NTOK = g.T, g.NSEQ, g.NTOK
    ps = g.ps
    with ExitStack() as es:
        sb = lambda n, s, d: g.sb(n, s, d, es)
        XFt = [sb("XFa%d" % i, [64, 16, 4, 128], BF16) for i in range(4)]
        TMt = [sb("TMa%d" % i, [128, 2, D], BF16) for i in range(4)]
        PK = [[sb("PK%d_%d" % (b, p), [128, 2048], BF16) for p in range(8)] for b in range(2)]
        AM1 = [[PK[b][p][:, :].rearrange("q (x r) -> q x r", x=2)[:, :, 0:256].rearrange("q x (h t) -> q h x t", h=2) for p in range(8)] for b in range(2)]
        AM2 = [[PK[b][p][:, :].rearrange("q (x r) -> q x r", x=2)[:, :, 256:512].rearrange("q x (h t) -> q h x t", h=2) for p in range(8)] for b in range(2)]
        TTo = [[PK[b][p][:, 1536:1792].rearrange("q (h t) -> q h t", h=2) for p in range(8)] for b in range(2)]
        AW = [[PK[b][p][:, 1792:2048] for p in range(8)] for b in range(2)]
        DDs = [[[sb("DD%d_%d_%d" % (b, p, i), [128, 2, 2, 128], BF16) for i in range(2)] for p in range(8)] for b in range(2)]
        TTs = [[[sb("TT%d_%d_%d" % (b, p, i), [128, 2, 128], BF16) for i in range(2)] for p in range(8)] for b in range(2)]
        ntile = NTOK // 128

        def loads(i):
            tk0 = i * 128
            xf, tmt = XFt[i % 4], TMt[i % 4]
            sy.dma("sp", xf[:, :, :, :], g.XF[i], w=[("XFt", i % 4)])
            sy.dma("sp", tmt[:, 0, :], g.TMO[0, tk0:tk0 + 128, :], w=[("TMt", i % 4)])
            sy.dma("sp", tmt[:, 1, :], g.TMO[3, tk0:tk0 + 128, :], w=[("TMt", i % 4)])

        def phaseA(i, bk):
            b = i % 2
            xf, tmt = XFt[i % 4], TMt[i % 4]
            kxf, ktm = ("XFt", i % 4), ("TMt", i % 4)
            DD, TT = DDs[b], TTs[b]
            for pb in (range(0, 4), range(4, 8)):
                bl = {}
                for p in pb:
                    b1, b2, b3 = bk.next(), bk.next(), bk.next()
                    bl[p] = (b1, b2, b3)
                    fns1, fns2, fns3 = [], [], []
                    for h in range(2):
                        hh = 2 * p + h
                        fns1.append(lambda e, h=h, hh=hh, b1=b1: e.matmul(
                            ps[b1][:, h * 256:(h + 1) * 256], lhsT=xf[0:64, hh, 0, :], rhs=xf[0:64, hh, 2:4, :], start=True, stop=True))
                        fns2.append(lambda e, h=h, hh=hh, b2=b2: e.matmul(
                            ps[b2][:, h * 256:(h + 1) * 256], lhsT=xf[0:64, hh, 1, :], rhs=xf[0:64, hh, 2:4, :], start=True, stop=True))
                        fns3.append(lambda e, h=h, hh=hh, b3=b3: e.matmul(
                            ps[b3][:, h * 128:(h + 1) * 128], lhsT=xf[0:64, hh, 2, :], rhs=xf[0:64, hh, 0, :], start=True, stop=True))
                    sy.op("pe", fns1, r=[kxf], w=[("ps", b1)], banks=[b1])
                    yield
                    sy.op("dve", lambda e, p=p, b1=b1: e.tensor_tensor(out=AM1[b][p], in0=ps[b1][:, :].rearrange("q (h x t) -> q h x t", h=2, x=2),
                                                                       in1=C["mask12"][:, :].rearrange("q (h x t) -> q h x t", h=2, x=2), op=ALU.mult),
                          r=[("ps", b1), ("c", "mask12")], w=[("AM1", b, p)], banks=[b1])
                    yield
                    sy.op("pe", fns2, r=[kxf], w=[("ps", b2)], banks=[b2])
                    yield
                    sy.op("dve", lambda e, p=p, b2=b2: e.tensor_tensor(out=AM2[b][p], in0=ps[b2][:, :].rearrange("q (h x t) -> q h x t", h=2, x=2),
                                                                       in1=C["mask12"][:, :].rearrange("q (h x t) -> q h x t", h=2, x=2), op=ALU.mult),
                          r=[("ps", b2), ("c", "mask12")], w=[("AM2", b, p)], banks=[b2])
                    yield
                    sy.op("pe", fns3, r=[kxf], w=[("ps", b3)], banks=[b3])
                    yield
                    sy.op("dve", lambda e, p=p, b3=b3: e.tensor_tensor(out=DD[p][0][:, :, 1, :], in0=ps[b3][:, 0:256].rearrange("p (h t) -> p h t", t=128),
                                                                       in1=C["mask3"][:, :].rearrange("p (h t) -> p h t", t=128), op=ALU.mult),
                          r=[("ps", b3), ("c", "mask3")], w=[("DDt", b, p, 0)], banks=[b3])
                    yield
                    sy.op("act", lambda e, p=p: e.activation(out=DD[p][0][:, :, 0, :], in_=AM1[b][p][:, :, 0, :], func=AF.Copy),
                          r=[("AM1", b, p)], w=[("DDn", b, p, 0)])
                    yield
                    sy.op("pool", lambda e, p=p: e.tensor_tensor(out=TT[p][0][:, :, :], in0=AM1[b][p][:, :, 0, :],
                                                                 in1=C["identb"][:, :].unsqueeze(1).to_broadcast([128, 2, 128]), op=ALU.add),
                          r=[("AM1", b, p), ("c", "identb")], w=[("TT", b, p, 0)])
                    yield
            for k in range(5):
                ci, ni = k % 2, (k + 1) % 2
                for pb in (range(0, 4), range(4, 8)):
                    bqs = {}
                    for p in pb:
                        bq = bk.next()
                        bqs[p] = bq
                        fns = []
                        for h in range(2):
                            if k < 4:
                                fns.append(lambda e, h=h, p=p, bq=bq: e.matmul(ps[bq][:, h * 256:h * 256 + 128], lhsT=DD[p][ci][:, h, 1, :],
                                                                               rhs=DD[p][ci][:, h, 0, :], start=True, stop=True))
                            fns.append(lambda e, h=h, p=p, bq=bq: e.matmul(ps[bq][:, h * 256 + 128:h * 256 + 256], lhsT=DD[p][ci][:, h, 0, :],
                                                                           rhs=DD[p][ci][:, h, 1, :], start=True, stop=True))
                        sy.op("pe", fns, r=[("DDn", b, p, ci), ("DDt", b, p, ci)], w=[("ps", bq)], banks=[bq])
                        yield
                    for p in pb:
                        bq = bqs[p]
                        if k < 4:
                            sy.op("act", lambda e, p=p, bq=bq: e.activation(out=DD[p][ni][:].rearrange("p a b c -> p (a b c)"),
                                                                            in_=ps[bq][:, :], func=AF.Copy),
                                  r=[("ps", bq)], w=[("DDn", b, p, ni), ("DDt", b, p, ni)], banks=[bq])
                        else:
                            sy.op("act", lambda e, p=p, bq=bq: e.activation(
                                out=DD[p][ni][:, :, 1, :], in_=ps[bq][:, :].rearrange("p (h x t) -> p h x t", h=2, x=2)[:, :, 1, :], func=AF.Copy),
                                r=[("ps", bq)], w=[("DDt", b, p, ni)], banks=[bq])
                        yield
                    bts = {}
                    for p in pb:
                        bt = bk.next()
                        bts[p] = bt
                        fns = []
                        for h in range(2):
                            fns.append(lambda e, h=h, p=p, bt=bt: e.matmul(ps[bt][:, h * 128:(h + 1) * 128], lhsT=DD[p][ni][:, h, 1, :],
                                                                           rhs=TT[p][ci][:, h, :], start=True, stop=True))
                        sy.op("pe", fns, r=[("TT", b, p, ci), ("DDt", b, p, ni)], w=[("ps", bt)], banks=[bt])
                        yield
                    for p in pb:
                        bt = bts[p]
                        dst = None if k == 4 else TT[p][ni]
                        kd = ("TTo", b, p) if k == 4 else ("TT", b, p, ni)
                        sy.op("dve", lambda e, p=p, bt=bt, dst=dst: e.tensor_tensor(out=(PK[b][p][:, 1536:1792] if dst is None else dst[:].rearrange("p h t -> p (h t)")), in0=ps[bt][:, 0:256],
                                                                                   in1=TT[p][ci][:].rearrange("p h t -> p (h t)"), op=ALU.add),
                              r=[("ps", bt), ("TT", b, p, ci)], w=[kd], banks=[bt])
                        yield
            for p in range(8):
                ba = bk.next()
                fns = []
                for h in range(2):
                    hh = 2 * p + h
                    fns.append(lambda e, h=h, p=p, ba=ba, hh=hh: e.matmul(ps[ba][:, h * 64:(h + 1) * 64], lhsT=AM2[b][p][:, h, 0, :],
                                                                          rhs=tmt[:, 0, hh * 64:(hh + 1) * 64], start=True, stop=True))
                    for c_ in range(2):
                        fns.append(lambda e, h=h, p=p, ba=ba, hh=hh, c_=c_: e.matmul(
                            ps[ba][c_ * 64:(c_ + 1) * 64, 128 + h * 64:128 + (h + 1) * 64], lhsT=tmt[:, 1, hh * 64:(hh + 1) * 64],
                            rhs=TTo[b][p][:, h, c_ * 64:(c_ + 1) * 64], start=True, stop=True))
                sy.op("pe", fns, r=[("AM2", b, p), ktm, ("TTo", b, p)], w=[("ps", ba)], banks=[ba])
                yield
                sy.op("act", lambda e, p=p, ba=ba: e.activation(out=AW[b][p], in_=ps[ba][:, 0:256], func=AF.Copy),
                      r=[("ps", ba)], w=[("AW", b, p)], banks=[ba])
                yield
                sy.dma("sp", g.AO[i, p][:, :], PK[b][p][:, 1024:2048], r=[("AM1", b, p), ("AM2", b, p), ("TTo", b, p), ("AW", b, p)],
                       w=[("dram", "ao", i, p)])
                yield


        loads(0)
        loads(1)
        for i in range(0, ntile, 2):
            if i + 2 < ntile:
                loads(i + 2)
                loads(i + 3)
            ga, gb = phaseA(i, Banks([0, 1, 2, 3])), phaseA(i + 1, Banks([4, 5, 6, 7]))
            interleave(ga, gb, ratio=1)


def rwkv_p2b(g, GC, S_all):
    sy, nc, C, pf, I = g.sy, g.nc, g.C, g.pf, g.I
    T, NSEQ, NTOK = g.T, g.NSEQ, g.NTOK
    ps = g.ps
    with ExitStack() as es:
        sb = lambda n, s, d: g.sb(n, s, d, es)
        Wo = load_w(g, es, "Wo", I["a_w_o"], 8, D)
        g2 = sb("g2", [128, D], BF16); sy.dma("pool", g2[:], I["a_g2"][:, :], w=["g2"])
        lnxg = bcast_rows(g, es, "lnxg", PTI("lnx_g"))
        lnxb = bcast_rows(g, es, "lnxb", PTI("lnx_b"))
        LB = ln_bufs(g, es, (0, 0), "l00", nbuf=2)
        GC2 = sb("GC2", [128, 16, NTOK // 64], F32)
        for d_ in range(2):
            for h_ in range(2):
                sy.dma("sp", GC2[d_ * 64:(d_ + 1) * 64].rearrange("k (o h) c -> k o h c", h=2)[:, :, h_, :],
                       GC[h_ * 64:(h_ + 1) * 64, :, :], w=["GC2"])
        XR = [sb("XR%d" % i, [128, 16, 128], BF16) for i in range(2)]
        TMt = [sb("TMb%d" % i, [128, 3, D], BF16) for i in range(3)]
        AOt = [sb("AOt%d" % i, [128, 8, 1024], BF16) for i in range(2)]
        sgt = [sb("sgt%d" % i, [128, 128], BF16) for i in range(3)]
        U = sb("U", [128, 16, 64], BF16)
        Hf = sb("Hf", [128, 16, 64], F32)
        Hbs = [sb("Hb%d" % i, [128, 16, 64], BF16) for i in range(2)]
        xres = sb("xres", [128, D], F32)
        Ysbs = [sb("Ysb%d" % i, [128, D], F32) for i in range(2)]
        bvs = [sb("bv%d" % i, [128, D], F32) for i in range(2)]
        yT = sb("yT", [128, 8, 128], BF16)
        z = sb("z", [128, D], F32)
        ysq = sb("ysq", [128, D], F32)
        sm = sb("gnsm", [128, 6, 16], F32)
        bkBm = Banks([2, 3, 4, 5])
        bkC1 = Banks([6])
        bkC2 = Banks([7])
        tiles = [(s_, tl_) for s_ in range(NSEQ) for tl_ in range(T // 128)]

        def loads(i):
            s_, tl = tiles[i]
            tk0 = s_ * T + tl * 128
            b = i % 2
            for d_ in range(2):
                sy.dma("sp", XR[b][d_ * 64:(d_ + 1) * 64, :, :], g.XF[tk0 // 128][:, :, 3, :], w=[("XR", b)])
            b3 = i % 3
            for j in range(3):
                sy.dma("sp", TMt[b3][:, j, :], g.TMO[j, tk0:tk0 + 128, :], w=[("TMt", b3)])
            for q_ in range(2):
                sy.dma("sp", AOt[b][:, q_ * 4:(q_ + 1) * 4, :], g.AO[tk0 // 128, q_ * 4:(q_ + 1) * 4].rearrange("p q c -> q p c"), w=[("AOt", b)])
            sy.dma("sp", sgt[b3][:, :], g.SG[:, tk0:tk0 + 128], w=[("sgt", b3)])

        def genB(i):
            s_, tl = tiles[i]
            tk0 = s_ * T + tl * 128
            b = i % 2
            xr, tmt, ao = XR[b], TMt[i % 3], AOt[b]
            kxf, ktm = ("XR", b), ("TMt", i % 3)
            bk = bkBm
            if tl == 0:
                sy.op("pool", lambda e: e.memset(Hf[:, :, :], 0.0), w=[("Hf", 0), ("Hf", 1)])
                sy.op("pool", lambda e: e.memset(Hbs[0][:, :, :], 0.0), w=[("Hb", 0)])
            kA = [("AOt", b)]
            kAM = [("AOt", b)]
            for c in range(2):
                cr = slice(c * 64, (c + 1) * 64)
                ch = tk0 // 64 + c
                Hb, Hbn = Hbs[c], Hbs[1 - c]
                kHb, kHbn = ("Hb", c), ("Hb", 1 - c)
                ub = [bk.next(), bk.next()]
                for hb in range(2):
                    fns = []
                    for h8 in range(8):
                        hh = hb * 8 + h8
                        p, h = hh // 2, hh % 2
                        fns.append(lambda e, p=p, h=h, hh=hh, h8=h8, hb=hb: e.matmul(
                            ps[ub[hb]][cr, h8 * 64:(h8 + 1) * 64], lhsT=ao[cr, p, 896 + h * 64:896 + (h + 1) * 64], rhs=Hb[cr, hh, :],
                            start=True, stop=False))
                        fns.append(lambda e, p=p, h=h, h8=h8, hb=hb: e.matmul(
                            ps[ub[hb]][cr, h8 * 64:(h8 + 1) * 64], lhsT=ao[cr, p, 512 + h * 128 + c * 64:512 + h * 128 + (c + 1) * 64], rhs=ao[cr, p, 768 + h * 64:768 + (h + 1) * 64],
                            start=False, stop=True))
                    sy.op("pe", fns, r=kA + [kHb], w=[("ps", ub[hb])], banks=[ub[hb]])
                    yield
                    sy.op("act", lambda e, hb=hb: e.activation(out=U[cr, hb * 8:(hb + 1) * 8, :].rearrange("p a b -> p (a b)"),
                                                               in_=ps[ub[hb]][cr, :], func=AF.Copy),
                          r=[("ps", ub[hb])], w=[("U", hb)], banks=[ub[hb]])
                    yield
                bhs = [bk.next(), bk.next()]
                sy.op("pool", lambda e, ch=ch: e.tensor_tensor(out=Hf[:, :, :], in0=Hf[:, :, :],
                                                               in1=GC2[:, :, ch:ch + 1].to_broadcast([128, 16, 64]), op=ALU.mult),
                      r=[("Hf", 0), ("Hf", 1), "GC2"], w=[("Hf", 0), ("Hf", 1)])
                yield
                for hb in range(2):
                    bh = bhs[hb]
                    fns = []
                    for h8 in range(8):
                        hh = hb * 8 + h8
                        p, h = hh // 2, hh % 2
                        for d_ in range(2):
                            ho = ps[bh][d_ * 64:(d_ + 1) * 64, h8 * 64:(h8 + 1) * 64]
                            fns.append(lambda e, ho=ho, hh=hh: e.matmul(ho, lhsT=tmt[cr, 2, hh * 64:(hh + 1) * 64], rhs=U[cr, hh, :],
                                                                        start=True, stop=False))
                            fns.append(lambda e, ho=ho, hh=hh: e.matmul(ho, lhsT=tmt[cr, 1, hh * 64:(hh + 1) * 64],
                                                                        rhs=tmt[cr, 0, hh * 64:(hh + 1) * 64], start=False, stop=True))
                    sy.op("pe", fns, r=[ktm, ("U", hb)], w=[("ps", bh)], banks=[bh])
                    yield
                    sy.op("dve", lambda e, hb=hb, bh=bh: e.tensor_tensor(out=Hf[:, hb * 8:(hb + 1) * 8, :].rearrange("p a b -> p (a b)"), in0=ps[bh][:, :],
                                                                         in1=Hf[:, hb * 8:(hb + 1) * 8, :].rearrange("p a b -> p (a b)"), op=ALU.add),
                          r=[("ps", bh), ("Hf", hb)], w=[("Hf", hb)], banks=[bh])
                    yield
                    sy.op("act", lambda e, hb=hb, Hbn=Hbn: e.activation(out=Hbn[:, hb * 8:(hb + 1) * 8, :].rearrange("p a b -> p (a b)"),
                                                                        in_=Hf[:, hb * 8:(hb + 1) * 8, :].rearrange("p a b -> p (a b)"), func=AF.Copy),
                          r=[("Hf", hb)], w=[kHbn])
                    yield
                for hb in range(2):
                    fns = []
                    for h8 in range(8):
                        hh = hb * 8 + h8
                        p, h = hh // 2, hh % 2
                        yo = ps[hb][cr, h8 * 64:(h8 + 1) * 64]
                        fns.append(lambda e, hh=hh, yo=yo: e.matmul(yo, lhsT=xr[cr, hh, c * 64:(c + 1) * 64], rhs=Hb[cr, hh, :],
                                                                    start=True, stop=False))
                        fns.append(lambda e, p=p, h=h, yo=yo, hh=hh: e.matmul(yo, lhsT=ao[cr, p, h * 128 + c * 64:h * 128 + (c + 1) * 64], rhs=U[cr, hh, :],
                                                                              start=False, stop=False))
                        fns.append(lambda e, p=p, h=h, yo=yo, hh=hh: e.matmul(yo, lhsT=ao[cr, p, 256 + h * 128 + c * 64:256 + h * 128 + (c + 1) * 64],
                                                                              rhs=tmt[cr, 0, hh * 64:(hh + 1) * 64], start=False, stop=True))
                    sy.op("pe", fns, r=[kxf, ktm, kHb, ("U", hb)] + kAM, w=[("ps", hb)], banks=[hb])
                    yield

        def genC1(i):
            s_, tl = tiles[i]
            tk0 = s_ * T + tl * 128
            b = i % 2
            gt = tk0 // 128
            tmt = TMt[i % 3]
            ktm = ("TMt", i % 3)
            Ysb_, bv = Ysbs[b], bvs[b]
            bk = bkC1
            sy.op("pool", lambda e: e.tensor_tensor(out=bv[:, :].rearrange("p (h v) -> p h v", v=64),
                                                    in0=tmt[:, 0, :].rearrange("p (h v) -> p h v", v=64),
                                                    in1=S_all[:, gt, :].unsqueeze(2).to_broadcast([128, 16, 64]), op=ALU.mult),
                  r=[ktm, ("S_all",)], w=[("bv", b)])
            yield
            sy.op("pool", lambda e: e.tensor_tensor(out=bv[:, :], in0=bv[:, :], in1=lnxb[:, :], op=ALU.add), r=[("bv", b), "lnxb"], w=[("bv", b)])
            yield
            for hb in range(2):
                sy.op("act", lambda e, hb=hb: e.activation(out=Ysb_[:, hb * 512:(hb + 1) * 512], in_=ps[hb][:, :], func=AF.Copy),
                      r=[("ps", hb)], w=[("Ysb", b, hb)], banks=[hb])
                yield
            kY = [("Ysb", b, 0), ("Ysb", b, 1)]
            y3 = Ysb_[:, :].rearrange("p (h v) -> p h v", v=64)
            sy.op("act", lambda e: e.activation(out=ysq[:, :], in_=Ysb_[:, :], func=AF.Square), r=kY, w=["ysq"])
            yield
            sy.op("dve", lambda e: e.tensor_reduce(out=sm[:, 0, :], in_=y3, axis=AX.X, op=ALU.add), r=kY, w=["sm0"])
            yield
            sy.op("dve", lambda e: e.tensor_reduce(out=sm[:, 1, :], in_=ysq[:, :].rearrange("p (h v) -> p h v", v=64), axis=AX.X, op=ALU.add),
                  r=["ysq"], w=["sm1"])
            yield
            sy.op("dve", lambda e: e.tensor_scalar(out=sm[:, 2, :], in0=sm[:, 0, :], scalar1=1.0 / 64, scalar2=None, op0=ALU.mult),
                  r=["sm0"], w=["sm2"])
            sy.op("dve", lambda e: e.tensor_tensor(out=sm[:, 3, :], in0=sm[:, 2, :], in1=sm[:, 2, :], op=ALU.mult), r=["sm2"], w=["sm3"])
            sy.op("dve", lambda e: e.scalar_tensor_tensor(out=sm[:, 4, :], in0=sm[:, 1, :], scalar=1.0 / 64, in1=sm[:, 3, :],
                                                          op0=ALU.mult, op1=ALU.subtract), r=["sm1", "sm3"], w=["sm4"])
            sy.op("dve", lambda e: e.tensor_scalar(out=sm[:, 4, :], in0=sm[:, 4, :], scalar1=GN_EPS, scalar2=None, op0=ALU.add),
                  r=["sm4"], w=["sm4"])
            yield
            sy.op("act", lambda e: e.activation(out=sm[:, 3, :], in_=sm[:, 4, :], func=AF.Sqrt), r=["sm4", "sm3"], w=["sm3"])
            sy.op("dve", lambda e: e.reciprocal(out=sm[:, 5, :], in_=sm[:, 3, :]), r=["sm3"], w=["sm5"])
            yield
            sy.op("dve", lambda e: e.tensor_tensor(out=y3, in0=y3, in1=sm[:, 2, :].unsqueeze(2).to_broadcast([128, 16, 64]), op=ALU.subtract),
                  r=kY + ["sm2", "ysq"], w=kY)
            yield
            sy.op("dve", lambda e: e.tensor_tensor(out=y3, in0=y3, in1=sm[:, 5, :].unsqueeze(2).to_broadcast([128, 16, 64]), op=ALU.mult),
                  r=kY + ["sm5"], w=kY)
            yield
            sy.op("dve", lambda e: e.tensor_tensor(out=Ysb_[:, :], in0=Ysb_[:, :], in1=lnxg[:, :], op=ALU.mult), r=kY + ["lnxg"], w=kY)
            yield
            sy.op("dve", lambda e: e.tensor_tensor(out=Ysb_[:, :], in0=Ysb_[:, :], in1=bv[:, :], op=ALU.add), r=kY + [("bv", b)], w=kY)
            yield
            for hb in range(2):
                bg = bk.next()
                sy.op("pe", lambda e, hb=hb, bg=bg: e.matmul(ps[bg][:, :], lhsT=sgt[i % 3][:, :], rhs=g2[:, hb * 512:(hb + 1) * 512],
                                                             start=True, stop=True), r=[("sgt", i % 3), "g2"], w=[("ps", bg)], banks=[bg])
                yield
                sy.op("dve", lambda e, hb=hb, bg=bg: e.tensor_tensor(out=Ysb_[:, hb * 512:(hb + 1) * 512], in0=ps[bg][:, :],
                                                                     in1=Ysb_[:, hb * 512:(hb + 1) * 512], op=ALU.mult),
                      r=[("ps", bg)] + kY, w=kY, banks=[bg])
                yield

        def genC2(i):
            s_, tl = tiles[i]
            tk0 = s_ * T + tl * 128
            b = i % 2
            gt = tk0 // 128
            tmt = TMt[i % 3]
            ktm = ("TMt", i % 3)
            Ysb_, bv = Ysbs[b], bvs[b]
            bk = bkC2
            kY = [("Ysb", b, 0), ("Ysb", b, 1)]
            sy.dma("sp", xres[:, :], I["x"][tk0:tk0 + 128, :], w=["xres"])
            for half in range(2):
                bb = bk.next()
                sy.op("pe", [lambda e, j=j, bb=bb, half=half: e.transpose(out=ps[bb][:, j * 128:(j + 1) * 128],
                                                                          in_=Ysb_[:, (half * 4 + j) * 128:(half * 4 + j + 1) * 128],
                                                                          identity=C["identf"][:, :]) for j in range(4)],
                      r=kY + [("c", "identf")], w=[("ps", bb)], banks=[bb])
                yield
                sy.op("act", lambda e, bb=bb, half=half: e.activation(out=yT[:, half * 4:(half + 1) * 4, :],
                                                                      in_=ps[bb][:, :].rearrange("p (j t) -> p j t", t=128), func=AF.Copy),
                      r=[("ps", bb)], w=["yT"], banks=[bb])
                yield
            for hb in range(2):
                bo = bk.next()
                sy.op("pe", [lambda e, kc=kc, bo=bo, hb=hb: e.matmul(ps[bo][:, :], lhsT=yT[:, kc, :], rhs=Wo[:, kc, hb * 512:(hb + 1) * 512],
                                                                     start=(kc == 0), stop=(kc == 7)) for kc in range(8)],
                      r=["yT", "Wo"], w=[("ps", bo)], banks=[bo])
                yield
                if hasattr(g, "dbg"):
                    sy.op("act", lambda e, bo=bo, hb=hb: e.activation(out=bv[:, hb * 512:(hb + 1) * 512], in_=ps[bo][:, :], func=AF.Copy),
                          r=[("ps", bo)], w=["bv"], banks=[bo])
                sy.op("dve", lambda e, bo=bo, hb=hb: e.scalar_tensor_tensor(
                    out=z[:, hb * 512:(hb + 1) * 512], in0=xres[:, hb * 512:(hb + 1) * 512], scalar=ALPHA, in1=ps[bo][:, :],
                    op0=ALU.mult, op1=ALU.add), r=[("ps", bo), "xres"], w=["z"], banks=[bo])
                yield
            if hasattr(g, "dbg"):
                sy.dma("sp", g.dbg["h0"][tk0:tk0 + 128, :], bv[:, :], r=["bv"], w=[("dram", "dbgh0", tk0)])
            resid_ln(g, LB, z, "z", (0, 0), g.X1, g.X1T, tk0, bk, "l00")
            yield


        def rr(gens, hook=None, hook_at=6):
            gens = [x for x in gens if x is not None]
            k = 0
            while gens:
                if hook is not None and k == hook_at:
                    hook()
                    hook = None
                k += 1
                for x in list(gens):
                    try:
                        next(x)
                    except StopIteration:
                        gens.remove(x)
            if hook is not None:
                hook()

        n = len(tiles)
        loads(0)
        loads(1)
        rr([genB(0)])
        for i in range(n + 1):
            hk = (lambda i=i: loads(i + 2)) if i + 2 < n else None
            rr([genB(i + 1) if i + 1 < n else None, genC1(i) if i < n else None, genC2(i - 1) if i >= 1 else None], hook=hk)


def mlp_layer(g, l, Xin, XinT, Xout, XoutT, final):
    sy, nc, C, pf, I = g.sy, g.nc, g.C, g.pf, g.I
    T, NSEQ = g.T, g.NSEQ
    ps = g.ps
    NT = T // 128
    NTB = T // 512
    with ExitStack() as es:
        sb = lambda n, s, d: g.sb(n, s, d, es)
        acc = sb("acc", [128, NT, D], F32)
        xT = sb("mxT", [128, 8, T], BF16)
        W1g = [sb("W1g%d" % i, [128, 8, 512], BF16) for i in range(2)]
        W2g = [sb("W2g%d" % i, [128, 4, D], BF16) for i in range(2)]
        hT = [sb("hT%d" % i, [128, 4, 512], BF16) for i in range(2)]
        rl = [sb("rl%d" % i, [128, 512], F32) for i in range(2)]
        tag = "l%d1" % l
        LB = ln_bufs(g, es, (l, 1), tag, nbuf=2)
        bkH = Banks([0, 1, 2, 3])
        bkS = Banks([4, 5, 6, 7])
        w1src = I["mlp_w1"][l].rearrange("(kc p) n -> p kc n", p=128)
        w2src = I["mlp_w2"][l].rearrange("(fc p) n -> p fc n", p=128)
        NW = NSEQ * 8
        rc = [0]

        def wload(wi):
            fg = wi % 8
            sy.dma("pool", W1g[wi % 2][:, :, :], w1src[:, :, fg * 512:(fg + 1) * 512], w=[("W1g", wi % 2)])
            sy.dma("pool", W2g[wi % 2][:, :, :], w2src[:, fg * 4:(fg + 1) * 4, :], w=[("W2g", wi % 2)])

        def aload(s_):
            for tb_ in range(NTB):
                sy.dma("sp", xT[:, :, tb_ * 512:(tb_ + 1) * 512],
                       XinT[:, :, s_ * T + tb_ * 512:s_ * T + (tb_ + 1) * 512].rearrange("c p t -> p c t"), w=[("mxT", tb_)])
            if s_ == 0:
                for tl in range(NT):
                    sy.dma("sp", acc[:, tl, :], Xin[tl * 128:(tl + 1) * 128, :], w=[("acc", tl)])

        def genH(s_, fg, tb, n):
            wi = s_ * 8 + fg
            w1, k1 = W1g[wi % 2], ("W1g", wi % 2)
            hTi, kh = hT[n % 2], ("hT", n % 2)
            for fc in range(4):
                bb = bkH.next()
                sy.op("pe", [lambda e, kc=kc, bb=bb, fc=fc: e.matmul(
                    ps[bb][:, :], lhsT=w1[:, kc, fc * 128:(fc + 1) * 128], rhs=xT[:, kc, tb * 512:(tb + 1) * 512],
                    start=(kc == 0), stop=(kc == 7)) for kc in range(8)], r=[("mxT", tb), k1], w=[("ps", bb)], banks=[bb])
                rli = rl[rc[0] % 2]
                kr = ("rl", rc[0] % 2)
                rc[0] += 1
                sy.op("act", lambda e, bb=bb, rli=rli: e.activation(out=rli[:, :], in_=ps[bb][:, :], func=AF.Relu),
                      r=[("ps", bb)], w=[kr], banks=[bb])
                sy.op("act", lambda e, rli=rli, fc=fc: e.activation(out=hTi[:, fc, :], in_=rli[:, :], func=AF.Square), r=[kr], w=[kh])
                yield

        def genS(s_, fg, tb, n):
            wi = s_ * 8 + fg
            w2, k2 = W2g[wi % 2], ("W2g", wi % 2)
            hTi, kh = hT[n % 2], ("hT", n % 2)
            for tt in range(4):
                tl = tb * 4 + tt
                for half in range(2):
                    bb = bkS.next()
                    sy.op("pe", [lambda e, fc=fc, bb=bb, tt=tt, half=half: e.matmul(
                        ps[bb][:, :], lhsT=hTi[:, fc, tt * 128:(tt + 1) * 128], rhs=w2[:, fc, half * 512:(half + 1) * 512],
                        start=(fc == 0), stop=(fc == 3)) for fc in range(4)], r=[kh, k2], w=[("ps", bb)], banks=[bb])
                    asl = acc[:, tl, half * 512:(half + 1) * 512]
                    if fg == 0:
                        sy.op("dve", lambda e, bb=bb, asl=asl: e.scalar_tensor_tensor(out=asl, in0=asl, scalar=ALPHA, in1=ps[bb][:, :],
                                                                                      op0=ALU.mult, op1=ALU.add),
                              r=[("ps", bb), ("acc", tl)], w=[("acc", tl)], banks=[bb])
                    else:
                        sy.op("dve", lambda e, bb=bb, asl=asl: e.tensor_tensor(out=asl, in0=ps[bb][:, :], in1=asl, op=ALU.add),
                              r=[("ps", bb), ("acc", tl)], w=[("acc", tl)], banks=[bb])
                    yield

        def genSeq(s_):
            aload(s_)
            blocks = [(fg, tb) for fg in range(8) for tb in range(NTB)]
            n0 = s_ * len(blocks)
            for _ in genH(s_, blocks[0][0], blocks[0][1], n0):
                yield
            for bi, (fg, tb) in enumerate(blocks):
                wi = s_ * 8 + fg
                if tb == 0 and wi + 1 < NW:
                    wload(wi + 1)
                gh = genH(s_, blocks[bi + 1][0], blocks[bi + 1][1], n0 + bi + 1) if bi + 1 < len(blocks) else None
                gs = genS(s_, fg, tb, n0 + bi)
                dh = gh is None
                ds = False
                while not (dh and ds):
                    if not dh:
                        try:
                            next(gh)
                        except StopIteration:
                            dh = True
                    if not ds:
                        try:
                            next(gs)
                        except StopIteration:
                            ds = True
                    yield

        def genLN(s_):
            for tl in range(NT):
                tk0 = s_ * T + tl * 128
                resid_ln(g, LB, acc[:, tl, :], ("acc", tl), (l, 1), Xout, XoutT, tk0, bkS, tag)
                if s_ + 1 < NSEQ:
                    tk1 = (s_ + 1) * T + tl * 128
                    sy.dma("sp", acc[:, tl, :], Xin[tk1:tk1 + 128, :], w=[("acc", tl)])
                yield

        wload(0)
        prev = None
        for s_ in range(NSEQ):
            interleave(prev, genSeq(s_), ratio=6)
            prev = genLN(s_)
        for _ in prev:
            pass


LAMBDA_INIT = 0.8 - 0.6 * math.exp(-0.3 * 1)


def attn_layer(g):
    sy, nc, C, pf, I = g.sy, g.nc, g.C, g.pf, g.I
    T, NSEQ, NTOK = g.T, g.NSEQ, g.NTOK
    ps = g.ps
    NT = T // 128
    NG = T // 512
    with ExitStack() as esl:
        KT = g.sb("KT", [128, 8, T], BF16, esl)
        QT = g.sb("QT", [128, 8, T], BF16, esl)
        Vp = g.sb("Vp", [128, NT, 8, 130], BF16, esl)
        EB = g.sb("EB", [128, 8, 2, 128], F32, esl)
        lamc = g.sb("lamc", [128, 4], F32, esl)
        sy.op("pool", lambda e: e.memset(Vp[:, :, :, 128:130], 1.0), w=["Vp1"])
        with ExitStack() as es:
            sb = lambda n, s, d: g.sb(n, s, d, es)
            oh = sb("oh", [32, 384], F32); sy.dma("sp", oh[:], I["onehot"][:, :], w=["oh"])
            rb = sb("rb", [32, 8], F32); sy.dma("sp", rb[:], I["rel_bias"][:, :], w=["rb"])
            bvs = sb("bvs", [8, 384], F32)
            sy.op("pe", lambda e: e.matmul(ps[0][0:8, 0:384], lhsT=rb[:, :], rhs=oh[:, :], start=True, stop=True), r=["oh", "rb"], w=[("ps", 0)], banks=[0])
            sy.op("act", lambda e: e.activation(out=bvs[:, :], in_=ps[0][0:8, 0:384], func=AF.Copy), r=[("ps", 0)], w=["bvs"], banks=[0])
            BVd = nc.dram_tensor("s_bv", [8, 384], F32, kind="Internal").ap()
            sy.dma("sp", BVd[:, :], bvs[:, :], r=["bvs"], w=["BVd"])
            hk = sb("hk", [128, 8, 2, 128], F32)
            for h in range(8):
                for j in range(2):
                    src = bass.AP(tensor=BVd.tensor, offset=h * 384 + j * 128, ap=[[1, 128], [1, 128]])
                    sy.dma("sp", hk[:, h, j, :], src, r=["BVd"], w=["hk"])
            for hh in range(4):
                sy.op("pe", lambda e, hh=hh: e.matmul(ps[1 + hh % 2][:, :], lhsT=C["antiI"][:, :], rhs=hk[:, 2 * hh:2 * hh + 2, :, :], start=True, stop=True),
                      r=["hk", ("c", "antiI")], w=[("ps", 1 + hh % 2)], banks=[1 + hh % 2])
                sy.op("act", lambda e, hh=hh: e.activation(out=EB[:, 2 * hh:2 * hh + 2, :, :].rearrange("p a b c -> p (a b c)"), in_=ps[1 + hh % 2][:, :], func=AF.Exp),
                      r=[("ps", 1 + hh % 2)], w=["EB"], banks=[1 + hh % 2])
            sy.op("dve", lambda e: e.tensor_tensor(out=EB[:, :, 0, :], in0=EB[:, :, 0, :], in1=C["maskd"][:, :].unsqueeze(1).to_broadcast([128, 8, 128]), op=ALU.mult),
                  r=["EB", ("c", "maskd")], w=["EB"])
            lam = sb("lam", [128, 256], F32)
            sy.dma("sp", lam[:, :], I["b_lam"][0:1, :].partition_broadcast(128), w=["lam"])
            lp = sb("lp", [128, 2, 64], F32)
            ls = sb("ls", [128, 4], F32)
            l4 = lam[:, :].rearrange("p (a b c) -> p a b c", a=2, b=2)
            sy.op("dve", lambda e: e.tensor_tensor(out=lp[:, :, :], in0=l4[:, :, 0, :], in1=l4[:, :, 1, :], op=ALU.mult), r=["lam"], w=["lp"])
            sy.op("dve", lambda e: e.tensor_reduce(out=ls[:, 0:2], in_=lp[:, :, :], axis=AX.X, op=ALU.add), r=["lp"], w=["ls"])
            sy.op("act", lambda e: e.activation(out=ls[:, 2:4], in_=ls[:, 0:2], func=AF.Exp), r=["ls"], w=["ls2"])
            sy.op("dve", lambda e: e.scalar_tensor_tensor(out=lamc[:, 0:1], in0=ls[:, 3:4], scalar=-LAMBDA_INIT, in1=ls[:, 2:3], op0=ALU.add, op1=ALU.subtract),
                  r=["ls2"], w=["lamc"])
        sy.barrier()
        for s in range(NSEQ):
            with ExitStack() as es:
                sb = lambda n, s_, d: g.sb(n, s_, d, es)
                xT = sb("axT", [128, 8, T], BF16)
                for kc in range(8):
                    sy.dma("sp", xT[:, kc, :], g.X2T[kc, :, s * T:(s + 1) * T], w=[("axT", kc)])
                kxT = [("axT", kc) for kc in range(8)]
                Wa = sb("Wa", [128, 8, D], BF16)
                Wb = sb("Wb", [128, 8, D], BF16)
                Wc = sb("Wc", [128, 8, D], BF16)
                bk = Banks(range(8))
                sy.dma("pool", Wa[:], I["b_w_kv"][:, 0:D].rearrange("(kc p) n -> p kc n", p=128), w=["Wa"])
                sy.dma("pool", Wb[:], I["b_w_q"].rearrange("(kc p) n -> p kc n", p=128), w=["Wb"])
                sy.dma("pool", Wc[:], I["b_w_kv"][:, D:2 * D].rearrange("(kc p) n -> p kc n", p=128), w=["Wc"])
                for (W, kW, dst, kd) in ((Wa, "Wa", KT, "KT"), (Wb, "Wb", QT, "QT")):
                    for oc in range(8):
                        for tb in range(NG):
                            bb = bk.next()
                            sy.op("pe", [lambda e, kc=kc, bb=bb, oc=oc, tb=tb, W=W: e.matmul(
                                ps[bb][:, :], lhsT=W[:, kc, oc * 128:(oc + 1) * 128], rhs=xT[:, kc, tb * 512:(tb + 1) * 512],
                                start=(kc == 0), stop=(kc == 7)) for kc in range(8)], r=kxT + [kW], w=[("ps", bb)], banks=[bb])
                            sy.op("act", lambda e, bb=bb, oc=oc, tb=tb, dst=dst: e.activation(out=dst[:, oc, tb * 512:(tb + 1) * 512], in_=ps[bb][:, :], func=AF.Copy),
                                  r=[("ps", bb)], w=[(kd, oc)], banks=[bb])
                for tl in range(NT):
                    for half in range(2):
                        bb = bk.next()
                        sy.op("pe", [lambda e, kc=kc, bb=bb, tl=tl, half=half: e.matmul(
                            ps[bb][:, :], lhsT=xT[:, kc, tl * 128:(tl + 1) * 128], rhs=Wc[:, kc, half * 512:(half + 1) * 512],
                            start=(kc == 0), stop=(kc == 7)) for kc in range(8)], r=kxT + ["Wc"], w=[("ps", bb)], banks=[bb])
                        sy.op("act", lambda e, bb=bb, tl=tl, half=half: e.activation(
                            out=Vp[:, tl, half * 4:(half + 1) * 4, 0:128], in_=ps[bb][:, :].rearrange("p (h e) -> p h e", e=128), func=AF.Copy),
                            r=[("ps", bb)], w=[("Vp", tl)], banks=[bb])
            sy.barrier()
            with ExitStack() as es:
                sb = lambda n, s_, d: g.sb(n, s_, d, es)
                Wo = load_w(g, es, "Wob", I["b_w_o"], 8, D)
                LB = ln_bufs(g, es, (1, 0), "l10")
                subg = bcast_rows(g, es, "subg", PTI("subln"))
                Oalls = [sb("Oall%d" % i, [128, 4, D], BF16) for i in range(2)]
                PT = [[sb("PT%d_%d" % (m, i), [128, 512], BF16) for i in range(3)] for m in range(2)]
                bkS = Banks([0, 1, 5, 6])
                etmp = [sb("etmp%d" % i, [128, 128], F32) for i in range(4)]
                cmb = sb("cmb", [128, 24], F32)
                Osb = [sb("Osb%d" % i, [128, 9, 130], F32) for i in range(2)]
                hc = 0
                tq = sb("tq", [128, 4, 128], F32)
                ob = sb("ob", [128, 4, 128], F32)
                osq = sb("osq", [128, 4, 128], F32)
                oT = sb("oT", [128, 8, 128], BF16)
                xres = [sb("axres%d" % i, [128, D], F32) for i in range(2)]
                z = sb("az", [128, D], F32)
                bk2 = Banks([7])
                SBK = [(0, 1), (5, 6)]
                it = 0
                ec = 0
                def genAttn(G):
                    nonlocal it, ec, hc
                    Oall = Oalls[G % 2]
                    if True:
                        def acc(m, j):
                            a = m * 4 + j
                            return ps[2 + a // 3][:, (a % 3) * 130:(a % 3) * 130 + 129], 2 + a // 3
                        started = set()

                        def stage1(h, kt):
                            nonlocal it, ec
                            q0 = max(kt, 4 * G)
                            c0 = (q0 - 4 * G) * 128
                            sbk = (bkS.next(), bkS.next())
                            pts = [PT[0][it % 3], PT[1][it % 3]]
                            kpt = [("PT", 0, it % 3), ("PT", 1, it % 3)]
                            it += 1
                            for m in range(2):
                                mr = slice(m * 64, (m + 1) * 64)
                                sy.op("pe", lambda e, m=m, mr=mr, sbk=sbk, c0=c0, kt=kt, q0=q0: e.matmul(
                                    ps[sbk[m]][:, c0:512], lhsT=KT[mr, h, kt * 128:(kt + 1) * 128], rhs=QT[mr, h, q0 * 128:(4 * G + 4) * 128],
                                    start=True, stop=True), r=[("KT", h), ("QT", h)], w=[("ps", sbk[m])], banks=[sbk[m]])
                            for m in range(2):
                                cc = c0
                                for near in range(2):
                                    qb = kt + near
                                    if qb < 4 * G or qb > 4 * G + 3:
                                        continue
                                    cb = (qb - 4 * G) * 128
                                    et = etmp[ec % 4]
                                    ke = ("etmp", ec % 4)
                                    ec += 1
                                    sy.op("act", lambda e, m=m, cb=cb, et=et, sbk=sbk: e.activation(out=et[:, :], in_=ps[sbk[m]][:, cb:cb + 128], func=AF.Exp, scale=0.125),
                                          r=[("ps", sbk[m])], w=[ke], banks=[sbk[m]])
                                    sy.op("dve", lambda e, m=m, cb=cb, et=et, near=near, pts=pts: e.tensor_tensor(
                                        out=pts[m][:, cb:cb + 128], in0=et[:, :], in1=EB[:, h, near, :], op=ALU.mult), r=[ke, "EB"], w=[kpt[m]])
                                    cc = cb + 128
                                if cc < 512:
                                    sy.op("act", lambda e, m=m, cc=cc, sbk=sbk, pts=pts: e.activation(out=pts[m][:, cc:512], in_=ps[sbk[m]][:, cc:512], func=AF.Exp, scale=0.125),
                                          r=[("ps", sbk[m])], w=[kpt[m]], banks=[sbk[m]])
                            return pts, kpt

                        def stage2(h, kt, pts, kpt):
                            fns = []
                            bset = set()
                            for m in range(2):
                                for j in range(4):
                                    qb = 4 * G + j
                                    if qb < kt:
                                        continue
                                    ap_, bnk = acc(m, j)
                                    st = bnk not in started
                                    started.add(bnk)
                                    bset.add(bnk)
                                    fns.append(lambda e, m=m, j=j, ap_=ap_, st=st, pts=pts, kt=kt: e.matmul(
                                        ap_, lhsT=pts[m][:, j * 128:(j + 1) * 128], rhs=Vp[:, kt, h, 0:129], start=st, stop=False, skip_group_check=True))
                            sy.op("pe", fns, r=kpt + [("Vp", kt), "Vp1"], w=[("ps", b_) for b_ in sorted(bset)], banks=sorted(bset))

                        def finish_head(h):
                            nonlocal hc
                            osb = Osb[hc % 2]
                            for bq in range(3):
                                if bq == 1:
                                    sy.op("dve", lambda e, bq=bq, osb=osb: e.tensor_copy(out=osb[:, 3 * bq:3 * bq + 3, :].rearrange("p a b -> p (a b)"), in_=ps[2 + bq][:, 0:390]),
                                          r=[("ps", 2 + bq)], w=[("Osb", hc % 2, bq)], banks=[2 + bq])
                                else:
                                    sy.op("act", lambda e, bq=bq, osb=osb: e.activation(out=osb[:, 3 * bq:3 * bq + 3, :].rearrange("p a b -> p (a b)"), in_=ps[2 + bq][:, 0:390], func=AF.Copy),
                                          r=[("ps", 2 + bq)], w=[("Osb", hc % 2, bq)], banks=[2 + bq])
                            kO = [("Osb", hc % 2, bq) for bq in range(3)]
                            hc += 1
                            yield
                            B4 = [128, 4, 128]
                            sy.op("dve", lambda e, osb=osb: e.reciprocal(out=cmb[:, 0:8], in_=osb[:, 0:8, 128]), r=kO, w=["cmb0"])
                            sy.op("dve", lambda e: e.tensor_scalar(out=cmb[:, 8:12], in0=cmb[:, 4:8], scalar1=lamc[:, 0:1], scalar2=None, op0=ALU.mult),
                                  r=["cmb0", "lamc"], w=["cmb1"])
                            sy.op("dve", lambda e, osb=osb: e.tensor_tensor(out=tq[:, :, :], in0=osb[:, 4:8, 0:128], in1=cmb[:, 8:12].unsqueeze(2).to_broadcast(B4), op=ALU.mult),
                                  r=kO + ["cmb1"], w=["tq"])
                            sy.op("pool", lambda e, osb=osb: e.tensor_tensor(out=ob[:, :, :], in0=osb[:, 0:4, 0:128], in1=cmb[:, 0:4].unsqueeze(2).to_broadcast(B4), op=ALU.mult),
                                  r=kO + ["cmb0"], w=["ob"])
                            yield
                            sy.op("dve", lambda e: e.tensor_tensor(out=ob[:, :, :], in0=ob[:, :, :], in1=tq[:, :, :], op=ALU.add), r=["ob", "tq"], w=["ob"])
                            sy.op("act", lambda e: e.activation(out=osq[:, :, :], in_=ob[:, :, :], func=AF.Square), r=["ob"], w=["osq"])
                            sy.op("dve", lambda e: e.tensor_reduce(out=cmb[:, 12:16], in_=osq[:, :, :], axis=AX.X, op=ALU.add), r=["osq"], w=["cmb3"])
                            sy.op("dve", lambda e: e.tensor_scalar(out=cmb[:, 16:20], in0=cmb[:, 12:16], scalar1=1.0 / 128, scalar2=SUBLN_EPS, op0=ALU.mult, op1=ALU.add),
                                  r=["cmb3"], w=["cmb4"])
                            yield
                            sy.op("pool", lambda e: e.tensor_tensor(out=cmb[:, 20:24], in0=cmb[:, 16:20], in1=C["neghalf"][:, 0:4], op=ALU.pow), r=["cmb4", ("c", "neghalf")], w=["cmb5"])
                            sy.op("dve", lambda e: e.tensor_tensor(out=ob[:, :, :], in0=ob[:, :, :], in1=cmb[:, 20:24].unsqueeze(2).to_broadcast(B4), op=ALU.mult),
                                  r=["ob", "cmb5"], w=["ob"])
                            sy.op("dve", lambda e: e.tensor_tensor(out=Oall[:, :, h * 128:(h + 1) * 128], in0=ob[:, :, :],
                                                                   in1=subg[:, h * 128:(h + 1) * 128].unsqueeze(1).to_broadcast(B4), op=ALU.mult),
                                  r=["ob", "subg"], w=[("Oall", G % 2, j_) for j_ in range(4)])
                            yield


                        nkt = 4 * G + 4
                        items = [(h_, kt_) for h_ in range(8) for kt_ in range(nkt)]
                        ctxs = {}
                        for n_ in range(min(2, len(items))):
                            ctxs[n_] = stage1(*items[n_])
                        for n_, (h_, kt_) in enumerate(items):
                            if n_ + 2 < len(items):
                                ctxs[n_ + 2] = stage1(*items[n_ + 2])
                            stage2(h_, kt_, *ctxs.pop(n_))
                            yield
                            if kt_ == nkt - 1:
                                started.clear()
                                for _ in finish_head(h_):
                                    yield

                def genWoLN(G):
                    Oall = Oalls[G % 2]
                    for j in range(4):
                        tl = 4 * G + j
                        tk0 = s * T + tl * 128
                        xr = xres[tl % 2]
                        sy.dma("sp", xr[:, :], g.X2[tk0:tk0 + 128, :], w=[("axres", tl % 2)])
                        for half in range(2):
                            bb = bk2.next()
                            psb = ps[bb][:, 0:256].bitcast(BF16)
                            sy.op("pe", [lambda e, jj=jj, psb=psb, half=half, j=j: e.transpose(out=psb[:, jj * 128:(jj + 1) * 128],
                                                                                            in_=Oall[:, j, (half * 4 + jj) * 128:(half * 4 + jj + 1) * 128],
                                                                                            identity=C["identb"][:, :]) for jj in range(4)],
                                  r=[("Oall", G % 2, j), ("c", "identb")], w=[("ps", bb)], banks=[bb])
                            sy.op("act", lambda e, psb=psb, half=half: e.activation(out=oT[:, half * 4:(half + 1) * 4, :], in_=psb.rearrange("p (j t) -> p j t", t=128),
                                                                                    func=AF.Identity, scale=(1.0 - LAMBDA_INIT)), r=[("ps", bb)], w=["oT"], banks=[bb])
                            yield
                        for half in range(2):
                            bo = bk2.next()
                            sy.op("pe", [lambda e, kc=kc, bo=bo, half=half: e.matmul(ps[bo][:, :], lhsT=oT[:, kc, :], rhs=Wo[:, kc, half * 512:(half + 1) * 512],
                                                                                   start=(kc == 0), stop=(kc == 7)) for kc in range(8)], r=["oT", "Wob"], w=[("ps", bo)], banks=[bo])
                            sy.op("dve", lambda e, bo=bo, half=half, xr=xr: e.scalar_tensor_tensor(
                                out=z[:, half * 512:(half + 1) * 512], in0=xr[:, half * 512:(half + 1) * 512], scalar=ALPHA, in1=ps[bo][:, :],
                                op0=ALU.mult, op1=ALU.add), r=[("ps", bo), ("axres", tl % 2)], w=["az"], banks=[bo])
                            yield
                        resid_ln(g, LB, z, "az", (1, 0), g.X3, g.X3T, tk0, bk2, "l10", use_sqrt=False)
                        yield

                prevg = None
                for G in range(NG):
                    interleave(genAttn(G), prevg, ratio=3)
                    prevg = genWoLN(G)
                for _ in prevg:
                    pass
            sy.barrier()


def make_in_maps(inputs, T, NSEQ, ncores):
    f = lambda a: np.ascontiguousarray(np.asarray(a, dtype=np.float32))
    x = f(inputs["x"]).reshape(-1, D)
    vec = {}
    mu = f(inputs["a_mu"])[0]
    for i in range(6):
        vec["mu%d" % i] = mu[i]
    vec["w0"] = f(inputs["a_w0"])[0]; vec["a0"] = f(inputs["a_a0"])[0]
    vec["k_k"] = f(inputs["a_k_k"])[0]; vec["k_a"] = f(inputs["a_k_a"])[0]
    vec["r_k"] = f(inputs["a_r_k"])[0].reshape(-1)
    vec["lnx_g"] = f(inputs["a_lnx_g"])[0]; vec["lnx_b"] = f(inputs["a_lnx_b"])[0]
    lg, lb = f(inputs["ln_g"]), f(inputs["ln_b"])
    for l in range(2):
        for i in range(2):
            vec["ln_g%d%d" % (l, i)] = lg[l, i]
            vec["ln_b%d%d" % (l, i)] = lb[l, i]
    vec["subln"] = np.tile(f(inputs["b_subln_g"])[0], 8)
    pf = np.stack([vec[n].reshape(8, 128).T for n in PF_NAMES], axis=1).reshape(128, NPF * 8)
    pt = np.stack([vec[n] for n in PT_NAMES], axis=0)
    common = {
        "a_w_r": f(inputs["a_w_r"])[0], "a_w_k": f(inputs["a_w_k"])[0], "a_w_v": f(inputs["a_w_v"])[0], "a_w_o": f(inputs["a_w_o"])[0],
        "b_w_q": f(inputs["b_w_q"])[0], "b_w_o": f(inputs["b_w_o"])[0], "b_w_kv": f(inputs["b_w_kv"]),
        "a_w1": f(inputs["a_w1"])[0], "a_a1": f(inputs["a_a1"])[0], "a_g1": f(inputs["a_g1"])[0],
        "a_w2": f(inputs["a_w2"])[0], "a_a2": f(inputs["a_a2"])[0], "a_g2": f(inputs["a_g2"])[0],
        "mlp_w1": f(inputs["mlp_w1"]), "mlp_w2": f(inputs["mlp_w2"]),
        "pf": np.ascontiguousarray(pf), "pt": np.ascontiguousarray(pt),
        "b_lam": f(inputs["b_lam"]).reshape(1, 256), "rel_bias": f(inputs["rel_bias"]),
        "onehot": _onehot_np(),
    }
    for k, v in _consts_np(256).items():
        common["c_" + k] = v
    maps = []
    for c in range(ncores):
        m = dict(common)
        m["x"] = np.ascontiguousarray(x[c * NSEQ * T:(c + 1) * NSEQ * T])
        maps.append(m)
    return maps


def _onehot_np():
    oh = np.zeros((32, 384), np.float32)
    for i in range(383):
        rel = 127 - i
        nb, max_exact = 16, 8
        ret = nb if rel > 0 else 0
        n = abs(rel)
        nf = np.float32(max(n, 1))
        large = max_exact + int(np.float32(np.log(nf / np.float32(max_exact))) / np.float32(math.log(128 / max_exact)) * (nb - max_exact))
        large = min(large, nb - 1)
        bkt = ret + (n if n < max_exact else large)
        oh[bkt, i] += 1.0
        oh[15, i] -= 1.0
    return oh


_PROG = {}


def kernel(**inputs):
    T, NSEQ, NC = 2048, 2, 8
    if "p" not in _PROG:
        _PROG["p"] = build_program(T, NSEQ)
    nc, g = _PROG["p"]
    maps = make_in_maps(inputs, T, NSEQ, NC)
    res = run_bass_kernel_spmd(nc, maps, core_ids=list(range(NC)))
    out = np.concatenate([np.asarray(r["out"]) for r in res.results], axis=0)
    return out.reshape(16, 2048, D).astype(np.float32)
```

```python
import math
import numpy as np
from contextlib import ExitStack
import concourse.bass as bass
import concourse.mybir as mybir
from concourse.bass_utils import run_bass_kernel_spmd

F32 = mybir.dt.float32
BF16 = mybir.dt.bfloat16
AF = mybir.ActivationFunctionType
ALU = mybir.AluOpType
AX = mybir.AxisListType

D = 1024
KC = 8
DFF = 4096
DEPTH = 2
ALPHA = (2.0 * DEPTH) ** 0.25
C1 = math.exp(-0.5)
GN_EPS = 64e-5
LN_EPS = 1e-5
SUBLN_EPS = 1e-5
NDMA = 40
STAGE = 9
SUB = 9


class Sync:
    def __init__(self, nc, es):
        self.nc = nc
        self.eng = {"pe": nc.tensor, "act": nc.scalar, "dve": nc.vector, "pool": nc.gpsimd, "sp": nc.sync}
        self.sem = {k: es.enter_context(nc.semaphore("s_" + k)) for k in self.eng}
        self.cnt = {k: 0 for k in self.eng}
        self.dsem = [es.enter_context(nc.semaphore("s_dma%d" % i)) for i in range(2 * NDMA)]
        self.dcnt = [0] * (2 * NDMA)
        self.drr = {"sp": 0, "pool": 0, "act": 0}
        self.known = {k: {} for k in self.eng}
        self.res = {}
        self.bank = {}
        self.ninst = 0

    def _semof(self, sk):
        return self.sem[sk] if isinstance(sk, str) else self.dsem[sk]

    def _wait(self, e, deps):
        kn = self.known[e]
        for sk, val in deps.items():
            if sk == "pe" and e == "pe":
                continue
            if kn.get(sk, 0) >= val:
                continue
            self.eng[e].wait_ge(self._semof(sk), val)
            self.ninst += 1
            kn[sk] = val

    def _deps(self, e, r, w, banks):
        deps = {}

        def add(tok):
            if tok is None:
                return
            sk, val = tok
            if deps.get(sk, 0) < val:
                deps[sk] = val
        for k in r:
            st = self.res.get(k)
            if st:
                add(st[0])
        for k in w:
            st = self.res.get(k)
            if st:
                add(st[0])
                for t in st[1]:
                    add(t)
        for b in banks:
            for e2, c in self.bank.get(b, {}).items():
                if e2 != e:
                    add((e2, c))
        return deps

    def _commit(self, tok, r, w):
        for k in r:
            self.res.setdefault(k, [None, []])[1].append(tok)
        for k in w:
            self.res[k] = [tok, []]

    def op(self, e, fns, r=(), w=(), banks=()):
        self._wait(e, self._deps(e, r, w, banks))
        if not isinstance(fns, (list, tuple)):
            fns = [fns]
        ins = None
        for f in fns:
            ins = f(self.eng[e])
            self.ninst += 1
        self.cnt[e] += 1
        ins.then_inc(self.sem[e], 1)
        tok = (e, self.cnt[e])
        self._commit(tok, r, w)
        for b in banks:
            self.bank.setdefault(b, {})[e] = self.cnt[e]
        return tok

    def dma(self, q, out, in_, r=(), w=()):
        i = self.drr[q] + (NDMA if q == "pool" else 0)
        self.drr[q] = (self.drr[q] + 1) % NDMA
        deps = self._deps(q, r, w, ())
        if self.dcnt[i] and deps.get(i, 0) < self.dcnt[i]:
            deps[i] = self.dcnt[i]
        self._wait(q, deps)
        self.eng[q].dma_start(out=out, in_=in_).then_inc(self.dsem[i], 16)
        self.ninst += 1
        self.dcnt[i] += 16
        tok = (i, self.dcnt[i])
        self._commit(tok, r, w)
        return tok

    def barrier(self):
        for e in self.eng:
            deps = {k: c for k, c in self.cnt.items() if c > 0}
            for i in range(2 * NDMA):
                if self.dcnt[i]:
                    deps[i] = self.dcnt[i]
            kn = self.known[e]
            for sk, val in deps.items():
                if kn.get(sk, 0) >= val:
                    continue
                self.eng[e].wait_ge(self._semof(sk), val)
                kn[sk] = val
        self.res = {}
        self.bank = {}


class Ctx:
    pass


def _consts_np(NB):
    import ml_dtypes
    c = {}
    p = np.arange(128)
    same = (p[:, None] // 64) == (p[None, :] // 64)
    s_lt_t = (p[:, None] % 64) < (p[None, :] % 64)
    s_le_t = (p[:, None] % 64) <= (p[None, :] % 64)
    strict = (same & s_lt_t).astype(np.float32)
    incl = (same & s_le_t).astype(np.float32)
    lower = (same & s_lt_t.T).astype(np.float32)
    m12 = np.concatenate([strict, incl], 1)
    c["mask12"] = np.concatenate([m12, m12], 1).astype(ml_dtypes.bfloat16)
    c["mask3"] = np.concatenate([lower, lower], 1).astype(ml_dtypes.bfloat16)
    c["identf"] = np.eye(128, dtype=np.float32)
    c["identb"] = np.eye(128).astype(ml_dtypes.bfloat16)
    c["blockones"] = same.astype(np.float32)
    sel = np.zeros((128, 2), np.float32)
    sel[:64, 0] = 1
    sel[64:, 1] = 1
    c["sel"] = sel.astype(ml_dtypes.bfloat16)
    rm = np.ones((128, NB), np.float32)
    rm[:, ::64] = 0
    c["resetmask"] = rm
    c["neghalf"] = np.full((128, NB), -0.5, np.float32)
    c["antiI"] = np.ascontiguousarray(np.eye(128, dtype=np.float32)[::-1])
    c["maskd"] = ((p[:, None] // 64) <= (p[None, :] // 64)).astype(np.float32)
    return c


CONST_SPECS = None


def build_program(T, NSEQ, dbg=False, layers=(0, 1)):
    NB = 256
    NTOK = T * NSEQ
    nc = bass.Bass("TRN2", target_bir_lowering=False)
    cn = _consts_np(NB)

    def din(name, shape, dt=F32):
        return nc.dram_tensor(name, list(shape), dt, kind="ExternalInput").ap()

    def dscr(name, shape, dt):
        return nc.dram_tensor(name, list(shape), dt, kind="Internal").ap()

    g = Ctx()
    g.nc = nc
    g.T, g.NSEQ, g.NTOK, g.NB = T, NSEQ, NTOK, NB
    I = {}
    I["x"] = din("x", [NTOK, D])
    for nm in ("a_w_r", "a_w_k", "a_w_v", "a_w_o", "b_w_q", "b_w_o"):
        I[nm] = din(nm, [D, D])
    I["b_w_kv"] = din("b_w_kv", [D, 2 * D])
    I["a_w1"] = din("a_w1", [D, 64]); I["a_a1"] = din("a_a1", [D, 64]); I["a_g1"] = din("a_g1", [D, 128])
    I["a_w2"] = din("a_w2", [64, D]); I["a_a2"] = din("a_a2", [64, D]); I["a_g2"] = din("a_g2", [128, D])
    I["mlp_w1"] = din("mlp_w1", [2, D, DFF]); I["mlp_w2"] = din("mlp_w2", [2, DFF, D])
    I["pf"] = din("pf", [128, NPF * 8])
    I["pt"] = din("pt", [NPT, D])
    I["b_lam"] = din("b_lam", [1, 256])
    I["rel_bias"] = din("rel_bias", [32, 8])
    I["onehot"] = din("onehot", [32, 384])
    for k, v in cn.items():
        I["c_" + k] = din("c_" + k, v.shape, BF16 if v.dtype != np.float32 else F32)
    g.I = I
    out = nc.dram_tensor("out", [NTOK, D], F32, kind="ExternalOutput").ap()
    g.out = out
    g.XF = dscr("s_xf", [NTOK // 128, 64, 16, 4, 128], BF16)
    g.TMO = dscr("s_tmo", [4, NTOK, D], BF16)
    g.SG = dscr("s_sg", [128, NTOK], BF16)
    g.AO = dscr("s_ao", [NTOK // 128, 8, 128, 1024], BF16)
    g.X1 = dscr("s_x1", [NTOK, D], F32)
    g.X1T = dscr("s_x1t", [8, 128, NTOK], BF16)
    g.X2 = dscr("s_x2", [NTOK, D], F32)
    g.X2T = dscr("s_x2t", [8, 128, NTOK], BF16)
    g.X3 = dscr("s_x3", [NTOK, D], F32)
    g.X3T = dscr("s_x3t", [8, 128, NTOK], BF16)
    if dbg:
        g.dbg = {}
        g.dbg["h0"] = nc.dram_tensor("dbg_h0", [NTOK, D], F32, kind="ExternalOutput").ap()
        g.X2 = nc.dram_tensor("dbg_x2", [NTOK, D], F32, kind="ExternalOutput").ap()

    with ExitStack() as es:
        sy = Sync(nc, es)
        g.sy = sy
        g.ps = [es.enter_context(nc.psum_tensor("psb%d" % b, [128, 512], F32)) for b in range(8)]
        used = {}

        def sb(name, shape, dt, stack=es):
            used[name] = used.get(name, 0) + 1
            if used[name] > 1:
                name = "%s_v%d" % (name, used[name])
            return stack.enter_context(nc.sbuf_tensor(name, list(shape), dt))
        g.sb = sb
        g.C = {}
        for k, v in cn.items():
            t = sb("k_" + k, v.shape, BF16 if v.dtype != np.float32 else F32)
            sy.dma("sp", t[:], I["c_" + k][:, :], w=[("c", k)])
            g.C[k] = t
        g.pf = sb("pfs", [128, NPF, 8], F32)
        sy.dma("sp", g.pf[:], I["pf"].rearrange("p (v c) -> p v c", c=8), w=["pf"])
        for b in range(8):
            sy.op("dve", lambda e, b=b: e.memset(g.ps[b][:], 0.0), w=[("ps", b)], banks=[b])
        sy.barrier()
        if 0 in layers and STAGE >= 2:
            rwkv_layer(g)
            sy.barrier()
            if STAGE >= 4:
                mlp_layer(g, 0, g.X1, g.X1T, g.X2, g.X2T, final=False)
            sy.barrier()
        if 1 in layers:
            attn_layer(g)
            sy.barrier()
            mlp_layer(g, 1, g.X3, g.X3T, g.out, None, final=True)
        sy.barrier()
    g.ninst = sy.ninst
    return nc, g


PF_NAMES = ["mu0", "mu1", "mu2", "mu3", "mu4", "mu5", "w0", "a0", "k_k", "k_a", "r_k",
            "ln_g00", "ln_b00", "ln_g01", "ln_b01", "ln_g10", "ln_b10", "ln_g11", "ln_b11"]
NPF = len(PF_NAMES)
PT_NAMES = ["lnx_g", "lnx_b", "ln_g00", "ln_b00", "ln_g01", "ln_b01", "ln_g10", "ln_b10", "ln_g11", "ln_b11",
            "subln"]
NPT = len(PT_NAMES)


def PFI(name):
    return PF_NAMES.index(name)


def PTI(name):
    return PT_NAMES.index(name)


class Banks:
    def __init__(self, ids):
        self.ids = list(ids)
        self.i = 0

    def next(self):
        b = self.ids[self.i % len(self.ids)]
        self.i += 1
        return b


def load_w(g, es, name, src, kcs, n):
    t = g.sb(name, [128, kcs, n], BF16, es)
    g.sy.dma("pool", t[:], src.rearrange("(kc p) n -> p kc n", p=128), w=[name])
    return t


def bcast_rows(g, es, name, row):
    t = g.sb(name, [128, D], F32, es)
    g.sy.dma("sp", t[:], g.I["pt"][row:row + 1, :].partition_broadcast(128), w=[name])
    return t


def resid_ln(g, es_bufs, z, zkey, li, out_tm, out_fm, tk0, bk, tag, use_sqrt=True):
    sy, nc, pf = g.sy, g.nc, g.pf
    B = es_bufs
    q = B["cnt"][0] % B["nbuf"]
    B["cnt"][0] += 1
    tq_ = "%s_%d" % (tag, q)
    st, mv, sm = B["st"][q], B["mv"][q], B["sm"][q]
    sy.op("dve", [lambda e, h=h: e.bn_stats(out=st[:, h, :], in_=z[:, h * 512:(h + 1) * 512]) for h in range(2)],
          r=[zkey], w=[tq_ + "st"])
    sy.op("dve", lambda e: e.bn_aggr(out=mv[:, :], in_=st[:].rearrange("p a b -> p (a b)")), r=[tq_ + "st"], w=[tq_ + "mv"])
    sy.op("dve", lambda e: e.tensor_scalar(out=sm[:, 0:1], in0=mv[:, 1:2], scalar1=LN_EPS, scalar2=None, op0=ALU.add),
          r=[tq_ + "mv"], w=[tq_ + "sm0"])
    if use_sqrt:
        sy.op("act", lambda e: e.activation(out=sm[:, 3:4], in_=sm[:, 0:1], func=AF.Sqrt), r=[tq_ + "sm0"], w=[tq_ + "sm3"])
        sy.op("dve", lambda e: e.reciprocal(out=sm[:, 1:2], in_=sm[:, 3:4]), r=[tq_ + "sm3"], w=[tq_ + "sm1"])
    else:
        sy.op("pool", lambda e: e.tensor_tensor(out=sm[:, 1:2], in0=sm[:, 0:1], in1=g.C["neghalf"][:, 0:1], op=ALU.pow),
              r=[tq_ + "sm0", ("c", "neghalf")], w=[tq_ + "sm1"])
    sy.op("dve", lambda e: e.scalar_tensor_tensor(out=sm[:, 2:3], in0=mv[:, 0:1], scalar=-1.0, in1=sm[:, 1:2],
                                                  op0=ALU.mult, op1=ALU.mult), r=[tq_ + "mv", tq_ + "sm1"], w=[tq_ + "sm2"])
    xn = B["xn"][q]
    sy.op("act", lambda e: e.activation(out=xn[:, :], in_=z[:, :], func=AF.Identity, bias=sm[:, 2:3], scale=sm[:, 1:2]),
          r=[zkey, tq_ + "sm1", tq_ + "sm2"], w=[tq_ + "xn"])
    gi, bi = PFI("ln_g%d%d" % li), PFI("ln_b%d%d" % li)
    xo = B["xo"][q]
    sy.op("dve", lambda e: e.tensor_tensor(out=xo[:, :], in0=xn[:, :], in1=B["lng"][:, :], op=ALU.mult),
          r=[tq_ + "xn", tag + "lng"], w=[tq_ + "xo"])
    sy.op("pool", lambda e: e.tensor_tensor(out=xo[:, :], in0=xo[:, :], in1=B["lnb"][:, :], op=ALU.add),
          r=[tq_ + "xo", tag + "lnb"], w=[tq_ + "xo"])
    sy.dma("sp", out_tm[tk0:tk0 + 128, :], xo[:, :], r=[tq_ + "xo"], w=[("dram", out_tm.name, tk0)])
    if out_fm is not None:
        xT = B["xT"][q]
        for half in range(2):
            b = bk.next()
            sy.op("pe", [lambda e, j=j, b=b, half=half: e.transpose(out=g.ps[b][:, j * 128:(j + 1) * 128],
                                                                      in_=xn[:, (half * 4 + j) * 128:(half * 4 + j + 1) * 128],
                                                                      identity=g.C["identf"][:, :]) for j in range(4)],
                  r=[tq_ + "xn", ("c", "identf")], w=[("ps", b)], banks=[b])
            sy.op("act", [lambda e, j=j, b=b, half=half: e.activation(
                out=xT[:, half * 4 + j, :], in_=g.ps[b][:, j * 128:(j + 1) * 128], func=AF.Identity,
                bias=pf[:, bi, half * 4 + j:half * 4 + j + 1], scale=pf[:, gi, half * 4 + j:half * 4 + j + 1]) for j in range(4)],
                  r=[("ps", b), "pf"], w=[tq_ + "xT"], banks=[b])
        sy.dma("sp", out_fm[:, :, tk0:tk0 + 128].rearrange("c p t -> p c t"), xT[:, :, :], r=[tq_ + "xT"],
               w=[("dram", out_fm.name, tk0)])


def ln_bufs(g, es, li, tag, nbuf=1):
    B = {"nbuf": nbuf, "cnt": [0]}
    B["st"] = [g.sb(tag + "st%d" % i, [128, 2, 6], F32, es) for i in range(nbuf)]
    B["mv"] = [g.sb(tag + "mv%d" % i, [128, 2], F32, es) for i in range(nbuf)]
    B["sm"] = [g.sb(tag + "sm%d" % i, [128, 4], F32, es) for i in range(nbuf)]
    B["xn"] = [g.sb(tag + "xn%d" % i, [128, D], F32, es) for i in range(nbuf)]
    B["xo"] = [g.sb(tag + "xo%d" % i, [128, D], F32, es) for i in range(nbuf)]
    B["xT"] = [g.sb(tag + "xT%d" % i, [128, 8, 128], BF16, es) for i in range(nbuf)]
    B["lng"] = bcast_rows(g, es, tag + "lng", PTI("ln_g%d%d" % li))
    B["lnb"] = bcast_rows(g, es, tag + "lnb", PTI("ln_b%d%d" % li))
    return B


def rwkv_layer(g):
    sy, nc, C, pf, I = g.sy, g.nc, g.C, g.pf, g.I
    T, NSEQ, NTOK, NB = g.T, g.NSEQ, g.NTOK, g.NB
    NTB = NB // 128
    NCH = NB // 64
    ps = g.ps
    with ExitStack() as esl:
        GC = g.sb("GC", [128, 8, NTOK // 64], F32, esl)
        S_all = g.sb("S_all", [128, NTOK // 128, 16], F32, esl)
        sgT = None
        with ExitStack() as es:
            sb = lambda n, s, d: g.sb(n, s, d, es)
            Wr = load_w(g, es, "Wr", I["a_w_r"], 8, D)
            Wk = load_w(g, es, "Wk", I["a_w_k"], 8, D)
            Wv = load_w(g, es, "Wv", I["a_w_v"], 8, D)
            w1 = load_w(g, es, "w1", I["a_w1"], 8, 64)
            a1 = load_w(g, es, "a1", I["a_a1"], 8, 64)
            g1 = load_w(g, es, "g1", I["a_g1"], 8, 128)
            w2 = sb("w2", [64, D], BF16); sy.dma("pool", w2[:], I["a_w2"][:, :], w=["w2"])
            a2 = sb("a2", [64, D], BF16); sy.dma("pool", a2[:], I["a_a2"][:, :], w=["a2"])
            dv = sb("dv", [128, 3, 8], F32)
            sy.op("dve", lambda e: e.tensor_scalar(out=dv[:, 0, :], in0=pf[:, PFI("w0"), :], scalar1=0.5, scalar2=None, op0=ALU.mult), r=["pf"], w=["dv"])
            sy.op("dve", lambda e: e.tensor_scalar(out=dv[:, 1, :], in0=pf[:, PFI("a0"), :], scalar1=0.5, scalar2=None, op0=ALU.mult), r=["pf"], w=["dv"])
            sy.op("dve", lambda e: e.tensor_scalar(out=dv[:, 2, :], in0=pf[:, PFI("k_a"), :], scalar1=-1.0, scalar2=1.0, op0=ALU.mult, op1=ALU.add), r=["pf"], w=["dv"])
            XS = [sb("XS%d" % i, [128, 8, NB + 1], BF16) for i in range(2)]
            xx = sb("xx", [128, 8, NB], BF16)
            xm = [sb("xm%d" % i, [128, 8, NB], BF16) for i in range(2)]
            xmk = [sb("xmk%d" % i, [128, 8, NB], BF16) for i in range(2)]
            xt = [sb("xt%d" % i, [128, D], F32) for i in range(4)]
            Rfs = [sb("Rf%d" % i, [128, 8, NB], F32) for i in range(2)]
            h1s = [sb("h1_%d" % i, [64, NB], BF16) for i in range(2)]
            h2s = [sb("h2_%d" % i, [64, NB], BF16) for i in range(2)]
            tgb = sb("tgb", [128, NB], F32)
            Vt = [sb("Vt%d" % i, [128, D], BF16) for i in range(2)]
            XFb = [sb("XFb%d" % i, [128, 4, NB], BF16) for i in range(2)]
            TMb = sb("TMb", [128, NTB, 3, D], BF16)
            tn = ["tw", "ta", "CS", "CSx", "Dh", "E1", "E2", "Ep", "Eh", "kkraw", "sq", "ssb", "rn", "kk", "t1", "k2",
                  "kb", "Bhat", "Khat", "Atm", "tg"]
            tms = [{n: sb("t%d_" % i + n, [128, NB], F32) for n in tn} for i in range(2)]
            tm = tms[0]
            prods = [sb("prod%d" % i, [128, NB], BF16) for i in range(2)]
            sgb = [sb("sgb%d" % i, [128, NB], BF16) for i in range(2)]
            bk = Banks([6])
            SB_ = 7
            bkos = [Banks([1, 2]), Banks([4, 5])]

            def genPre(blk, s, b):
                bp = blk % 2
                Rf, h1, h2 = Rfs[bp], h1s[bp], h2s[bp]
                if True:
                    tok0 = s * T + b * NB
                    XSc, XSp = XS[blk % 2], XS[(blk + 1) % 2]
                    kXS = ("XS", blk % 2)
                    if b == 0:
                        sy.op("pool", lambda e: e.memset(XSc[:, :, 0:1], 0.0), w=[kXS])
                        yield
                    else:
                        sy.op("pool", lambda e: e.tensor_copy(out=XSc[:, :, 0:1], in_=XSp[:, :, NB:NB + 1]),
                              r=[("XS", (blk + 1) % 2)], w=[kXS])
                        yield
                    def xload(bi_):
                        s2_, b2_ = blocks[bi_]
                        t0_ = s2_ * T + b2_ * NB
                        for tt_ in range(NTB):
                            xi_ = (bi_ % 2) * 2 + tt_
                            sy.dma("sp", xt[xi_][:, :], I["x"][t0_ + tt_ * 128: t0_ + (tt_ + 1) * 128, :], w=[("xt", xi_)])
                    if blk == 0:
                        xload(0)
                    if blk + 1 < len(blocks):
                        xload(blk + 1)
                    for tt in range(NTB):
                        xti = xt[(blk % 2) * 2 + tt]
                        for half in range(2):
                            bb = bk.next()
                            sy.op("pe", [lambda e, j=j, bb=bb, half=half, xti=xti: e.transpose(
                                out=ps[bb][:, j * 128:(j + 1) * 128], in_=xti[:, (half * 4 + j) * 128:(half * 4 + j + 1) * 128],
                                identity=C["identf"][:, :]) for j in range(4)],
                                r=[("xt", (blk % 2) * 2 + tt), ("c", "identf")], w=[("ps", bb)], banks=[bb])
                            sy.op("act", lambda e, bb=bb, half=half, tt=tt: e.activation(
                                out=XSc[:, half * 4:(half + 1) * 4, 1 + tt * 128:1 + (tt + 1) * 128],
                                in_=ps[bb][:, :].rearrange("p (j t) -> p j t", t=128), func=AF.Copy),
                                r=[("ps", bb)], w=[kXS], banks=[bb])
                            yield
                    sy.op("dve", lambda e: e.tensor_tensor(out=xx[:, :, :], in0=XSc[:, :, 0:NB], in1=XSc[:, :, 1:NB + 1], op=ALU.subtract),
                          r=[kXS], w=["xx"])
                    yield

                    def mix(mi, di, dst=None, key=None):
                        dst = xm[di] if dst is None else dst
                        key = ("xm", di) if key is None else key
                        sy.op("dve", [lambda e, kc=kc: e.scalar_tensor_tensor(
                            out=dst[:, kc, :], in0=xx[:, kc, :], scalar=pf[:, mi, kc:kc + 1], in1=XSc[:, kc, 1:NB + 1],
                            op0=ALU.mult, op1=ALU.add) for kc in range(8)], r=["xx", kXS, "pf"], w=[key])
                        return dst, key

                    xa_, kx = mix(1, 0)
                    bb = bk.next()
                    sy.op("pe", [lambda e, kc=kc, bb=bb, xa_=xa_: e.matmul(ps[bb][0:64, 0:NB], lhsT=w1[:, kc, :], rhs=xa_[:, kc, :],
                                                                         start=(kc == 0), stop=(kc == 7)) for kc in range(8)],
                          r=[kx, "w1"], w=[("ps", bb)], banks=[bb])
                    yield
                    sy.op("act", lambda e, bb=bb: e.activation(out=h1[:, :], in_=ps[bb][0:64, 0:NB], func=AF.Tanh),
                          r=[("ps", bb)], w=[("h1", bp)], banks=[bb])
                    yield
                    xa_, kx = mix(4, 1)
                    bb = bk.next()
                    sy.op("pe", [lambda e, kc=kc, bb=bb, xa_=xa_: e.matmul(ps[bb][0:64, 0:NB], lhsT=a1[:, kc, :], rhs=xa_[:, kc, :],
                                                                         start=(kc == 0), stop=(kc == 7)) for kc in range(8)],
                          r=[kx, "a1"], w=[("ps", bb)], banks=[bb])
                    yield
                    sy.op("act", lambda e, bb=bb: e.activation(out=h2[:, :], in_=ps[bb][0:64, 0:NB], func=AF.Copy),
                          r=[("ps", bb)], w=[("h2", bp)], banks=[bb])
                    yield
                    xa_, kx = mix(5, 0)
                    bb = bk.next()
                    sy.op("pe", [lambda e, kc=kc, bb=bb, xa_=xa_: e.matmul(ps[bb][:, 0:NB], lhsT=g1[:, kc, :], rhs=xa_[:, kc, :],
                                                                         start=(kc == 0), stop=(kc == 7)) for kc in range(8)],
                          r=[kx, "g1"], w=[("ps", bb)], banks=[bb])
                    yield
                    sy.op("act", lambda e, bb=bb: e.activation(out=tgb[:, :], in_=ps[bb][:, 0:NB], func=AF.Tanh, scale=0.5),
                          r=[("ps", bb)], w=["tg"], banks=[bb])
                    yield
                    sgi = sgb[blk % 2]
                    sy.op("dve", lambda e, sgi=sgi: e.tensor_scalar(out=sgi[:, :], in0=tgb[:, :], scalar1=0.5, scalar2=0.5,
                                                                    op0=ALU.mult, op1=ALU.add), r=["tg"], w=[("sgb", blk % 2)])
                    yield
                    sy.dma("sp", g.SG[:, tok0:tok0 + NB], sgi[:, :], r=[("sgb", blk % 2)], w=[("dram", "sg", tok0)])
                    yield
                    xa_, kx = mix(3, 1)
                    for tt in range(NTB):
                        vti = Vt[tt % 2]
                        for half in range(2):
                            bb = bk.next()
                            sy.op("pe", [lambda e, kc=kc, bb=bb, xa_=xa_, tt=tt, half=half: e.matmul(
                                ps[bb][:, :], lhsT=xa_[:, kc, tt * 128:(tt + 1) * 128], rhs=Wv[:, kc, half * 512:(half + 1) * 512],
                                start=(kc == 0), stop=(kc == 7)) for kc in range(8)],
                                r=[kx, "Wv"], w=[("ps", bb)], banks=[bb])
                            sy.op("act", lambda e, bb=bb, vti=vti, half=half: e.activation(
                                out=vti[:, half * 512:(half + 1) * 512], in_=ps[bb][:, :], func=AF.Copy),
                                r=[("ps", bb)], w=[("Vt", tt % 2)], banks=[bb])
                        sy.dma("sp", g.TMO[0, tok0 + tt * 128:tok0 + (tt + 1) * 128, :], vti[:, :], r=[("Vt", tt % 2)],
                               w=[("dram", "tmo0", tok0 + tt * 128)])
                        yield
                    xa_, kx = mix(0, 0)
                    for oc in range(8):
                        bb = bk.next()
                        sy.op("pe", [lambda e, kc=kc, bb=bb, xa_=xa_, oc=oc: e.matmul(
                            ps[bb][:, 0:NB], lhsT=Wr[:, kc, oc * 128:(oc + 1) * 128], rhs=xa_[:, kc, :],
                            start=(kc == 0), stop=(kc == 7)) for kc in range(8)], r=[kx, "Wr"], w=[("ps", bb)], banks=[bb])
                        sy.op("act", lambda e, bb=bb, oc=oc: e.activation(out=Rf[:, oc, :], in_=ps[bb][:, 0:NB], func=AF.Copy),
                              r=[("ps", bb)], w=[("Rf", bp, oc)], banks=[bb])
                        yield
                    mix(2, 1, xmk[bp], ("xmk", bp))
                    yield

            def genMain(blk, s, b):
                bp = blk % 2
                Rf, h1, h2 = Rfs[bp], h1s[bp], h2s[bp]
                xa_, kx = xmk[bp], ("xmk", bp)
                tok0 = s * T + b * NB
                if True:
                    def genOC(oc, t, prod, tsi):
                        bko = bkos[tsi]
                        bK, bW, bA = (0, 3)[tsi], bko.next(), bko.next()
                        yield
                        sy.op("pe", [lambda e, kc=kc, xa_=xa_, oc=oc, bK=bK: e.matmul(
                            ps[bK][:, 0:NB], lhsT=Wk[:, kc, oc * 128:(oc + 1) * 128], rhs=xa_[:, kc, :],
                            start=(kc == 0), stop=(kc == 7)) for kc in range(8)], r=[kx, "Wk"], w=[("ps", bK)], banks=[bK])
                        yield
                        sy.op("pe", lambda e, oc=oc, bW=bW: e.matmul(ps[bW][:, 0:NB], lhsT=w2[:, oc * 128:(oc + 1) * 128], rhs=h1[:, :],
                                                                    start=True, stop=True), r=["w2", ("h1", bp)], w=[("ps", bW)], banks=[bW])
                        yield
                        sy.op("pe", lambda e, oc=oc, bA=bA: e.matmul(ps[bA][:, 0:NB], lhsT=a2[:, oc * 128:(oc + 1) * 128], rhs=h2[:, :],
                                                                    start=True, stop=True), r=["a2", ("h2", bp)], w=[("ps", bA)], banks=[bA])
                        yield
                        sy.op("act", lambda e, oc=oc, bW=bW: e.activation(out=t["tw"][:, :], in_=ps[bW][:, 0:NB], func=AF.Tanh,
                                                                          bias=dv[:, 0, oc:oc + 1], scale=0.5),
                              r=[("ps", bW), "dv"], w=[(tsi, "tw")], banks=[bW])
                        yield
                        sy.op("act", lambda e, oc=oc, bA=bA: e.activation(out=t["ta"][:, :], in_=ps[bA][:, 0:NB], func=AF.Tanh,
                                                                          bias=dv[:, 1, oc:oc + 1], scale=0.5),
                              r=[("ps", bA), "dv"], w=[(tsi, "ta")], banks=[bA])
                        yield
                        sy.op("act", lambda e, oc=oc, bK=bK: e.activation(out=t["kkraw"][:, :], in_=ps[bK][:, 0:NB], func=AF.Identity,
                                                                          scale=pf[:, PFI("k_k"), oc:oc + 1]),
                              r=[("ps", bK), "pf"], w=[(tsi, "kkraw")], banks=[bK])
                        yield
                        sy.op("dve", lambda e: e.tensor_scalar(out=t["tw"][:, :], in0=t["tw"][:, :], scalar1=0.5, scalar2=0.5,
                                                               op0=ALU.mult, op1=ALU.add), r=[(tsi, "tw")], w=[(tsi, "tw")])
                        yield
                        sy.op("dve", lambda e: e.tensor_scalar(out=t["ta"][:, :], in0=t["ta"][:, :], scalar1=0.5, scalar2=0.5,
                                                               op0=ALU.mult, op1=ALU.add), r=[(tsi, "ta")], w=[(tsi, "ta")])
                        yield
                        sy.op("pool", lambda e: e.tensor_tensor(out=t["sq"][:, :], in0=t["kkraw"][:, :], in1=t["kkraw"][:, :], op=ALU.mult),
                              r=[(tsi, "kkraw")], w=[(tsi, "sq")])
                        yield
                        bS = bko.next()
                        yield
                        sy.op("pe", lambda e, bS=bS: e.matmul(ps[bS][:, 0:NB], lhsT=C["blockones"][:, :], rhs=t["sq"][:, :], start=True, stop=True),
                              r=[(tsi, "sq"), ("c", "blockones")], w=[("ps", bS)], banks=[bS])
                        yield
                        sy.op("dve", lambda e: e.tensor_tensor_scan(out=t["CS"][:, :], data0=C["resetmask"][:, :], data1=t["tw"][:, :],
                                                                    initial=0.0, op0=ALU.mult, op1=ALU.add),
                              r=[(tsi, "tw"), ("c", "resetmask")], w=[(tsi, "CS")])
                        yield
                        sy.op("dve", lambda e, oc=oc: e.tensor_scalar(out=t["t1"][:, :], in0=t["ta"][:, :], scalar1=pf[:, PFI("k_a"), oc:oc + 1],
                                                                      scalar2=dv[:, 2, oc:oc + 1], op0=ALU.mult, op1=ALU.add),
                              r=[(tsi, "ta"), "pf", "dv"], w=[(tsi, "t1")])
                        yield
                        sy.op("dve", lambda e, bK=bK: e.tensor_tensor(out=t["k2"][:, :], in0=ps[bK][:, 0:NB], in1=t["t1"][:, :], op=ALU.mult),
                              r=[("ps", bK), (tsi, "t1")], w=[(tsi, "k2")], banks=[bK])
                        yield
                        sy.op("dve", lambda e, bS=bS: e.tensor_scalar(out=t["ssb"][:, :], in0=ps[bS][:, 0:NB], scalar1=1e-24, scalar2=None, op0=ALU.max),
                              r=[("ps", bS)], w=[(tsi, "ssb")], banks=[bS])
                        yield
                        sy.op("dve", lambda e: e.tensor_tensor(out=t["CSx"][:, :], in0=t["CS"][:, :], in1=t["tw"][:, :], op=ALU.subtract),
                              r=[(tsi, "CS"), (tsi, "tw")], w=[(tsi, "CSx")])
                        yield
                        cs3 = t["CS"][:, :].rearrange("p (c j) -> p c j", j=64)
                        yield
                        sy.op("dve", lambda e, cs3=cs3: e.tensor_tensor(out=t["Dh"][:, :].rearrange("p (c j) -> p c j", j=64),
                                                                        in0=cs3[:, :, 63:64].to_broadcast([128, NCH, 64]), in1=cs3,
                                                                        op=ALU.subtract), r=[(tsi, "CS")], w=[(tsi, "Dh")])
                        yield
                        sy.op("act", lambda e: e.activation(out=t["E1"][:, :], in_=t["CS"][:, :], func=AF.Exp, scale=-C1), r=[(tsi, "CS")], w=[(tsi, "E1")])
                        yield
                        sy.op("act", lambda e: e.activation(out=t["E2"][:, :], in_=t["CS"][:, :], func=AF.Exp, scale=C1), r=[(tsi, "CS")], w=[(tsi, "E2")])
                        yield
                        sy.op("act", lambda e: e.activation(out=t["sq"][:, :], in_=t["ssb"][:, :], func=AF.Sqrt), r=[(tsi, "ssb")], w=[(tsi, "sq")])
                        yield
                        sy.op("act", lambda e: e.activation(out=t["Ep"][:, :], in_=t["CSx"][:, :], func=AF.Exp, scale=-C1), r=[(tsi, "CSx")], w=[(tsi, "Ep")])
                        yield
                        sy.op("act", lambda e: e.activation(out=t["Eh"][:, :], in_=t["Dh"][:, :], func=AF.Exp, scale=-C1), r=[(tsi, "Dh")], w=[(tsi, "Eh")])
                        yield
                        ch0 = tok0 // 64
                        yield
                        sy.op("dve", lambda e, oc=oc, ch0=ch0: e.tensor_copy(
                            out=GC[:, oc, ch0:ch0 + NCH], in_=t["E1"][:, :].rearrange("p (c j) -> p c j", j=64)[:, :, 63]),
                            r=[(tsi, "E1")], w=[("GC", blk, oc)])
                        yield
                        sy.op("dve", lambda e: e.reciprocal(out=t["rn"][:, :], in_=t["sq"][:, :]), r=[(tsi, "sq")], w=[(tsi, "rn")])
                        yield
                        sy.op("pool", lambda e: e.tensor_tensor(out=t["kk"][:, :], in0=t["kkraw"][:, :], in1=t["rn"][:, :], op=ALU.mult),
                              r=[(tsi, "kkraw"), (tsi, "rn")], w=[(tsi, "kk")])
                        yield
                        sy.op("pool", lambda e: e.tensor_tensor(out=t["kb"][:, :], in0=t["kk"][:, :], in1=t["ta"][:, :], op=ALU.mult),
                              r=[(tsi, "kk"), (tsi, "ta")], w=[(tsi, "kb")])
                        yield
                        xfb = XFb[oc % 2]
                        yield
                        kxf = ("XFb", oc % 2)
                        yield
                        sy.op("pool", lambda e, xfb=xfb: e.tensor_tensor(out=xfb[:, 0, :], in0=t["kb"][:, :], in1=t["E2"][:, :], op=ALU.mult),
                              r=[(tsi, "kb"), (tsi, "E2")], w=[kxf])
                        yield
                        sy.op("pool", lambda e: e.tensor_tensor(out=t["Bhat"][:, :], in0=t["kb"][:, :], in1=t["Eh"][:, :], op=ALU.mult),
                              r=[(tsi, "kb"), (tsi, "Eh")], w=[(tsi, "Bhat")])
                        yield
                        sy.op("dve", lambda e, xfb=xfb: e.tensor_tensor(out=xfb[:, 1, :], in0=t["k2"][:, :], in1=t["E2"][:, :], op=ALU.mult),
                              r=[(tsi, "k2"), (tsi, "E2"), kxf], w=[kxf])
                        yield
                        sy.op("pool", lambda e: e.tensor_tensor(out=t["Khat"][:, :], in0=t["k2"][:, :], in1=t["Eh"][:, :], op=ALU.mult),
                              r=[(tsi, "k2"), (tsi, "Eh")], w=[(tsi, "Khat")])
                        yield
                        sy.op("dve", lambda e: e.scalar_tensor_tensor(out=t["Atm"][:, :], in0=t["kk"][:, :], scalar=-1.0, in1=t["Ep"][:, :],
                                                                      op0=ALU.mult, op1=ALU.mult), r=[(tsi, "kk"), (tsi, "Ep")], w=[(tsi, "Atm")])
                        yield
                        sy.op("act", lambda e, xfb=xfb: e.activation(out=xfb[:, 2, :], in_=t["Atm"][:, :], func=AF.Copy), r=[(tsi, "Atm"), kxf], w=[kxf])
                        yield
                        sy.op("dve", lambda e, xfb=xfb, oc=oc: e.tensor_tensor(out=xfb[:, 3, :], in0=Rf[:, oc, :], in1=t["E1"][:, :], op=ALU.mult),
                              r=[("Rf", bp, oc), (tsi, "E1"), kxf], w=[kxf])
                        yield
                        sy.op("dve", lambda e, oc=oc: e.scalar_tensor_tensor(out=prod[:, :], in0=Rf[:, oc, :], scalar=pf[:, PFI("r_k"), oc:oc + 1],
                                                                             in1=t["k2"][:, :], op0=ALU.mult, op1=ALU.mult),
                              r=[("Rf", bp, oc), (tsi, "k2"), "pf"], w=[(tsi, "prod")])
                        yield
                        sy.op("pe", [lambda e, tt=tt, oc=oc: e.matmul(ps[SB_][:, tt * 16 + oc * 2: tt * 16 + oc * 2 + 2],
                                                                      lhsT=prod[:, tt * 128:(tt + 1) * 128], rhs=C["sel"][:, :],
                                                                      start=True, stop=True) for tt in range(NTB)],
                              r=[(tsi, "prod"), ("c", "sel")], w=[("ps", SB_)], banks=[SB_])
                        yield
                        for tt in range(NTB):
                            bT = bko.next()
                            srcs = [t["Khat"], t["Bhat"], t["Atm"]]
                            sy.op("pe", [lambda e, j=j, tt=tt, bT=bT, srcs=srcs: e.transpose(
                                out=ps[bT][:, j * 128:(j + 1) * 128], in_=srcs[j][:, tt * 128:(tt + 1) * 128], identity=C["identf"][:, :])
                                for j in range(3)], r=[(tsi, "Khat"), (tsi, "Bhat"), (tsi, "Atm"), ("c", "identf")], w=[("ps", bT)], banks=[bT])
                            sy.op("act", lambda e, tt=tt, bT=bT, oc=oc: e.activation(
                                out=TMb[:, tt, :, oc * 128:(oc + 1) * 128], in_=ps[bT][:, 0:384].rearrange("p (j t) -> p j t", t=128),
                                func=AF.Copy), r=[("ps", bT)], w=[("TMb", tt, oc)], banks=[bT])
                            yield
                        for h_ in range(2):
                            for tt_ in range(NTB):
                                sy.dma("sp", g.XF[tok0 // 128 + tt_, :, 2 * oc + h_, :, :], xfb[h_ * 64:(h_ + 1) * 64, :, tt_ * 128:(tt_ + 1) * 128],
                                       r=[kxf], w=[("dram", "xf", oc, tok0, h_, tt_)])
                        yield
                    for oc in range(0, 8, 2):
                        alive = [genOC(oc, tms[0], prods[0], 0), genOC(oc + 1, tms[1], prods[1], 1)]
                        while alive:
                            for gn_ in list(alive):
                                try:
                                    next(gn_)
                                except StopIteration:
                                    alive.remove(gn_)
                            yield
                    for tt in range(NTB):
                        for j in range(3):
                            sy.dma("sp", g.TMO[1 + j, tok0 + tt * 128:tok0 + (tt + 1) * 128, :], TMb[:, tt, j, :], r=[("TMb", tt, o_) for o_ in range(8)],
                                   w=[("dram", "tmo%d" % (1 + j), tok0 + tt * 128)])
                    gt0 = tok0 // 128
                    sy.op("dve", lambda e, gt0=gt0: e.tensor_copy(out=S_all[:, gt0:gt0 + NTB, :],
                                                                  in_=ps[SB_][:, 0:NTB * 16].rearrange("p (t h) -> p t h", h=16)),
                          r=[("ps", SB_)], w=[("S_all", blk)], banks=[SB_])
                    yield

            blocks = [(s_, b_) for s_ in range(NSEQ) for b_ in range(T // NB)]
            for _ in genPre(0, *blocks[0]):
                pass
            for i_ in range(len(blocks)):
                nxt = genPre(i_ + 1, *blocks[i_ + 1]) if i_ + 1 < len(blocks) else None
                interleave(genMain(i_, *blocks[i_]), nxt, ratio=3)
        sy.barrier()
        if STAGE >= 3:
            rwkv_p2a(g)
            sy.barrier()
            rwkv_p2b(g, GC, S_all)


def interleave(ga, gb, ratio=4):
    da = ga is None
    db = gb is None
    while not (da and db):
        if not da:
            for _ in range(ratio):
                try:
                    next(ga)
                except StopIteration:
                    da = True
                    break
        if not db:
            try:
                next(gb)
            except StopIteration:
                db = True


def rwkv_p2_old(g, GC, S_all, sgT):
    sy, nc, C, pf, I = g.sy, g.nc, g.C, g.pf, g.I
    T, NSEQ, NTOK = g.T, g.NSEQ, g.NTOK
    ps = g.ps
    with ExitStack() as es:
        sb = lambda n, s, d: g.sb(n, s, d, es)
        Wo = load_w(g, es, "Wo", I["a_w_o"], 8, D)
        g2 = sb("g2", [128, D], BF16); sy.dma("pool", g2[:], I["a_g2"][:, :], w=["g2"])
        lnxg = bcast_rows(g, es, "lnxg", PTI("lnx_g"))
        lnxb = bcast_rows(g, es, "lnxb", PTI("lnx_b"))
        LB = ln_bufs(g, es, (0, 0), "l00")
        XFt = [sb("XFt%d" % i, [128, 16, 4, 128], BF16) for i in range(2)]
        GC2 = sb("GC2", [128, 16, NTOK // 64], F32)
        for d_ in range(2):
            for h_ in range(2):
                sy.dma("sp", GC2[d_ * 64:(d_ + 1) * 64].rearrange("k (o h) c -> k o h c", h=2)[:, :, h_, :],
                       GC[h_ * 64:(h_ + 1) * 64, :, :], w=["GC2"])
        xfsrc = g.XF.rearrange("o (h k) f t -> k (o h) f t", h=2)
        TMt = [sb("TMt%d" % i, [128, 4, D], BF16) for i in range(2)]
        sgt = [sb("sgt%d" % i, [128, 128], BF16) for i in range(2)]
        AM1 = [[sb("AM1_%d_%d" % (b, p), [128, 2, 2, 128], BF16) for p in range(8)] for b in range(2)]
        AM2 = [[sb("AM2_%d_%d" % (b, p), [128, 2, 2, 128], BF16) for p in range(8)] for b in range(2)]
        TTo = [[sb("TTo%d_%d" % (b, p), [128, 2, 128], BF16) for p in range(8)] for b in range(2)]
        AW = [[sb("AW%d_%d" % (b, p), [128, 256], BF16) for p in range(8)] for b in range(2)]
        DD = [[sb("DD%d_%d" % (p, i), [128, 2, 2, 128], BF16) for i in range(2)] for p in range(8)]
        TT = [[sb("TT%d_%d" % (p, i), [128, 2, 128], BF16) for i in range(2)] for p in range(8)]
        U = sb("U", [128, 16, 64], BF16)
        Hf = sb("Hf", [128, 16, 64], F32)
        Hbs = [sb("Hb%d" % i, [128, 16, 64], BF16) for i in range(2)]
        xres = sb("xres", [128, D], F32)
        Ysb = sb("Ysb", [128, D], F32)
        bv = LB["xo"][0]
        yT = sb("yT", [128, 8, 128], BF16)
        z = sb("z", [128, D], F32)
        ysq = z
        sm = sb("gnsm", [128, 6, 16], F32)
        bkA = Banks([2, 3, 4, 5])
        bkB = Banks([6, 7])
        tiles = [(s_, tl_) for s_ in range(NSEQ) for tl_ in range(T // 128)]

        def loads(i):
            s_, tl = tiles[i]
            tk0 = s_ * T + tl * 128
            xf, tmt = XFt[i % 2], TMt[i % 2]
            kxf, ktm = ("XFt", i % 2), ("TMt", i % 2)
            for d_ in range(2):
                for q_ in range(16):
                    sy.dma("sp", xf[d_ * 64:(d_ + 1) * 64, q_, :, :], xfsrc[:, q_, :, tk0:tk0 + 128], w=[kxf])
            for j in range(4):
                sy.dma("sp", tmt[:, j, :], g.TMO[j, tk0:tk0 + 128, :], w=[ktm])
            sy.dma("sp", sgt[i % 2][:, :], g.SG[:, tk0:tk0 + 128], w=[("sgt", i % 2)])

        def phaseA(i):
            b = i % 2
            xf, tmt = XFt[b], TMt[b]
            kxf, ktm = ("XFt", b), ("TMt", b)
            bk = bkA
            for pb in (range(0, 4), range(4, 8)):
                bl = {}
                for p in pb:
                    b1, b2, b3 = bk.next(), bk.next(), bk.next()
                    bl[p] = (b1, b2, b3)
                    fns1, fns2, fns3 = [], [], []
                    for h in range(2):
                        hh = 2 * p + h
                        fns1.append(lambda e, h=h, hh=hh, b1=b1: e.matmul(
                            ps[b1][:, h * 256:(h + 1) * 256], lhsT=xf[0:64, hh, 0, :], rhs=xf[0:64, hh, 2:4, :], start=True, stop=True))
                        fns2.append(lambda e, h=h, hh=hh, b2=b2: e.matmul(
                            ps[b2][:, h * 256:(h + 1) * 256], lhsT=xf[0:64, hh, 1, :], rhs=xf[0:64, hh, 2:4, :], start=True, stop=True))
                        fns3.append(lambda e, h=h, hh=hh, b3=b3: e.matmul(
                            ps[b3][:, h * 128:(h + 1) * 128], lhsT=xf[0:64, hh, 2, :], rhs=xf[0:64, hh, 0, :], start=True, stop=True))
                    sy.op("pe", fns1, r=[kxf], w=[("ps", b1)], banks=[b1])
                    yield
                    sy.op("dve", lambda e, p=p, b1=b1: e.tensor_tensor(out=AM1[b][p][:].rearrange("p a b c -> p (a b c)"), in0=ps[b1][:, :],
                                                                       in1=C["mask12"][:, :], op=ALU.mult),
                          r=[("ps", b1), ("c", "mask12")], w=[("AM1", b, p)], banks=[b1])
                    yield
                    sy.op("pe", fns2, r=[kxf], w=[("ps", b2)], banks=[b2])
                    yield
                    sy.op("dve", lambda e, p=p, b2=b2: e.tensor_tensor(out=AM2[b][p][:].rearrange("p a b c -> p (a b c)"), in0=ps[b2][:, :],
                                                                       in1=C["mask12"][:, :], op=ALU.mult),
                          r=[("ps", b2), ("c", "mask12")], w=[("AM2", b, p)], banks=[b2])
                    yield
                    sy.op("pe", fns3, r=[kxf], w=[("ps", b3)], banks=[b3])
                    yield
                    sy.op("dve", lambda e, p=p, b3=b3: e.tensor_tensor(out=DD[p][0][:, :, 1, :], in0=ps[b3][:, 0:256].rearrange("p (h t) -> p h t", t=128),
                                                                       in1=C["mask3"][:, :].rearrange("p (h t) -> p h t", t=128), op=ALU.mult),
                          r=[("ps", b3), ("c", "mask3")], w=[("DDt", p, 0)], banks=[b3])
                    yield
                    sy.op("act", lambda e, p=p: e.activation(out=DD[p][0][:, :, 0, :], in_=AM1[b][p][:, :, 0, :], func=AF.Copy),
                          r=[("AM1", b, p)], w=[("DDn", p, 0)])
                    yield
                    sy.op("pool", lambda e, p=p: e.tensor_tensor(out=TT[p][0][:, :, :], in0=AM1[b][p][:, :, 0, :],
                                                                 in1=C["identb"][:, :].unsqueeze(1).to_broadcast([128, 2, 128]), op=ALU.add),
                          r=[("AM1", b, p), ("c", "identb")], w=[("TT", p, 0)])
                    yield
            for k in range(5):
                ci, ni = k % 2, (k + 1) % 2
                for pb in (range(0, 4), range(4, 8)):
                    bqs = {}
                    for p in pb:
                        bq = bk.next()
                        bqs[p] = bq
                        fns = []
                        for h in range(2):
                            if k < 4:
                                fns.append(lambda e, h=h, p=p, bq=bq: e.matmul(ps[bq][:, h * 256:h * 256 + 128], lhsT=DD[p][ci][:, h, 1, :],
                                                                               rhs=DD[p][ci][:, h, 0, :], start=True, stop=True))
                            fns.append(lambda e, h=h, p=p, bq=bq: e.matmul(ps[bq][:, h * 256 + 128:h * 256 + 256], lhsT=DD[p][ci][:, h, 0, :],
                                                                           rhs=DD[p][ci][:, h, 1, :], start=True, stop=True))
                        sy.op("pe", fns, r=[("DDn", p, ci), ("DDt", p, ci)], w=[("ps", bq)], banks=[bq])
                        yield
                    for p in pb:
                        bq = bqs[p]
                        if k < 4:
                            sy.op("act", lambda e, p=p, bq=bq: e.activation(out=DD[p][ni][:].rearrange("p a b c -> p (a b c)"),
                                                                            in_=ps[bq][:, :], func=AF.Copy),
                                  r=[("ps", bq)], w=[("DDn", p, ni), ("DDt", p, ni)], banks=[bq])
                        else:
                            sy.op("act", lambda e, p=p, bq=bq: e.activation(
                                out=DD[p][ni][:, :, 1, :], in_=ps[bq][:, :].rearrange("p (h x t) -> p h x t", h=2, x=2)[:, :, 1, :], func=AF.Copy),
                                r=[("ps", bq)], w=[("DDt", p, ni)], banks=[bq])
                        yield
                    bts = {}
                    for p in pb:
                        bt = bk.next()
                        bts[p] = bt
                        fns = []
                        for h in range(2):
                            fns.append(lambda e, h=h, p=p, bt=bt: e.matmul(ps[bt][:, h * 128:(h + 1) * 128], lhsT=DD[p][ni][:, h, 1, :],
                                                                           rhs=TT[p][ci][:, h, :], start=True, stop=True))
                        sy.op("pe", fns, r=[("TT", p, ci), ("DDt", p, ni)], w=[("ps", bt)], banks=[bt])
                        yield
                    for p in pb:
                        bt = bts[p]
                        dst = TTo[b][p] if k == 4 else TT[p][ni]
                        kd = ("TTo", b, p) if k == 4 else ("TT", p, ni)
                        sy.op("dve", lambda e, p=p, bt=bt, dst=dst: e.tensor_tensor(out=dst[:].rearrange("p h t -> p (h t)"), in0=ps[bt][:, 0:256],
                                                                                   in1=TT[p][ci][:].rearrange("p h t -> p (h t)"), op=ALU.add),
                              r=[("ps", bt), ("TT", p, ci)], w=[kd], banks=[bt])
                        yield
            for p in range(8):
                ba = bk.next()
                fns = []
                for h in range(2):
                    hh = 2 * p + h
                    fns.append(lambda e, h=h, p=p, ba=ba, hh=hh: e.matmul(ps[ba][:, h * 64:(h + 1) * 64], lhsT=AM2[b][p][:, h, 0, :],
                                                                          rhs=tmt[:, 0, hh * 64:(hh + 1) * 64], start=True, stop=True))
                    for c_ in range(2):
                        fns.append(lambda e, h=h, p=p, ba=ba, hh=hh, c_=c_: e.matmul(
                            ps[ba][c_ * 64:(c_ + 1) * 64, 128 + h * 64:128 + (h + 1) * 64], lhsT=tmt[:, 3, hh * 64:(hh + 1) * 64],
                            rhs=TTo[b][p][:, h, c_ * 64:(c_ + 1) * 64], start=True, stop=True))
                sy.op("pe", fns, r=[("AM2", b, p), ktm, ("TTo", b, p)], w=[("ps", ba)], banks=[ba])
                yield
                sy.op("act", lambda e, p=p, ba=ba: e.activation(out=AW[b][p][:, :], in_=ps[ba][:, 0:256], func=AF.Copy),
                      r=[("ps", ba)], w=[("AW", b, p)], banks=[ba])
                yield

        def phaseBC(i):
            s_, tl = tiles[i]
            tk0 = s_ * T + tl * 128
            b = i % 2
            gt = tk0 // 128
            xf, tmt = XFt[b], TMt[b]
            kxf, ktm = ("XFt", b), ("TMt", b)
            bk = bkB
            if tl == 0:
                sy.op("pool", lambda e: e.memset(Hf[:, :, :], 0.0), w=[("Hf", 0), ("Hf", 1)])
                sy.op("pool", lambda e: e.memset(Hbs[0][:, :, :], 0.0), w=[("Hb", 0)])
            kA = [("AW", b, q) for q in range(8)] + [("TTo", b, q) for q in range(8)]
            kAM = [("AM1", b, q) for q in range(8)] + [("AM2", b, q) for q in range(8)]
            for c in range(2):
                cr = slice(c * 64, (c + 1) * 64)
                ch = tk0 // 64 + c
                Hb, Hbn = Hbs[c], Hbs[1 - c]
                kHb, kHbn = ("Hb", c), ("Hb", 1 - c)
                ub = [bk.next(), bk.next()]
                for hb in range(2):
                    fns = []
                    for h8 in range(8):
                        hh = hb * 8 + h8
                        p, h = hh // 2, hh % 2
                        fns.append(lambda e, p=p, h=h, hh=hh, h8=h8, hb=hb: e.matmul(
                            ps[ub[hb]][cr, h8 * 64:(h8 + 1) * 64], lhsT=AW[b][p][cr, 128 + h * 64:128 + (h + 1) * 64], rhs=Hb[cr, hh, :],
                            start=True, stop=False))
                        fns.append(lambda e, p=p, h=h, h8=h8, hb=hb: e.matmul(
                            ps[ub[hb]][cr, h8 * 64:(h8 + 1) * 64], lhsT=TTo[b][p][cr, h, c * 64:(c + 1) * 64], rhs=AW[b][p][cr, h * 64:(h + 1) * 64],
                            start=False, stop=True))
                    sy.op("pe", fns, r=kA + [kHb], w=[("ps", ub[hb])], banks=[ub[hb]])
                    yield
                    sy.op("act", lambda e, hb=hb: e.activation(out=U[cr, hb * 8:(hb + 1) * 8, :].rearrange("p a b -> p (a b)"),
                                                               in_=ps[ub[hb]][cr, :], func=AF.Copy),
                          r=[("ps", ub[hb])], w=[("U", hb)], banks=[ub[hb]])
                    yield
                bhs = [bk.next(), bk.next()]
                sy.op("pool", lambda e, ch=ch: e.tensor_tensor(out=Hf[:, :, :], in0=Hf[:, :, :],
                                                               in1=GC2[:, :, ch:ch + 1].to_broadcast([128, 16, 64]), op=ALU.mult),
                      r=[("Hf", 0), ("Hf", 1), "GC2"], w=[("Hf", 0), ("Hf", 1)])
                yield
                for hb in range(2):
                    bh = bhs[hb]
                    fns = []
                    for h8 in range(8):
                        hh = hb * 8 + h8
                        p, h = hh // 2, hh % 2
                        for d_ in range(2):
                            ho = ps[bh][d_ * 64:(d_ + 1) * 64, h8 * 64:(h8 + 1) * 64]
                            fns.append(lambda e, ho=ho, hh=hh: e.matmul(ho, lhsT=tmt[cr, 2, hh * 64:(hh + 1) * 64], rhs=U[cr, hh, :],
                                                                        start=True, stop=False))
                            fns.append(lambda e, ho=ho, hh=hh: e.matmul(ho, lhsT=tmt[cr, 1, hh * 64:(hh + 1) * 64],
                                                                        rhs=tmt[cr, 0, hh * 64:(hh + 1) * 64], start=False, stop=True))
                    sy.op("pe", fns, r=[ktm, ("U", hb)], w=[("ps", bh)], banks=[bh])
                    yield
                    sy.op("dve", lambda e, hb=hb, bh=bh: e.tensor_tensor(out=Hf[:, hb * 8:(hb + 1) * 8, :].rearrange("p a b -> p (a b)"), in0=ps[bh][:, :],
                                                                         in1=Hf[:, hb * 8:(hb + 1) * 8, :].rearrange("p a b -> p (a b)"), op=ALU.add),
                          r=[("ps", bh), ("Hf", hb)], w=[("Hf", hb)], banks=[bh])
                    yield
                    sy.op("act", lambda e, hb=hb, Hbn=Hbn: e.activation(out=Hbn[:, hb * 8:(hb + 1) * 8, :].rearrange("p a b -> p (a b)"),
                                                                        in_=Hf[:, hb * 8:(hb + 1) * 8, :].rearrange("p a b -> p (a b)"), func=AF.Copy),
                          r=[("Hf", hb)], w=[kHbn])
                    yield
                for hb in range(2):
                    fns = []
                    for h8 in range(8):
                        hh = hb * 8 + h8
                        p, h = hh // 2, hh % 2
                        yo = ps[hb][cr, h8 * 64:(h8 + 1) * 64]
                        fns.append(lambda e, hh=hh, yo=yo: e.matmul(yo, lhsT=xf[cr, hh, 3, c * 64:(c + 1) * 64], rhs=Hb[cr, hh, :],
                                                                    start=True, stop=False))
                        fns.append(lambda e, p=p, h=h, yo=yo, hh=hh: e.matmul(yo, lhsT=AM1[b][p][cr, h, 1, c * 64:(c + 1) * 64], rhs=U[cr, hh, :],
                                                                              start=False, stop=False))
                        fns.append(lambda e, p=p, h=h, yo=yo, hh=hh: e.matmul(yo, lhsT=AM2[b][p][cr, h, 1, c * 64:(c + 1) * 64],
                                                                              rhs=tmt[cr, 0, hh * 64:(hh + 1) * 64], start=False, stop=True))
                    sy.op("pe", fns, r=[kxf, ktm, kHb, ("U", hb)] + kAM, w=[("ps", hb)], banks=[hb])
                    yield
            sy.dma("sp", xres[:, :], I["x"][tk0:tk0 + 128, :], w=["xres"])
            sy.op("pool", lambda e: e.tensor_tensor(out=bv[:, :].rearrange("p (h v) -> p h v", v=64),
                                                    in0=tmt[:, 0, :].rearrange("p (h v) -> p h v", v=64),
                                                    in1=S_all[:, gt, :].unsqueeze(2).to_broadcast([128, 16, 64]), op=ALU.mult),
                  r=[ktm, ("S_all",)], w=["bv", "l00xo"])
            yield
            sy.op("pool", lambda e: e.tensor_tensor(out=bv[:, :], in0=bv[:, :], in1=lnxb[:, :], op=ALU.add), r=["bv", "lnxb"], w=["bv", "l00xo"])
            yield
            for hb in range(2):
                sy.op("act", lambda e, hb=hb: e.activation(out=Ysb[:, hb * 512:(hb + 1) * 512], in_=ps[hb][:, :], func=AF.Copy),
                      r=[("ps", hb)], w=[("Ysb", hb)], banks=[hb])
                yield
            kY = [("Ysb", 0), ("Ysb", 1)]
            y3 = Ysb[:, :].rearrange("p (h v) -> p h v", v=64)
            sy.op("act", lambda e: e.activation(out=ysq[:, :], in_=Ysb[:, :], func=AF.Square), r=kY + ["z"], w=["ysq", "z"])
            yield
            sy.op("dve", lambda e: e.tensor_reduce(out=sm[:, 0, :], in_=y3, axis=AX.X, op=ALU.add), r=kY, w=["sm0"])
            yield
            sy.op("dve", lambda e: e.tensor_reduce(out=sm[:, 1, :], in_=ysq[:, :].rearrange("p (h v) -> p h v", v=64), axis=AX.X, op=ALU.add),
                  r=["ysq"], w=["sm1"])
            yield
            sy.op("dve", lambda e: e.tensor_scalar(out=sm[:, 2, :], in0=sm[:, 0, :], scalar1=1.0 / 64, scalar2=None, op0=ALU.mult),
                  r=["sm0"], w=["sm2"])
            sy.op("dve", lambda e: e.tensor_tensor(out=sm[:, 3, :], in0=sm[:, 2, :], in1=sm[:, 2, :], op=ALU.mult), r=["sm2"], w=["sm3"])
            sy.op("dve", lambda e: e.scalar_tensor_tensor(out=sm[:, 4, :], in0=sm[:, 1, :], scalar=1.0 / 64, in1=sm[:, 3, :],
                                                          op0=ALU.mult, op1=ALU.subtract), r=["sm1", "sm3"], w=["sm4"])
            sy.op("dve", lambda e: e.tensor_scalar(out=sm[:, 4, :], in0=sm[:, 4, :], scalar1=GN_EPS, scalar2=None, op0=ALU.add),
                  r=["sm4"], w=["sm4"])
            yield
            sy.op("act", lambda e: e.activation(out=sm[:, 3, :], in_=sm[:, 4, :], func=AF.Sqrt), r=["sm4", "sm3"], w=["sm3"])
            sy.op("dve", lambda e: e.reciprocal(out=sm[:, 5, :], in_=sm[:, 3, :]), r=["sm3"], w=["sm5"])
            yield
            sy.op("dve", lambda e: e.tensor_tensor(out=y3, in0=y3, in1=sm[:, 2, :].unsqueeze(2).to_broadcast([128, 16, 64]), op=ALU.subtract),
                  r=kY + ["sm2", "ysq"], w=kY)
            yield
            sy.op("dve", lambda e: e.tensor_tensor(out=y3, in0=y3, in1=sm[:, 5, :].unsqueeze(2).to_broadcast([128, 16, 64]), op=ALU.mult),
                  r=kY + ["sm5"], w=kY)
            yield
            sy.op("dve", lambda e: e.tensor_tensor(out=Ysb[:, :], in0=Ysb[:, :], in1=lnxg[:, :], op=ALU.mult), r=kY + ["lnxg"], w=kY)
            yield
            sy.op("dve", lambda e: e.tensor_tensor(out=Ysb[:, :], in0=Ysb[:, :], in1=bv[:, :], op=ALU.add), r=kY + ["bv", "l00xo"], w=kY)
            yield
            for hb in range(2):
                bg = bk.next()
                sy.op("pe", lambda e, hb=hb, bg=bg: e.matmul(ps[bg][:, :], lhsT=sgt[b][:, :], rhs=g2[:, hb * 512:(hb + 1) * 512],
                                                             start=True, stop=True), r=[("sgt", b), "g2"], w=[("ps", bg)], banks=[bg])
                yield
                sy.op("dve", lambda e, hb=hb, bg=bg: e.tensor_tensor(out=Ysb[:, hb * 512:(hb + 1) * 512], in0=ps[bg][:, :],
                                                                     in1=Ysb[:, hb * 512:(hb + 1) * 512], op=ALU.mult),
                      r=[("ps", bg)] + kY, w=kY, banks=[bg])
                yield
            for half in range(2):
                bb = bk.next()
                sy.op("pe", [lambda e, j=j, bb=bb, half=half: e.transpose(out=ps[bb][:, j * 128:(j + 1) * 128],
                                                                          in_=Ysb[:, (half * 4 + j) * 128:(half * 4 + j + 1) * 128],
                                                                          identity=C["identf"][:, :]) for j in range(4)],
                      r=kY + [("c", "identf")], w=[("ps", bb)], banks=[bb])
                yield
                sy.op("act", lambda e, bb=bb, half=half: e.activation(out=yT[:, half * 4:(half + 1) * 4, :],
                                                                      in_=ps[bb][:, :].rearrange("p (j t) -> p j t", t=128), func=AF.Copy),
                      r=[("ps", bb)], w=["yT"], banks=[bb])
                yield
            for hb in range(2):
                bo = bk.next()
                sy.op("pe", [lambda e, kc=kc, bo=bo, hb=hb: e.matmul(ps[bo][:, :], lhsT=yT[:, kc, :], rhs=Wo[:, kc, hb * 512:(hb + 1) * 512],
                                                                     start=(kc == 0), stop=(kc == 7)) for kc in range(8)],
                      r=["yT", "Wo"], w=[("ps", bo)], banks=[bo])
                yield
                if hasattr(g, "dbg"):
                    sy.op("act", lambda e, bo=bo, hb=hb: e.activation(out=bv[:, hb * 512:(hb + 1) * 512], in_=ps[bo][:, :], func=AF.Copy),
                          r=[("ps", bo)], w=["bv"], banks=[bo])
                sy.op("dve", lambda e, bo=bo, hb=hb: e.scalar_tensor_tensor(
                    out=z[:, hb * 512:(hb + 1) * 512], in0=xres[:, hb * 512:(hb + 1) * 512], scalar=ALPHA, in1=ps[bo][:, :],
                    op0=ALU.mult, op1=ALU.add), r=[("ps", bo), "xres", "ysq"], w=["z"], banks=[bo])
                yield
            if hasattr(g, "dbg"):
                sy.dma("sp", g.dbg["h0"][tk0:tk0 + 128, :], bv[:, :], r=["bv"], w=[("dram", "dbgh0", tk0)])
            resid_ln(g, LB, z, "z", (0, 0), g.X1, g.X1T, tk0, bk, "l00")
            yield

        loads(0)
        for _ in phaseA(0):
            pass
        for i in range(len(tiles)):
            if i + 1 < len(tiles):
                loads(i + 1)
            interleave(phaseA(i + 1) if i + 1 < len(tiles) else None, phaseBC(i), ratio=3)


def rwkv_p2a(g):
    sy, nc, C, pf, I = g.sy, g.nc, g.C, g.pf, g.I
    T, NSEQ, NTOK = g.T, g.NSEQ, g.NTOK
    ps = g.ps
    with ExitStack() as es:
        sb = lambda n, s, d: g.sb(n, s, d, es)
        XFt = [sb("XFa%d" % i, [64, 16, 4, 128], BF16) for i in range(4)]
        TMt = [sb("TMa%d" % i, [128, 2, D], BF16) for i in range(4)]
        PK = [[sb("PK%d_%d" % (b, p), [128, 2048], BF16) for p in range(8)] for b in range(2)]
        AM1 = [[PK[b][p][:, :].rearrange("q (x r) -> q x r", x=2)[:, :, 0:256].rearrange("q x (h t) -> q h x t", h=2) for p in range(8)] for b in range(2)]
        AM2 = [[PK[b][p][:, :].rearrange("q (x r) -> q x r", x=2)[:, :, 256:512].rearrange("q x (h t) -> q h x t", h=2) for p in range(8)] for b in range(2)]
        TTo = [[PK[b][p][:, 1536:1792].rearrange("q (h t) -> q h t", h=2) for p in range(8)] for b in range(2)]
        AW = [[PK[b][p][:, 1792:2048] for p in range(8)] for b in range(2)]
        DDs = [[[sb("DD%d_%d_%d" % (b, p, i), [128, 2, 2, 128], BF16) for i in range(2)] for p in range(8)] for b in range(2)]
        TTs = [[[sb("TT%d_%d_%d" % (b, p, i), [128, 2, 128], BF16) for i in range(2)] for p in range(8)] for b in range(2)]
        ntile = NTOK // 128

        def loads(i):
            tk0 = i * 128
            xf, tmt = XFt[i % 4], TMt[i % 4]
            sy.dma("sp", xf[:, :, :, :], g.XF[i], w=[("XFt", i % 4)])
            sy.dma("sp", tmt[:, 0, :], g.TMO[0, tk0:tk0 + 128, :], w=[("TMt", i % 4)])
            sy.dma("sp", tmt[:, 1, :], g.TMO[3, tk0:tk0 + 128, :], w=[("TMt", i % 4)])

        def phaseA(i, bk):
            b = i % 2
            xf, tmt = XFt[i % 4], TMt[i % 4]
            kxf, ktm = ("XFt", i % 4), ("TMt", i % 4)
            DD, TT = DDs[b], TTs[b]
            for pb in (range(0, 4), range(4, 8)):
                bl = {}
                for p in pb:
                    b1, b2, b3 = bk.next(), bk.next(), bk.next()
                    bl[p] = (b1, b2, b3)
                    fns1, fns2, fns3 = [], [], []
                    for h in range(2):
                        hh = 2 * p + h
                        fns1.append(lambda e, h=h, hh=hh, b1=b1: e.matmul(
                            ps[b1][:, h * 256:(h + 1) * 256], lhsT=xf[0:64, hh, 0, :], rhs=xf[0:64, hh, 2:4, :], start=True, stop=True))
                        fns2.append(lambda e, h=h, hh=hh, b2=b2: e.matmul(
                            ps[b2][:, h * 256:(h + 1) * 256], lhsT=xf[0:64, hh, 1, :], rhs=xf[0:64, hh, 2:4, :], start=True, stop=True))
                        fns3.append(lambda e, h=h, hh=hh, b3=b3: e.matmul(
                            ps[b3][:, h * 128:(h + 1) * 128], lhsT=xf[0:64, hh, 2, :], rhs=xf[0:64, hh, 0, :], start=True, stop=True))
                    sy.op("pe", fns1, r=[kxf], w=[("ps", b1)], banks=[b1])
                    yield
                    sy.op("dve", lambda e, p=p, b1=b1: e.tensor_tensor(out=AM1[b][p], in0=ps[b1][:, :].rearrange("q (h x t) -> q h x t", h=2, x=2),
                                                                       in1=C["mask12"][:, :].rearrange("q (h x t) -> q h x t", h=2, x=2), op=ALU.mult),
                          r=[("ps", b1), ("c", "mask12")], w=[("AM1", b, p)], banks=[b1])
                    yield
                    sy.op("pe", fns2, r=[kxf], w=[("ps", b2)], banks=[b2])
                    yield
                    sy.op("dve", lambda e, p=p, b2=b2: e.tensor_tensor(out=AM2[b][p], in0=ps[b2][:, :].rearrange("q (h x t) -> q h x t", h=2, x=2),
                                                                       in1=C["mask12"][:, :].rearrange("q (h x t) -> q h x t", h=2, x=2), op=ALU.mult),
                          r=[("ps", b2), ("c", "mask12")], w=[("AM2", b, p)], banks=[b2])
                    yield
                    sy.op("pe", fns3, r=[kxf], w=[("ps", b3)], banks=[b3])
                    yield
                    sy.op("dve", lambda e, p=p, b3=b3: e.tensor_tensor(out=DD[p][0][:, :, 1, :], in0=ps[b3][:, 0:256].rearrange("p (h t) -> p h t", t=128),
                                                                       in1=C["mask3"][:, :].rearrange("p (h t) -> p h t", t=128), op=ALU.mult),
                          r=[("ps", b3), ("c", "mask3")], w=[("DDt", b, p, 0)], banks=[b3])
                    yield
                    sy.op("act", lambda e, p=p: e.activation(out=DD[p][0][:, :, 0, :], in_=AM1[b][p][:, :, 0, :], func=AF.Copy),
                          r=[("AM1", b, p)], w=[("DDn", b, p, 0)])
                    yield
                    sy.op("pool", lambda e, p=p: e.tensor_tensor(out=TT[p][0][:, :, :], in0=AM1[b][p][:, :, 0, :],
                                                                 in1=C["identb"][:, :].unsqueeze(1).to_broadcast([128, 2, 128]), op=ALU.add),
                          r=[("AM1", b, p), ("c", "identb")], w=[("TT", b, p, 0)])
                    yield
            for k in range(5):
                ci, ni = k % 2, (k + 1) % 2
                for pb in (range(0, 4), range(4, 8)):
                    bqs = {}
                    for p in pb:
                        bq = bk.next()
                        bqs[p] = bq
                        fns = []
                        for h in range(2):
                            if k < 4:
                                fns.append(lambda e, h=h, p=p, bq=bq: e.matmul(ps[bq][:, h * 256:h * 256 + 128], lhsT=DD[p][ci][:, h, 1, :],
                                                                               rhs=DD[p][ci][:, h, 0, :], start=True, stop=True))
                            fns.append(lambda e, h=h, p=p, bq=bq: e.matmul(ps[bq][:, h * 256 + 128:h * 256 + 256], lhsT=DD[p][ci][:, h, 0, :],
                                                                           rhs=DD[p][ci][:, h, 1, :], start=True, stop=True))
                        sy.op("pe", fns, r=[("DDn", b, p, ci), ("DDt", b, p, ci)], w=[("ps", bq)], banks=[bq])
                        yield
                    for p in pb:
                        bq = bqs[p]
                        if k < 4:
                            sy.op("act", lambda e, p=p, bq=bq: e.activation(out=DD[p][ni][:].rearrange("p a b c -> p (a b c)"),
                                                                            in_=ps[bq][:, :], func=AF.Copy),
                                  r=[("ps", bq)], w=[("DDn", b, p, ni), ("DDt", b, p, ni)], banks=[bq])
                        else:
                            sy.op("act", lambda e, p=p, bq=bq: e.activation(
                                out=DD[p][ni][:, :, 1, :], in_=ps[bq][:, :].rearrange("p (h x t) -> p h x t", h=2, x=2)[:, :, 1, :], func=AF.Copy),
                                r=[("ps", bq)], w=[("DDt", b, p, ni)], banks=[bq])
                        yield
                    bts = {}
                    for p in pb:
                        bt = bk.next()
                        bts[p] = bt
                        fns = []
                        for h in range(2):
                            fns.append(lambda e, h=h, p=p, bt=bt: e.matmul(ps[bt][:, h * 128:(h + 1) * 128], lhsT=DD[p][ni][:, h, 1, :],
                                                                           rhs=TT[p][ci][:, h, :], start=True, stop=True))
                        sy.op("pe", fns, r=[("TT", b, p, ci), ("DDt", b, p, ni)], w=[("ps", bt)], banks=[bt])
                        yield
                    for p in pb:
                        bt = bts[p]
                        dst = None if k == 4 else TT[p][ni]
                        kd = ("TTo", b, p) if k == 4 else ("TT", b, p, ni)
                        sy.op("dve", lambda e, p=p, bt=bt, dst=dst: e.tensor_tensor(out=(PK[b][p][:, 1536:1792] if dst is None else dst[:].rearrange("p h t -> p (h t)")), in0=ps[bt][:, 0:256],
                                                                                   in1=TT[p][ci][:].rearrange("p h t -> p (h t)"), op=ALU.add),
                              r=[("ps", bt), ("TT", b, p, ci)], w=[kd], banks=[bt])
                        yield
            for p in range(8):
                ba = bk.next()
                fns = []
                for h in range(2):
                    hh = 2 * p + h
                    fns.append(lambda e, h=h, p=p, ba=ba, hh=hh: e.matmul(ps[ba][:, h * 64:(h + 1) * 64], lhsT=AM2[b][p][:, h, 0, :],
                                                                          rhs=tmt[:, 0, hh * 64:(hh + 1) * 64], start=True, stop=True))
                    for c_ in range(2):
                        fns.append(lambda e, h=h, p=p, ba=ba, hh=hh, c_=c_: e.matmul(
                            ps[ba][c_ * 64:(c_ + 1) * 64, 128 + h * 64:128 + (h + 1) * 64], lhsT=tmt[:, 1, hh * 64:(hh + 1) * 64],
                            rhs=TTo[b][p][:, h, c_ * 64:(c_ + 1) * 64], start=True, stop=True))
                sy.op("pe", fns, r=[("AM2", b, p), ktm, ("TTo", b, p)], w=[("ps", ba)], banks=[ba])
                yield
                sy.op("act", lambda e, p=p, ba=ba: e.activation(out=AW[b][p], in_=ps[ba][:, 0:256], func=AF.Copy),
                      r=[("ps", ba)], w=[("AW", b, p)], banks=[ba])
                yield
                sy.dma("sp", g.AO[i, p][:, :], PK[b][p][:, 1024:2048], r=[("AM1", b, p), ("AM2", b, p), ("TTo", b, p), ("AW", b, p)],
                       w=[("dram", "ao", i, p)])
                yield


        loads(0)
        loads(1)
        for i in range(0, ntile, 2):
            if i + 2 < ntile:
                loads(i + 2)
                loads(i + 3)
            ga, gb = phaseA(i, Banks([0, 1, 2, 3])), phaseA(i + 1, Banks([4, 5, 6, 7]))
            interleave(ga, gb, ratio=1)


def rwkv_p2b(g, GC, S_all):
    sy, nc, C, pf, I = g.sy, g.nc, g.C, g.pf, g.I
    T, NSEQ, NTOK = g.T, g.NSEQ, g.NTOK
    ps = g.ps
    with ExitStack() as es:
        sb = lambda n, s, d: g.sb(n, s, d, es)
        Wo = load_w(g, es, "Wo", I["a_w_o"], 8, D)
        g2 = sb("g2", [128, D], BF16); sy.dma("pool", g2[:], I["a_g2"][:, :], w=["g2"])
        lnxg = bcast_rows(g, es, "lnxg", PTI("lnx_g"))
        lnxb = bcast_rows(g, es, "lnxb", PTI("lnx_b"))
        LB = ln_bufs(g, es, (0, 0), "l00", nbuf=2)
        GC2 = sb("GC2", [128, 16, NTOK // 64], F32)
        for d_ in range(2):
            for h_ in range(2):
                sy.dma("sp", GC2[d_ * 64:(d_ + 1) * 64].rearrange("k (o h) c -> k o h c", h=2)[:, :, h_, :],
                       GC[h_ * 64:(h_ + 1) * 64, :, :], w=["GC2"])
        XR = [sb("XR%d" % i, [128, 16, 128], BF16) for i in range(2)]
        TMt = [sb("TMb%d" % i, [128, 3, D], BF16) for i in range(3)]
        AOt = [sb("AOt%d" % i, [128, 8, 1024], BF16) for i in range(2)]
        sgt = [sb("sgt%d" % i, [128, 128], BF16) for i in range(3)]
        U = sb("U", [128, 16, 64], BF16)
        Hf = sb("Hf", [128, 16, 64], F32)
        Hbs = [sb("Hb%d" % i, [128, 16, 64], BF16) for i in range(2)]
        xres = sb("xres", [128, D], F32)
        Ysbs = [sb("Ysb%d" % i, [128, D], F32) for i in range(2)]
        bvs = [sb("bv%d" % i, [128, D], F32) for i in range(2)]
        yT = sb("yT", [128, 8, 128], BF16)
        z = sb("z", [128, D], F32)
        ysq = sb("ysq", [128, D], F32)
        sm = sb("gnsm", [128, 6, 16], F32)
        bkBm = Banks([2, 3, 4, 5])
        bkC1 = Banks([6])
        bkC2 = Banks([7])
        tiles = [(s_, tl_) for s_ in range(NSEQ) for tl_ in range(T // 128)]

        def loads(i):
            s_, tl = tiles[i]
            tk0 = s_ * T + tl * 128
            b = i % 2
            for d_ in range(2):
                sy.dma("sp", XR[b][d_ * 64:(d_ + 1) * 64, :, :], g.XF[tk0 // 128][:, :, 3, :], w=[("XR", b)])
            b3 = i % 3
            for j in range(3):
                sy.dma("sp", TMt[b3][:, j, :], g.TMO[j, tk0:tk0 + 128, :], w=[("TMt", b3)])
            for q_ in range(2):
                sy.dma("sp", AOt[b][:, q_ * 4:(q_ + 1) * 4, :], g.AO[tk0 // 128, q_ * 4:(q_ + 1) * 4].rearrange("p q c -> q p c"), w=[("AOt", b)])
            sy.dma("sp", sgt[b3][:, :], g.SG[:, tk0:tk0 + 128], w=[("sgt", b3)])

        def genB(i):
            s_, tl = tiles[i]
            tk0 = s_ * T + tl * 128
            b = i % 2
            xr, tmt, ao = XR[b], TMt[i % 3], AOt[b]
            kxf, ktm = ("XR", b), ("TMt", i % 3)
            bk = bkBm
            if tl == 0:
                sy.op("pool", lambda e: e.memset(Hf[:, :, :], 0.0), w=[("Hf", 0), ("Hf", 1)])
                sy.op("pool", lambda e: e.memset(Hbs[0][:, :, :], 0.0), w=[("Hb", 0)])
            kA = [("AOt", b)]
            kAM = [("AOt", b)]
            for c in range(2):
                cr = slice(c * 64, (c + 1) * 64)
                ch = tk0 // 64 + c
                Hb, Hbn = Hbs[c], Hbs[1 - c]
                kHb, kHbn = ("Hb", c), ("Hb", 1 - c)
                ub = [bk.next(), bk.next()]
                for hb in range(2):
                    fns = []
                    for h8 in range(8):
                        hh = hb * 8 + h8
                        p, h = hh // 2, hh % 2
                        fns.append(lambda e, p=p, h=h, hh=hh, h8=h8, hb=hb: e.matmul(
                            ps[ub[hb]][cr, h8 * 64:(h8 + 1) * 64], lhsT=ao[cr, p, 896 + h * 64:896 + (h + 1) * 64], rhs=Hb[cr, hh, :],
                            start=True, stop=False))
                        fns.append(lambda e, p=p, h=h, h8=h8, hb=hb: e.matmul(
                            ps[ub[hb]][cr, h8 * 64:(h8 + 1) * 64], lhsT=ao[cr, p, 512 + h * 128 + c * 64:512 + h * 128 + (c + 1) * 64], rhs=ao[cr, p, 768 + h * 64:768 + (h + 1) * 64],
                            start=False, stop=True))
                    sy.op("pe", fns, r=kA + [kHb], w=[("ps", ub[hb])], banks=[ub[hb]])
                    yield
                    sy.op("act", lambda e, hb=hb: e.activation(out=U[cr, hb * 8:(hb + 1) * 8, :].rearrange("p a b -> p (a b)"),
                                                               in_=ps[ub[hb]][cr, :], func=AF.Copy),
                          r=[("ps", ub[hb])], w=[("U", hb)], banks=[ub[hb]])
                    yield
                bhs = [bk.next(), bk.next()]
                sy.op("pool", lambda e, ch=ch: e.tensor_tensor(out=Hf[:, :, :], in0=Hf[:, :, :],
                                                               in1=GC2[:, :, ch:ch + 1].to_broadcast([128, 16, 64]), op=ALU.mult),
                      r=[("Hf", 0), ("Hf", 1), "GC2"], w=[("Hf", 0), ("Hf", 1)])
                yield
                for hb in range(2):
                    bh = bhs[hb]
                    fns = []
                    for h8 in range(8):
                        hh = hb * 8 + h8
                        p, h = hh // 2, hh % 2
                        for d_ in range(2):
                            ho = ps[bh][d_ * 64:(d_ + 1) * 64, h8 * 64:(h8 + 1) * 64]
                            fns.append(lambda e, ho=ho, hh=hh: e.matmul(ho, lhsT=tmt[cr, 2, hh * 64:(hh + 1) * 64], rhs=U[cr, hh, :],
                                                                        start=True, stop=False))
                            fns.append(lambda e, ho=ho, hh=hh: e.matmul(ho, lhsT=tmt[cr, 1, hh * 64:(hh + 1) * 64],
                                                                        rhs=tmt[cr, 0, hh * 64:(hh + 1) * 64], start=False, stop=True))
                    sy.op("pe", fns, r=[ktm, ("U", hb)], w=[("ps", bh)], banks=[bh])
                    yield
                    sy.op("dve", lambda e, hb=hb, bh=bh: e.tensor_tensor(out=Hf[:, hb * 8:(hb + 1) * 8, :].rearrange("p a b -> p (a b)"), in0=ps[bh][:, :],
                                                                         in1=Hf[:, hb * 8:(hb + 1) * 8, :].rearrange("p a b -> p (a b)"), op=ALU.add),
                          r=[("ps", bh), ("Hf", hb)], w=[("Hf", hb)], banks=[bh])
                    yield
                    sy.op("act", lambda e, hb=hb, Hbn=Hbn: e.activation(out=Hbn[:, hb * 8:(hb + 1) * 8, :].rearrange("p a b -> p (a b)"),
                                                                        in_=Hf[:, hb * 8:(hb + 1) * 8, :].rearrange("p a b -> p (a b)"), func=AF.Copy),
                          r=[("Hf", hb)], w=[kHbn])
                    yield
                for hb in range(2):
                    fns = []
                    for h8 in range(8):
                        hh = hb * 8 + h8
                        p, h = hh // 2, hh % 2
                        yo = ps[hb][cr, h8 * 64:(h8 + 1) * 64]
                        fns.append(lambda e, hh=hh, yo=yo: e.matmul(yo, lhsT=xr[cr, hh, c * 64:(c + 1) * 64], rhs=Hb[cr, hh, :],
                                                                    start=True, stop=False))
                        fns.append(lambda e, p=p, h=h, yo=yo, hh=hh: e.matmul(yo, lhsT=ao[cr, p, h * 128 + c * 64:h * 128 + (c + 1) * 64], rhs=U[cr, hh, :],
                                                                              start=False, stop=False))
                        fns.append(lambda e, p=p, h=h, yo=yo, hh=hh: e.matmul(yo, lhsT=ao[cr, p, 256 + h * 128 + c * 64:256 + h * 128 + (c + 1) * 64],
                                                                              rhs=tmt[cr, 0, hh * 64:(hh + 1) * 64], start=False, stop=True))
                    sy.op("pe", fns, r=[kxf, ktm, kHb, ("U", hb)] + kAM, w=[("ps", hb)], banks=[hb])
                    yield

        def genC1(i):
            s_, tl = tiles[i]
            tk0 = s_ * T + tl * 128
            b = i % 2
            gt = tk0 // 128
            tmt = TMt[i % 3]
            ktm = ("TMt", i % 3)
            Ysb_, bv = Ysbs[b], bvs[b]
            bk = bkC1
            sy.op("pool", lambda e: e.tensor_tensor(out=bv[:, :].rearrange("p (h v) -> p h v", v=64),
                                                    in0=tmt[:, 0, :].rearrange("p (h v) -> p h v", v=64),
                                                    in1=S_all[:, gt, :].unsqueeze(2).to_broadcast([128, 16, 64]), op=ALU.mult),
                  r=[ktm, ("S_all",)], w=[("bv", b)])
            yield
            sy.op("pool", lambda e: e.tensor_tensor(out=bv[:, :], in0=bv[:, :], in1=lnxb[:, :], op=ALU.add), r=[("bv", b), "lnxb"], w=[("bv", b)])
            yield
            for hb in range(2):
                sy.op("act", lambda e, hb=hb: e.activation(out=Ysb_[:, hb * 512:(hb + 1) * 512], in_=ps[hb][:, :], func=AF.Copy),
                      r=[("ps", hb)], w=[("Ysb", b, hb)], banks=[hb])
                yield
            kY = [("Ysb", b, 0), ("Ysb", b, 1)]
            y3 = Ysb_[:, :].rearrange("p (h v) -> p h v", v=64)
            sy.op("act", lambda e: e.activation(out=ysq[:, :], in_=Ysb_[:, :], func=AF.Square), r=kY, w=["ysq"])
            yield
            sy.op("dve", lambda e: e.tensor_reduce(out=sm[:, 0, :], in_=y3, axis=AX.X, op=ALU.add), r=kY, w=["sm0"])
            yield
            sy.op("dve", lambda e: e.tensor_reduce(out=sm[:, 1, :], in_=ysq[:, :].rearrange("p (h v) -> p h v", v=64), axis=AX.X, op=ALU.add),
                  r=["ysq"], w=["sm1"])
            yield
            sy.op("dve", lambda e: e.tensor_scalar(out=sm[:, 2, :], in0=sm[:, 0, :], scalar1=1.0 / 64, scalar2=None, op0=ALU.mult),
                  r=["sm0"], w=["sm2"])
            sy.op("dve", lambda e: e.tensor_tensor(out=sm[:, 3, :], in0=sm[:, 2, :], in1=sm[:, 2, :], op=ALU.mult), r=["sm2"], w=["sm3"])
            sy.op("dve", lambda e: e.scalar_tensor_tensor(out=sm[:, 4, :], in0=sm[:, 1, :], scalar=1.0 / 64, in1=sm[:, 3, :],
                                                          op0=ALU.mult, op1=ALU.subtract), r=["sm1", "sm3"], w=["sm4"])
            sy.op("dve", lambda e: e.tensor_scalar(out=sm[:, 4, :], in0=sm[:, 4, :], scalar1=GN_EPS, scalar2=None, op0=ALU.add),
                  r=["sm4"], w=["sm4"])
            yield
            sy.op("act", lambda e: e.activation(out=sm[:, 3, :], in_=sm[:, 4, :], func=AF.Sqrt), r=["sm4", "sm3"], w=["sm3"])
            sy.op("dve", lambda e: e.reciprocal(out=sm[:, 5, :], in_=sm[:, 3, :]), r=["sm3"], w=["sm5"])
            yield
            sy.op("dve", lambda e: e.tensor_tensor(out=y3, in0=y3, in1=sm[:, 2, :].unsqueeze(2).to_broadcast([128, 16, 64]), op=ALU.subtract),
                  r=kY + ["sm2", "ysq"], w=kY)
            yield
            sy.op("dve", lambda e: e.tensor_tensor(out=y3, in0=y3, in1=sm[:, 5, :].unsqueeze(2).to_broadcast([128, 16, 64]), op=ALU.mult),
                  r=kY + ["sm5"], w=kY)
            yield
            sy.op("dve", lambda e: e.tensor_tensor(out=Ysb_[:, :], in0=Ysb_[:, :], in1=lnxg[:, :], op=ALU.mult), r=kY + ["lnxg"], w=kY)
            yield
            sy.op("dve", lambda e: e.tensor_tensor(out=Ysb_[:, :], in0=Ysb_[:, :], in1=bv[:, :], op=ALU.add), r=kY + [("bv", b)], w=kY)
            yield
            for hb in range(2):
                bg = bk.next()
                sy.op("pe", lambda e, hb=hb, bg=bg: e.matmul(ps[bg][:, :], lhsT=sgt[i % 3][:, :], rhs=g2[:, hb * 512:(hb + 1) * 512],
                                                             start=True, stop=True), r=[("sgt", i % 3), "g2"], w=[("ps", bg)], banks=[bg])
                yield
                sy.op("dve", lambda e, hb=hb, bg=bg: e.tensor_tensor(out=Ysb_[:, hb * 512:(hb + 1) * 512], in0=ps[bg][:, :],
                                                                     in1=Ysb_[:, hb * 512:(hb + 1) * 512], op=ALU.mult),
                      r=[("ps", bg)] + kY, w=kY, banks=[bg])
                yield

        def genC2(i):
            s_, tl = tiles[i]
            tk0 = s_ * T + tl * 128
            b = i % 2
            gt = tk0 // 128
            tmt = TMt[i % 3]
            ktm = ("TMt", i % 3)
            Ysb_, bv = Ysbs[b], bvs[b]
            bk = bkC2
            kY = [("Ysb", b, 0), ("Ysb", b, 1)]
            sy.dma("sp", xres[:, :], I["x"][tk0:tk0 + 128, :], w=["xres"])
            for half in range(2):
                bb = bk.next()
                sy.op("pe", [lambda e, j=j, bb=bb, half=half: e.transpose(out=ps[bb][:, j * 128:(j + 1) * 128],
                                                                          in_=Ysb_[:, (half * 4 + j) * 128:(half * 4 + j + 1) * 128],
                                                                          identity=C["identf"][:, :]) for j in range(4)],
                      r=kY + [("c", "identf")], w=[("ps", bb)], banks=[bb])
                yield
                sy.op("act", lambda e, bb=bb, half=half: e.activation(out=yT[:, half * 4:(half + 1) * 4, :],
                                                                      in_=ps[bb][:, :].rearrange("p (j t) -> p j t", t=128), func=AF.Copy),
                      r=[("ps", bb)], w=["yT"], banks=[bb])
                yield
            for hb in range(2):
                bo = bk.next()
                sy.op("pe", [lambda e, kc=kc, bo=bo, hb=hb: e.matmul(ps[bo][:, :], lhsT=yT[:, kc, :], rhs=Wo[:, kc, hb * 512:(hb + 1) * 512],
                                                                     start=(kc == 0), stop=(kc == 7)) for kc in range(8)],
                      r=["yT", "Wo"], w=[("ps", bo)], banks=[bo])
                yield
                if hasattr(g, "dbg"):
                    sy.op("act", lambda e, bo=bo, hb=hb: e.activation(out=bv[:, hb * 512:(hb + 1) * 512], in_=ps[bo][:, :], func=AF.Copy),
                          r=[("ps", bo)], w=["bv"], banks=[bo])
                sy.op("dve", lambda e, bo=bo, hb=hb: e.scalar_tensor_tensor(
                    out=z[:, hb * 512:(hb + 1) * 512], in0=xres[:, hb * 512:(hb + 1) * 512], scalar=ALPHA, in1=ps[bo][:, :],
                    op0=ALU.mult, op1=ALU.add), r=[("ps", bo), "xres"], w=["z"], banks=[bo])
                yield
            if hasattr(g, "dbg"):
                sy.dma("sp", g.dbg["h0"][tk0:tk0 + 128, :], bv[:, :], r=["bv"], w=[("dram", "dbgh0", tk0)])
            resid_ln(g, LB, z, "z", (0, 0), g.X1, g.X1T, tk0, bk, "l00")
            yield


        def rr(gens, hook=None, hook_at=6):
            gens = [x for x in gens if x is not None]
            k = 0
            while gens:
                if hook is not None and k == hook_at:
                    hook()
                    hook = None
                k += 1
                for x in list(gens):
                    try:
                        next(x)
                    except StopIteration:
                        gens.remove(x)
            if hook is not None:
                hook()

        n = len(tiles)
        loads(0)
        loads(1)
        rr([genB(0)])
        for i in range(n + 1):
            hk = (lambda i=i: loads(i + 2)) if i + 2 < n else None
            rr([genB(i + 1) if i + 1 < n else None, genC1(i) if i < n else None, genC2(i - 1) if i >= 1 else None], hook=hk)


def mlp_layer(g, l, Xin, XinT, Xout, XoutT, final):
    sy, nc, C, pf, I = g.sy, g.nc, g.C, g.pf, g.I
    T, NSEQ = g.T, g.NSEQ
    ps = g.ps
    NT = T // 128
    NTB = T // 512
    with ExitStack() as es:
        sb = lambda n, s, d: g.sb(n, s, d, es)
        acc = sb("acc", [128, NT, D], F32)
        xT = sb("mxT", [128, 8, T], BF16)
        W1g = [sb("W1g%d" % i, [128, 8, 512], BF16) for i in range(2)]
        W2g = [sb("W2g%d" % i, [128, 4, D], BF16) for i in range(2)]
        hT = [sb("hT%d" % i, [128, 4, 512], BF16) for i in range(2)]
        rl = [sb("rl%d" % i, [128, 512], F32) for i in range(2)]
        tag = "l%d1" % l
        LB = ln_bufs(g, es, (l, 1), tag, nbuf=2)
        bkH = Banks([0, 1, 2, 3])
        bkS = Banks([4, 5, 6, 7])
        w1src = I["mlp_w1"][l].rearrange("(kc p) n -> p kc n", p=128)
        w2src = I["mlp_w2"][l].rearrange("(fc p) n -> p fc n", p=128)
        NW = NSEQ * 8
        rc = [0]

        def wload(wi):
            fg = wi % 8
            sy.dma("pool", W1g[wi % 2][:, :, :], w1src[:, :, fg * 512:(fg + 1) * 512], w=[("W1g", wi % 2)])
            sy.dma("pool", W2g[wi % 2][:, :, :], w2src[:, fg * 4:(fg + 1) * 4, :], w=[("W2g", wi % 2)])

        def aload(s_):
            for tb_ in range(NTB):
                sy.dma("sp", xT[:, :, tb_ * 512:(tb_ + 1) * 512],
                       XinT[:, :, s_ * T + tb_ * 512:s_ * T + (tb_ + 1) * 512].rearrange("c p t -> p c t"), w=[("mxT", tb_)])
            if s_ == 0:
                for tl in range(NT):
                    sy.dma("sp", acc[:, tl, :], Xin[tl * 128:(tl + 1) * 128, :], w=[("acc", tl)])

        def genH(s_, fg, tb, n):
            wi = s_ * 8 + fg
            w1, k1 = W1g[wi % 2], ("W1g", wi % 2)
            hTi, kh = hT[n % 2], ("hT", n % 2)
            for fc in range(4):
                bb = bkH.next()
                sy.op("pe", [lambda e, kc=kc, bb=bb, fc=fc: e.matmul(
                    ps[bb][:, :], lhsT=w1[:, kc, fc * 128:(fc + 1) * 128], rhs=xT[:, kc, tb * 512:(tb + 1) * 512],
                    start=(kc == 0), stop=(kc == 7)) for kc in range(8)], r=[("mxT", tb), k1], w=[("ps", bb)], banks=[bb])
                rli = rl[rc[0] % 2]
                kr = ("rl", rc[0] % 2)
                rc[0] += 1
                sy.op("act", lambda e, bb=bb, rli=rli: e.activation(out=rli[:, :], in_=ps[bb][:, :], func=AF.Relu),
                      r=[("ps", bb)], w=[kr], banks=[bb])
                sy.op("act", lambda e, rli=rli, fc=fc: e.activation(out=hTi[:, fc, :], in_=rli[:, :], func=AF.Square), r=[kr], w=[kh])
                yield

        def genS(s_, fg, tb, n):
            wi = s_ * 8 + fg
            w2, k2 = W2g[wi % 2], ("W2g", wi % 2)
            hTi, kh = hT[n % 2], ("hT", n % 2)
            for tt in range(4):
                tl = tb * 4 + tt
                for half in range(2):
                    bb = bkS.next()
                    sy.op("pe", [lambda e, fc=fc, bb=bb, tt=tt, half=half: e.matmul(
                        ps[bb][:, :], lhsT=hTi[:, fc, tt * 128:(tt + 1) * 128], rhs=w2[:, fc, half * 512:(half + 1) * 512],
                        start=(fc == 0), stop=(fc == 3)) for fc in range(4)], r=[kh, k2], w=[("ps", bb)], banks=[bb])
                    asl = acc[:, tl, half * 512:(half + 1) * 512]
                    if fg == 0:
                        sy.op("dve", lambda e, bb=bb, asl=asl: e.scalar_tensor_tensor(out=asl, in0=asl, scalar=ALPHA, in1=ps[bb][:, :],
                                                                                      op0=ALU.mult, op1=ALU.add),
                              r=[("ps", bb), ("acc", tl)], w=[("acc", tl)], banks=[bb])
                    else:
                        sy.op("dve", lambda e, bb=bb, asl=asl: e.tensor_tensor(out=asl, in0=ps[bb][:, :], in1=asl, op=ALU.add),
                              r=[("ps", bb), ("acc", tl)], w=[("acc", tl)], banks=[bb])
                    yield

        def genSeq(s_):
            aload(s_)
            blocks = [(fg, tb) for fg in range(8) for tb in range(NTB)]
            n0 = s_ * len(blocks)
            for _ in genH(s_, blocks[0][0], blocks[0][1], n0):
                yield
            for bi, (fg, tb) in enumerate(blocks):
                wi = s_ * 8 + fg
                if tb == 0 and wi + 1 < NW:
                    wload(wi + 1)
                gh = genH(s_, blocks[bi + 1][0], blocks[bi + 1][1], n0 + bi + 1) if bi + 1 < len(blocks) else None
                gs = genS(s_, fg, tb, n0 + bi)
                dh = gh is None
                ds = False
                while not (dh and ds):
                    if not dh:
                        try:
                            next(gh)
                        except StopIteration:
                            dh = True
                    if not ds:
                        try:
                            next(gs)
                        except StopIteration:
                            ds = True
                    yield

        def genLN(s_):
            for tl in range(NT):
                tk0 = s_ * T + tl * 128
                resid_ln(g, LB, acc[:, tl, :], ("acc", tl), (l, 1), Xout, XoutT, tk0, bkS, tag)
                if s_ + 1 < NSEQ:
                    tk1 = (s_ + 1) * T + tl * 128
                    sy.dma("sp", acc[:, tl, :], Xin[tk1:tk1 + 128, :], w=[("acc", tl)])
                yield

        wload(0)
        prev = None
        for s_ in range(NSEQ):
            interleave(prev, genSeq(s_), ratio=6)
            prev = genLN(s_)
        for _ in prev:
            pass


LAMBDA_INIT = 0.8 - 0.6 * math.exp(-0.3 * 1)


def attn_layer(g):
    sy, nc, C, pf, I = g.sy, g.nc, g.C, g.pf, g.I
    T, NSEQ, NTOK = g.T, g.NSEQ, g.NTOK
    ps = g.ps
    NT = T // 128
    NG = T // 512
    with ExitStack() as esl:
        KT = g.sb("KT", [128, 8, T], BF16, esl)
        QT = g.sb("QT", [128, 8, T], BF16, esl)
        Vp = g.sb("Vp", [128, NT, 8, 130], BF16, esl)
        EB = g.sb("EB", [128, 8, 2, 128], F32, esl)
        lamc = g.sb("lamc", [128, 4], F32, esl)
        sy.op("pool", lambda e: e.memset(Vp[:, :, :, 128:130], 1.0), w=["Vp1"])
        with ExitStack() as es:
            sb = lambda n, s, d: g.sb(n, s, d, es)
            oh = sb("oh", [32, 384], F32); sy.dma("sp", oh[:], I["onehot"][:, :], w=["oh"])
            rb = sb("rb", [32, 8], F32); sy.dma("sp", rb[:], I["rel_bias"][:, :], w=["rb"])
            bvs = sb("bvs", [8, 384], F32)
            sy.op("pe", lambda e: e.matmul(ps[0][0:8, 0:384], lhsT=rb[:, :], rhs=oh[:, :], start=True, stop=True), r=["oh", "rb"], w=[("ps", 0)], banks=[0])
            sy.op("act", lambda e: e.activation(out=bvs[:, :], in_=ps[0][0:8, 0:384], func=AF.Copy), r=[("ps", 0)], w=["bvs"], banks=[0])
            BVd = nc.dram_tensor("s_bv", [8, 384], F32, kind="Internal").ap()
            sy.dma("sp", BVd[:, :], bvs[:, :], r=["bvs"], w=["BVd"])
            hk = sb("hk", [128, 8, 2, 128], F32)
            for h in range(8):
                for j in range(2):
                    src = bass.AP(tensor=BVd.tensor, offset=h * 384 + j * 128, ap=[[1, 128], [1, 128]])
                    sy.dma("sp", hk[:, h, j, :], src, r=["BVd"], w=["hk"])
            for hh in range(4):
                sy.op("pe", lambda e, hh=hh: e.matmul(ps[1 + hh % 2][:, :], lhsT=C["antiI"][:, :], rhs=hk[:, 2 * hh:2 * hh + 2, :, :], start=True, stop=True),
                      r=["hk", ("c", "antiI")], w=[("ps", 1 + hh % 2)], banks=[1 + hh % 2])
                sy.op("act", lambda e, hh=hh: e.activation(out=EB[:, 2 * hh:2 * hh + 2, :, :].rearrange("p a b c -> p (a b c)"), in_=ps[1 + hh % 2][:, :], func=AF.Exp),
                      r=[("ps", 1 + hh % 2)], w=["EB"], banks=[1 + hh % 2])
            sy.op("dve", lambda e: e.tensor_tensor(out=EB[:, :, 0, :], in0=EB[:, :, 0, :], in1=C["maskd"][:, :].unsqueeze(1).to_broadcast([128, 8, 128]), op=ALU.mult),
                  r=["EB", ("c", "maskd")], w=["EB"])
            lam = sb("lam", [128, 256], F32)
            sy.dma("sp", lam[:, :], I["b_lam"][0:1, :].partition_broadcast(128), w=["lam"])
            lp = sb("lp", [128, 2, 64], F32)
            ls = sb("ls", [128, 4], F32)
            l4 = lam[:, :].rearrange("p (a b c) -> p a b c", a=2, b=2)
            sy.op("dve", lambda e: e.tensor_tensor(out=lp[:, :, :], in0=l4[:, :, 0, :], in1=l4[:, :, 1, :], op=ALU.mult), r=["lam"], w=["lp"])
            sy.op("dve", lambda e: e.tensor_reduce(out=ls[:, 0:2], in_=lp[:, :, :], axis=AX.X, op=ALU.add), r=["lp"], w=["ls"])
            sy.op("act", lambda e: e.activation(out=ls[:, 2:4], in_=ls[:, 0:2], func=AF.Exp), r=["ls"], w=["ls2"])
            sy.op("dve", lambda e: e.scalar_tensor_tensor(out=lamc[:, 0:1], in0=ls[:, 3:4], scalar=-LAMBDA_INIT, in1=ls[:, 2:3], op0=ALU.add, op1=ALU.subtract),
                  r=["ls2"], w=["lamc"])
        sy.barrier()
        for s in range(NSEQ):
            with ExitStack() as es:
                sb = lambda n, s_, d: g.sb(n, s_, d, es)
                xT = sb("axT", [128, 8, T], BF16)
                for kc in range(8):
                    sy.dma("sp", xT[:, kc, :], g.X2T[kc, :, s * T:(s + 1) * T], w=[("axT", kc)])
                kxT = [("axT", kc) for kc in range(8)]
                Wa = sb("Wa", [128, 8, D], BF16)
                Wb = sb("Wb", [128, 8, D], BF16)
                Wc = sb("Wc", [128, 8, D], BF16)
                bk = Banks(range(8))
                sy.dma("pool", Wa[:], I["b_w_kv"][:, 0:D].rearrange("(kc p) n -> p kc n", p=128), w=["Wa"])
                sy.dma("pool", Wb[:], I["b_w_q"].rearrange("(kc p) n -> p kc n", p=128), w=["Wb"])
                sy.dma("pool", Wc[:], I["b_w_kv"][:, D:2 * D].rearrange("(kc p) n -> p kc n", p=128), w=["Wc"])
                for (W, kW, dst, kd) in ((Wa, "Wa", KT, "KT"), (Wb, "Wb", QT, "QT")):
                    for oc in range(8):
                        for tb in range(NG):
                            bb = bk.next()
                            sy.op("pe", [lambda e, kc=kc, bb=bb, oc=oc, tb=tb, W=W: e.matmul(
                                ps[bb][:, :], lhsT=W[:, kc, oc * 128:(oc + 1) * 128], rhs=xT[:, kc, tb * 512:(tb + 1) * 512],
                                start=(kc == 0), stop=(kc == 7)) for kc in range(8)], r=kxT + [kW], w=[("ps", bb)], banks=[bb])
                            sy.op("act", lambda e, bb=bb, oc=oc, tb=tb, dst=dst: e.activation(out=dst[:, oc, tb * 512:(tb + 1) * 512], in_=ps[bb][:, :], func=AF.Copy),
                                  r=[("ps", bb)], w=[(kd, oc)], banks=[bb])
                for tl in range(NT):
                    for half in range(2):
                        bb = bk.next()
                        sy.op("pe", [lambda e, kc=kc, bb=bb, tl=tl, half=half: e.matmul(
                            ps[bb][:, :], lhsT=xT[:, kc, tl * 128:(tl + 1) * 128], rhs=Wc[:, kc, half * 512:(half + 1) * 512],
                            start=(kc == 0), stop=(kc == 7)) for kc in range(8)], r=kxT + ["Wc"], w=[("ps", bb)], banks=[bb])
                        sy.op("act", lambda e, bb=bb, tl=tl, half=half: e.activation(
                            out=Vp[:, tl, half * 4:(half + 1) * 4, 0:128], in_=ps[bb][:, :].rearrange("p (h e) -> p h e", e=128), func=AF.Copy),
                            r=[("ps", bb)], w=[("Vp", tl)], banks=[bb])
            sy.barrier()
            with ExitStack() as es:
                sb = lambda n, s_, d: g.sb(n, s_, d, es)
                Wo = load_w(g, es, "Wob", I["b_w_o"], 8, D)
                LB = ln_bufs(g, es, (1, 0), "l10")
                subg = bcast_rows(g, es, "subg", PTI("subln"))
                Oalls = [sb("Oall%d" % i, [128, 4, D], BF16) for i in range(2)]
                PT = [[sb("PT%d_%d" % (m, i), [128, 512], BF16) for i in range(3)] for m in range(2)]
                bkS = Banks([0, 1, 5, 6])
                etmp = [sb("etmp%d" % i, [128, 128], F32) for i in range(4)]
                cmb = sb("cmb", [128, 24], F32)
                Osb = [sb("Osb%d" % i, [128, 9, 130], F32) for i in range(2)]
                hc = 0
                tq = sb("tq", [128, 4, 128], F32)
                ob = sb("ob", [128, 4, 128], F32)
                osq = sb("osq", [128, 4, 128], F32)
                oT = sb("oT", [128, 8, 128], BF16)
                xres = [sb("axres%d" % i, [128, D], F32) for i in range(2)]
                z = sb("az", [128, D], F32)
                bk2 = Banks([7])
                SBK = [(0, 1), (5, 6)]
                it = 0
                ec = 0
                def genAttn(G):
                    nonlocal it, ec, hc
                    Oall = Oalls[G % 2]
                    if True:
                        def acc(m, j):
                            a = m * 4 + j
                            return ps[2 + a // 3][:, (a % 3) * 130:(a % 3) * 130 + 129], 2 + a // 3
                        started = set()

                        def stage1(h, kt):
                            nonlocal it, ec
                            q0 = max(kt, 4 * G)
                            c0 = (q0 - 4 * G) * 128
                            sbk = (bkS.next(), bkS.next())
                            pts = [PT[0][it % 3], PT[1][it % 3]]
                            kpt = [("PT", 0, it % 3), ("PT", 1, it % 3)]
                            it += 1
                            for m in range(2):
                                mr = slice(m * 64, (m + 1) * 64)
                                sy.op("pe", lambda e, m=m, mr=mr, sbk=sbk, c0=c0, kt=kt, q0=q0: e.matmul(
                                    ps[sbk[m]][:, c0:512], lhsT=KT[mr, h, kt * 128:(kt + 1) * 128], rhs=QT[mr, h, q0 * 128:(4 * G + 4) * 128],
                                    start=True, stop=True), r=[("KT", h), ("QT", h)], w=[("ps", sbk[m])], banks=[sbk[m]])
                            for m in range(2):
                                cc = c0
                                for near in range(2):
                                    qb = kt + near
                                    if qb < 4 * G or qb > 4 * G + 3:
                                        continue
                                    cb = (qb - 4 * G) * 128
                                    et = etmp[ec % 4]
                                    ke = ("etmp", ec % 4)
                                    ec += 1
                                    sy.op("act", lambda e, m=m, cb=cb, et=et, sbk=sbk: e.activation(out=et[:, :], in_=ps[sbk[m]][:, cb:cb + 128], func=AF.Exp, scale=0.125),
                                          r=[("ps", sbk[m])], w=[ke], banks=[sbk[m]])
                                    sy.op("dve", lambda e, m=m, cb=cb, et=et, near=near, pts=pts: e.tensor_tensor(
                                        out=pts[m][:, cb:cb + 128], in0=et[:, :], in1=EB[:, h, near, :], op=ALU.mult), r=[ke, "EB"], w=[kpt[m]])
                                    cc = cb + 128
                                if cc < 512:
                                    sy.op("act", lambda e, m=m, cc=cc, sbk=sbk, pts=pts: e.activation(out=pts[m][:, cc:512], in_=ps[sbk[m]][:, cc:512], func=AF.Exp, scale=0.125),
                                          r=[("ps", sbk[m])], w=[kpt[m]], banks=[sbk[m]])
                            return pts, kpt

                        def stage2(h, kt, pts, kpt):
                            fns = []
                            bset = set()
                            for m in range(2):
                                for j in range(4):
                                    qb = 4 * G + j
                                    if qb < kt:
                                        continue
                                    ap_, bnk = acc(m, j)
                                    st = bnk not in started
                                    started.add(bnk)
                                    bset.add(bnk)
                                    fns.append(lambda e, m=m, j=j, ap_=ap_, st=st, pts=pts, kt=kt: e.matmul(
                                        ap_, lhsT=pts[m][:, j * 128:(j + 1) * 128], rhs=Vp[:, kt, h, 0:129], start=st, stop=False, skip_group_check=True))
                            sy.op("pe", fns, r=kpt + [("Vp", kt), "Vp1"], w=[("ps", b_) for b_ in sorted(bset)], banks=sorted(bset))

                        def finish_head(h):
                            nonlocal hc
                            osb = Osb[hc % 2]
                            for bq in range(3):
                                if bq == 1:
                                    sy.op("dve", lambda e, bq=bq, osb=osb: e.tensor_copy(out=osb[:, 3 * bq:3 * bq + 3, :].rearrange("p a b -> p (a b)"), in_=ps[2 + bq][:, 0:390]),
                                          r=[("ps", 2 + bq)], w=[("Osb", hc % 2, bq)], banks=[2 + bq])
                                else:
                                    sy.op("act", lambda e, bq=bq, osb=osb: e.activation(out=osb[:, 3 * bq:3 * bq + 3, :].rearrange("p a b -> p (a b)"), in_=ps[2 + bq][:, 0:390], func=AF.Copy),
                                          r=[("ps", 2 + bq)], w=[("Osb", hc % 2, bq)], banks=[2 + bq])
                            kO = [("Osb", hc % 2, bq) for bq in range(3)]
                            hc += 1
                            yield
                            B4 = [128, 4, 128]
                            sy.op("dve", lambda e, osb=osb: e.reciprocal(out=cmb[:, 0:8], in_=osb[:, 0:8, 128]), r=kO, w=["cmb0"])
                            sy.op("dve", lambda e: e.tensor_scalar(out=cmb[:, 8:12], in0=cmb[:, 4:8], scalar1=lamc[:, 0:1], scalar2=None, op0=ALU.mult),
                                  r=["cmb0", "lamc"], w=["cmb1"])
                            sy.op("dve", lambda e, osb=osb: e.tensor_tensor(out=tq[:, :, :], in0=osb[:, 4:8, 0:128], in1=cmb[:, 8:12].unsqueeze(2).to_broadcast(B4), op=ALU.mult),
                                  r=kO + ["cmb1"], w=["tq"])
                            sy.op("pool", lambda e, osb=osb: e.tensor_tensor(out=ob[:, :, :], in0=osb[:, 0:4, 0:128], in1=cmb[:, 0:4].unsqueeze(2).to_broadcast(B4), op=ALU.mult),
                                  r=kO + ["cmb0"], w=["ob"])
                            yield
                            sy.op("dve", lambda e: e.tensor_tensor(out=ob[:, :, :], in0=ob[:, :, :], in1=tq[:, :, :], op=ALU.add), r=["ob", "tq"], w=["ob"])
                            sy.op("act", lambda e: e.activation(out=osq[:, :, :], in_=ob[:, :, :], func=AF.Square), r=["ob"], w=["osq"])
                            sy.op("dve", lambda e: e.tensor_reduce(out=cmb[:, 12:16], in_=osq[:, :, :], axis=AX.X, op=ALU.add), r=["osq"], w=["cmb3"])
                            sy.op("dve", lambda e: e.tensor_scalar(out=cmb[:, 16:20], in0=cmb[:, 12:16], scalar1=1.0 / 128, scalar2=SUBLN_EPS, op0=ALU.mult, op1=ALU.add),
                                  r=["cmb3"], w=["cmb4"])
                            yield
                            sy.op("pool", lambda e: e.tensor_tensor(out=cmb[:, 20:24], in0=cmb[:, 16:20], in1=C["neghalf"][:, 0:4], op=ALU.pow), r=["cmb4", ("c", "neghalf")], w=["cmb5"])
                            sy.op("dve", lambda e: e.tensor_tensor(out=ob[:, :, :], in0=ob[:, :, :], in1=cmb[:, 20:24].unsqueeze(2).to_broadcast(B4), op=ALU.mult),
                                  r=["ob", "cmb5"], w=["ob"])
                            sy.op("dve", lambda e: e.tensor_tensor(out=Oall[:, :, h * 128:(h + 1) * 128], in0=ob[:, :, :],
                                                                   in1=subg[:, h * 128:(h + 1) * 128].unsqueeze(1).to_broadcast(B4), op=ALU.mult),
                                  r=["ob", "subg"], w=[("Oall", G % 2, j_) for j_ in range(4)])
                            yield


                        nkt = 4 * G + 4
                        items = [(h_, kt_) for h_ in range(8) for kt_ in range(nkt)]
                        ctxs = {}
                        for n_ in range(min(2, len(items))):
                            ctxs[n_] = stage1(*items[n_])
                        for n_, (h_, kt_) in enumerate(items):
                            if n_ + 2 < len(items):
                                ctxs[n_ + 2] = stage1(*items[n_ + 2])
                            stage2(h_, kt_, *ctxs.pop(n_))
                            yield
                            if kt_ == nkt - 1:
                                started.clear()
                                for _ in finish_head(h_):
                                    yield

                def genWoLN(G):
                    Oall = Oalls[G % 2]
                    for j in range(4):
                        tl = 4 * G + j
                        tk0 = s * T + tl * 128
                        xr = xres[tl % 2]
                        sy.dma("sp", xr[:, :], g.X2[tk0:tk0 + 128, :], w=[("axres", tl % 2)])
                        for half in range(2):
                            bb = bk2.next()
                            psb = ps[bb][:, 0:256].bitcast(BF16)
                            sy.op("pe", [lambda e, jj=jj, psb=psb, half=half, j=j: e.transpose(out=psb[:, jj * 128:(jj + 1) * 128],
                                                                                            in_=Oall[:, j, (half * 4 + jj) * 128:(half * 4 + jj + 1) * 128],
                                                                                            identity=C["identb"][:, :]) for jj in range(4)],
                                  r=[("Oall", G % 2, j), ("c", "identb")], w=[("ps", bb)], banks=[bb])
                            sy.op("act", lambda e, psb=psb, half=half: e.activation(out=oT[:, half * 4:(half + 1) * 4, :], in_=psb.rearrange("p (j t) -> p j t", t=128),
                                                                                    func=AF.Identity, scale=(1.0 - LAMBDA_INIT)), r=[("ps", bb)], w=["oT"], banks=[bb])
                            yield
                        for half in range(2):
                            bo = bk2.next()
                            sy.op("pe", [lambda e, kc=kc, bo=bo, half=half: e.matmul(ps[bo][:, :], lhsT=oT[:, kc, :], rhs=Wo[:, kc, half * 512:(half + 1) * 512],
                                                                                   start=(kc == 0), stop=(kc == 7)) for kc in range(8)], r=["oT", "Wob"], w=[("ps", bo)], banks=[bo])
                            sy.op("dve", lambda e, bo=bo, half=half, xr=xr: e.scalar_tensor_tensor(
                                out=z[:, half * 512:(half + 1) * 512], in0=xr[:, half * 512:(half + 1) * 512], scalar=ALPHA, in1=ps[bo][:, :],
                                op0=ALU.mult, op1=ALU.add), r=[("ps", bo), ("axres", tl % 2)], w=["az"], banks=[bo])
                            yield
                        resid_ln(g, LB, z, "az", (1, 0), g.X3, g.X3T, tk0, bk2, "l10", use_sqrt=False)
                        yield

                prevg = None
                for G in range(NG):
                    interleave(genAttn(G), prevg, ratio=8)
                    prevg = genWoLN(G)
                for _ in prevg:
                    pass
            sy.barrier()


def make_in_maps(inputs, T, NSEQ, ncores):
    f = lambda a: np.ascontiguousarray(np.asarray(a, dtype=np.float32))
    x = f(inputs["x"]).reshape(-1, D)
    vec = {}
    mu = f(inputs["a_mu"])[0]
    for i in range(6):
        vec["mu%d" % i] = mu[i]
    vec["w0"] = f(inputs["a_w0"])[0]; vec["a0"] = f(inputs["a_a0"])[0]
    vec["k_k"] = f(inputs["a_k_k"])[0]; vec["k_a"] = f(inputs["a_k_a"])[0]
    vec["r_k"] = f(inputs["a_r_k"])[0].reshape(-1)
    vec["lnx_g"] = f(inputs["a_lnx_g"])[0]; vec["lnx_b"] = f(inputs["a_lnx_b"])[0]
    lg, lb = f(inputs["ln_g"]), f(inputs["ln_b"])
    for l in range(2):
        for i in range(2):
            vec["ln_g%d%d" % (l, i)] = lg[l, i]
            vec["ln_b%d%d" % (l, i)] = lb[l, i]
    vec["subln"] = np.tile(f(inputs["b_subln_g"])[0], 8)
    pf = np.stack([vec[n].reshape(8, 128).T for n in PF_NAMES], axis=1).reshape(128, NPF * 8)
    pt = np.stack([vec[n] for n in PT_NAMES], axis=0)
    common = {
        "a_w_r": f(inputs["a_w_r"])[0], "a_w_k": f(inputs["a_w_k"])[0], "a_w_v": f(inputs["a_w_v"])[0], "a_w_o": f(inputs["a_w_o"])[0],
        "b_w_q": f(inputs["b_w_q"])[0], "b_w_o": f(inputs["b_w_o"])[0], "b_w_kv": f(inputs["b_w_kv"]),
        "a_w1": f(inputs["a_w1"])[0], "a_a1": f(inputs["a_a1"])[0], "a_g1": f(inputs["a_g1"])[0],
        "a_w2": f(inputs["a_w2"])[0], "a_a2": f(inputs["a_a2"])[0], "a_g2": f(inputs["a_g2"])[0],
        "mlp_w1": f(inputs["mlp_w1"]), "mlp_w2": f(inputs["mlp_w2"]),
        "pf": np.ascontiguousarray(pf), "pt": np.ascontiguousarray(pt),
        "b_lam": f(inputs["b_lam"]).reshape(1, 256), "rel_bias": f(inputs["rel_bias"]),
        "onehot": _onehot_np(),
    }
    for k, v in _consts_np(256).items():
        common["c_" + k] = v
    maps = []
    for c in range(ncores):
        m = dict(common)
        m["x"] = np.ascontiguousarray(x[c * NSEQ * T:(c + 1) * NSEQ * T])
        maps.append(m)
    return maps


def _onehot_np():
    oh = np.zeros((32, 384), np.float32)
    for i in range(383):
        rel = 127 - i
        nb, max_exact = 16, 8
        ret = nb if rel > 0 else 0
        n = abs(rel)
        nf = np.float32(max(n, 1))
        large = max_exact + int(np.float32(np.log(nf / np.float32(max_exact))) / np.float32(math.log(128 / max_exact)) * (nb - max_exact))
        large = min(large, nb - 1)
        bkt = ret + (n if n < max_exact else large)
        oh[bkt, i] += 1.0
        oh[15, i] -= 1.0
    return oh


_PROG = {}


def kernel(**inputs):
    T, NSEQ, NC = 2048, 2, 8
    if "p" not in _PROG:
        _PROG["p"] = build_program(T, NSEQ)
    nc, g = _PROG["p"]
    maps = make_in_maps(inputs, T, NSEQ, NC)
    res = run_bass_kernel_spmd(nc, maps, core_ids=list(range(NC)))
    out = np.concatenate([np.asarray(r["out"]) for r in res.results], axis=0)
    return out.reshape(16, 2048, D).astype(np.float32)
```
